# Optimizing a Trainium2 kernel written in Bass

```python
import jax, jax.numpy as jnp
from jax import lax
import numpy as np

D_MODEL = 4096
BATCH = 2
SEQ = 4096
DEPTH = 2

GRID_W = 64
CTX_LEN = 256
D_LRU = D_MODEL // 2
LRU_HEADS = 16
LRU_BLOCK = D_LRU // LRU_HEADS
CONV_W = 4
LRU_C = 8.0
D_POOL = D_MODEL // 2
POOL_WINDOWS = (2, 4, 8, 16)
N_POOL = len(POOL_WINDOWS)
POOL_GROUP = D_POOL // N_POOL
D_IN_EVEN = 2 * D_LRU + D_POOL
MLSTM_HEADS = 8
MLSTM_DV = D_MODEL // MLSTM_HEADS
MLSTM_DQK = MLSTM_DV // 2
MLSTM_CHUNK = 64
D_IN_ODD = 2 * MLSTM_HEADS * MLSTM_DQK + 2 * D_MODEL + 4 * MLSTM_HEADS
N_EXPERTS = 16
D_EXPERT = D_MODEL // 4
CAPACITY_FACTOR = 2
N_EVEN = (DEPTH + 1) // 2
N_ODD = DEPTH // 2
DN_ALPHA = (2 * DEPTH) ** 0.25
DN_BETA = (8 * DEPTH) ** -0.25
LN_EPS = 1e-5

kernel_name = "hybrid_rglru_pool_mlstm_ecmoe_diffusion"

F32 = jnp.float32


def flip_seq(t):
    return t[:, ::-1]


def no_flip(t):
    return t


def layer_norm(x, g=None, b=None):
    xf = x.astype(F32)
    mu = jnp.mean(xf, axis=-1, keepdims=True)
    var = jnp.mean(jnp.square(xf - mu), axis=-1, keepdims=True)
    y = (xf - mu) * lax.rsqrt(var + LN_EPS)
    if g is not None:
        y = y * g + b
    return y.astype(x.dtype)


def adaln(cond, w, b):
    m = jax.nn.silu(cond) @ w + b
    return jnp.split(m[..., None, :], 6, axis=-1)


def dwconv(x, w, b):
    C = x.shape[-1]
    y = lax.conv_general_dilated(x, w[:, None, :].astype(x.dtype), window_strides=(1,),
                                 padding=[((CONV_W - 1) // 2, CONV_W // 2)],
                                 dimension_numbers=('NWC', 'WIO', 'NWC'), feature_group_count=C)
    return y + b


def linear_scan(a, bx, h0):
    def combine(l, r):
        return (l[0] * r[0], r[0] * l[1] + r[1])
    A, H = lax.associative_scan(combine, (a, bx), axis=1)
    return H + A * h0[:, None, :]


def rglru_coeffs(x, gate_w, gate_b, lam):
    B, L, _ = x.shape
    xb = x.reshape(B, L, LRU_HEADS, LRU_BLOCK)
    g = jnp.einsum('blhi,ghij->gblhj', xb, gate_w).reshape(2, B, L, D_LRU) + gate_b[:, None, None, :]
    r = jax.nn.sigmoid(g[0].astype(F32))
    i = jax.nn.sigmoid(g[1].astype(F32))
    log_a = LRU_C * r * jax.nn.log_sigmoid(lam.astype(F32))
    a = jnp.exp(log_a)
    mult = jnp.sqrt(-jnp.expm1(2.0 * log_a))
    return a, mult * (i * x.astype(F32))


def rglru_bidir(xc, xl, gate_w, gate_b, lam):
    B = xl.shape[0]
    out_c = 0.0
    out_l = 0.0
    for d in range(2):
        f = flip_seq if d else no_flip
        ac, bc = rglru_coeffs(f(xc), gate_w[d], gate_b[d], lam[d])
        hc = linear_scan(ac, bc, jnp.zeros((B, D_LRU), F32))
        al, bl = rglru_coeffs(f(xl), gate_w[d], gate_b[d], lam[d])
        hl = linear_scan(al, bl, hc[:, -1])
        out_c = out_c + f(hc)
        out_l = out_l + f(hl)
    return out_c, out_l


def window_mean(x, w, axis):
    L = x.shape[axis]
    pad = [(0, 0)] * x.ndim
    pad[axis] = (1, 0)
    cs = jnp.pad(jnp.cumsum(x.astype(F32), axis=axis), pad)
    pos = jnp.arange(L)
    lo = jnp.clip(pos - w // 2, 0, L)
    hi = jnp.clip(pos + w - w // 2, 0, L)
    shape = [1] * x.ndim
    shape[axis] = L
    cnt = (hi - lo).astype(F32).reshape(shape)
    return (jnp.take(cs, hi, axis=axis) - jnp.take(cs, lo, axis=axis)) / cnt


def pool_mix(s, pool_w, pool_scale, on_grid):
    B, L, _ = s.shape
    groups = s.reshape(B, L, N_POOL, POOL_GROUP)
    diffs = []
    for g, w in enumerate(POOL_WINDOWS):
        xg = groups[:, :, g]
        if on_grid:
            rows = L // GRID_W
            grid = xg.reshape(B, rows, GRID_W, POOL_GROUP)
            mean = window_mean(window_mean(grid, w, 2), w, 1).reshape(B, L, POOL_GROUP)
        else:
            mean = window_mean(xg, w, 1)
        diffs.append((mean - xg.astype(F32)).astype(s.dtype))
    d = jnp.stack(diffs, axis=2)
    y = jnp.einsum('blgi,gij->blgj', d, pool_w).reshape(B, L, D_POOL)
    return y * pool_scale


def even_mixer(uc, ul, w_in, conv_w, conv_b, gate_w, gate_b, lam, pool_w, pool_scale, w_out, ctx_out):
    xc, zc, sc = jnp.split(uc @ w_in, [D_LRU, 2 * D_LRU], axis=-1)
    xl, zl, sl = jnp.split(ul @ w_in, [D_LRU, 2 * D_LRU], axis=-1)
    hc, hl = rglru_bidir(dwconv(xc, conv_w, conv_b), dwconv(xl, conv_w, conv_b), gate_w, gate_b, lam)
    yl = jnp.concatenate([hl.astype(ul.dtype) * jax.nn.gelu(zl),
                          pool_mix(sl, pool_w, pool_scale, True)], axis=-1) @ w_out
    yc = None
    if ctx_out:
        yc = jnp.concatenate([hc.astype(uc.dtype) * jax.nn.gelu(zc),
                              pool_mix(sc, pool_w, pool_scale, False)], axis=-1) @ w_out
    return yc, yl


def mlstm_chunked(q, k, v, ig, lf, state):
    B, L, H, _ = q.shape
    nc = L // MLSTM_CHUNK

    def to_chunks(t):
        return jnp.moveaxis(t.reshape((B, nc, MLSTM_CHUNK) + t.shape[2:]), 1, 0)

    xs = (to_chunks(q), to_chunks(k), to_chunks(v), to_chunks(ig), to_chunks(lf))
    causal = jnp.tril(jnp.ones((MLSTM_CHUNK, MLSTM_CHUNK), bool))[None, :, :, None]

    def step(carry, inp):
        C, n, m = carry
        qc, kc, vc, ic, fc = inp
        b = jnp.cumsum(fc, axis=1)
        dmat = jnp.where(causal, b[:, :, None, :] - b[:, None, :, :] + ic[:, None, :, :], -jnp.inf)
        inter = b + m[:, None, :]
        m_comb = jnp.maximum(inter, jnp.max(dmat, axis=2))
        s = jnp.einsum('bthd,bshd->btsh', qc, kc) * jnp.exp(dmat - m_comb[:, :, None, :])
        e_inter = jnp.exp(inter - m_comb)
        num = jnp.einsum('btsh,bshv->bthv', s, vc) + e_inter[..., None] * jnp.einsum('bthd,bhdv->bthv', qc, C)
        den = jnp.sum(s, axis=2) + e_inter * jnp.einsum('bthd,bhd->bth', qc, n)
        h = num / jnp.maximum(jnp.abs(den), jnp.exp(-m_comb))[..., None]
        b_end = b[:, -1]
        g = b_end[:, None, :] - b + ic
        m_new = jnp.maximum(b_end + m, jnp.max(g, axis=1))
        w_s = jnp.exp(g - m_new[:, None, :])
        decay = jnp.exp(b_end + m - m_new)
        C_new = decay[..., None, None] * C + jnp.einsum('bsh,bshd,bshv->bhdv', w_s, kc, vc)
        n_new = decay[..., None] * n + jnp.einsum('bsh,bshd->bhd', w_s, kc)
        return (C_new, n_new, m_new), h

    state, hs = lax.scan(step, state, xs)
    return jnp.moveaxis(hs, 0, 1).reshape(B, L, H, MLSTM_DV), state


def head_norm(h, g):
    B, L = h.shape[:2]
    mu = jnp.mean(h, axis=-1, keepdims=True)
    var = jnp.mean(jnp.square(h - mu), axis=-1, keepdims=True)
    return ((h - mu) * lax.rsqrt(var + LN_EPS)).reshape(B, L, D_MODEL) * g


def odd_mixer(uc, ul, w_in, gate_b, norm_g, w_out, ctx_out):
    qk = MLSTM_HEADS * MLSTM_DQK
    split_at = [qk, 2 * qk, 2 * qk + D_MODEL, 2 * qk + 2 * D_MODEL]

    def project(u):
        B, L, _ = u.shape
        q, k, v, o, g = jnp.split(u @ w_in, split_at, axis=-1)
        q = q.reshape(B, L, MLSTM_HEADS, MLSTM_DQK).astype(F32) * (MLSTM_DQK ** -0.5)
        k = k.reshape(B, L, MLSTM_HEADS, MLSTM_DQK).astype(F32)
        v = v.reshape(B, L, MLSTM_HEADS, MLSTM_DV).astype(F32)
        g = (g + gate_b).astype(F32).reshape(B, L, 2, 2, MLSTM_HEADS)
        return q, k, v, o, g

    qc, kc, vc, oc, gc = project(uc)
    ql, kl, vl, ol, gl = project(ul)
    B = ul.shape[0]
    h_c = 0.0
    h_l = 0.0
    for d in range(2):
        f = flip_seq if d else no_flip
        state0 = (jnp.zeros((B, MLSTM_HEADS, MLSTM_DQK, MLSTM_DV), F32),
                  jnp.zeros((B, MLSTM_HEADS, MLSTM_DQK), F32),
                  jnp.zeros((B, MLSTM_HEADS), F32))
        out_c, state = mlstm_chunked(f(qc), f(kc), f(vc), f(gc[:, :, d, 0]),
                                     jax.nn.log_sigmoid(f(gc[:, :, d, 1])), state0)
        out_l, _ = mlstm_chunked(f(ql), f(kl), f(vl), f(gl[:, :, d, 0]),
                                 jax.nn.log_sigmoid(f(gl[:, :, d, 1])), state)
        h_c = h_c + f(out_c)
        h_l = h_l + f(out_l)

    def finish(h, o):
        y = jax.nn.sigmoid(o.astype(F32)) * head_norm(h, norm_g)
        return y.astype(o.dtype) @ w_out

    yl = finish(h_l, ol)
    yc = finish(h_c, oc) if ctx_out else None
    return yc, yl


def expert_choice_moe(x, w_router, w_gu, w_down):
    B, n, D = x.shape
    cap = CAPACITY_FACTOR * n // N_EXPERTS
    aff = jax.nn.softmax((x @ w_router).astype(F32), axis=-1)
    gsel, idx = lax.top_k(jnp.swapaxes(aff, 1, 2), cap)
    xe = jax.vmap(lambda xb, ib: xb[ib])(x, idx)
    hg, hu = jnp.split(jnp.einsum('becd,edf->becf', xe, w_gu), 2, axis=-1)
    ye = jnp.einsum('becf,efd->becd', jax.nn.silu(hg) * hu, w_down) * gsel[..., None].astype(x.dtype)
    return jax.vmap(lambda ib, yb: jnp.zeros((n, D), yb.dtype).at[ib.reshape(-1)].add(yb.reshape(-1, D)))(idx, ye)


def setup_inputs(seed: int = 0) -> dict:
    key = jax.random.key(seed)
    ks = jax.random.split(key, 26)
    D = D_MODEL

    def nrm(k, shape, s):
        return jax.random.normal(k, shape, F32) * s

    x = nrm(ks[0], (BATCH, SEQ, D), 1.0)
    c = nrm(ks[1], (BATCH, D), 1.0)
    ctx = nrm(ks[2], (BATCH, CTX_LEN, D), 1.0)
    c_ctx = nrm(ks[3], (D,), 1.0)
    ada_w = nrm(ks[4], (DEPTH, D, 6 * D), 0.5 * D ** -0.5)
    ada_b = nrm(ks[5], (DEPTH, 6 * D), 0.02)
    ln_g = 1.0 + nrm(ks[6], (DEPTH, 2, D), 0.02)
    ln_b = nrm(ks[7], (DEPTH, 2, D), 0.02)
    even_w_in = nrm(ks[8], (N_EVEN, D, D_IN_EVEN), D ** -0.5)
    even_conv_w = nrm(ks[9], (N_EVEN, CONV_W, D_LRU), CONV_W ** -0.5)
    even_conv_b = nrm(ks[10], (N_EVEN, D_LRU), 0.02)
    lru_gate_w = nrm(ks[11], (N_EVEN, 2, 2, LRU_HEADS, LRU_BLOCK, LRU_BLOCK), LRU_BLOCK ** -0.5)
    lru_gate_b = nrm(ks[12], (N_EVEN, 2, 2, D_LRU), 0.02)
    a_pow_c = jax.random.uniform(ks[13], (N_EVEN, 2, D_LRU), F32, 0.9, 0.999)
    sig = a_pow_c ** (1.0 / LRU_C)
    lru_lambda = jnp.log(sig) - jnp.log1p(-sig)
    pool_w = nrm(ks[14], (N_EVEN, N_POOL, POOL_GROUP, POOL_GROUP), POOL_GROUP ** -0.5)
    pool_scale = 1.0 + nrm(ks[15], (N_EVEN, D_POOL), 0.02)
    even_w_out = nrm(ks[16], (N_EVEN, D_LRU + D_POOL, D), DN_BETA * (D_LRU + D_POOL) ** -0.5)
    odd_w_in = nrm(ks[17], (N_ODD, D, D_IN_ODD), D ** -0.5)
    i_b = nrm(ks[18], (N_ODD, 2, 1, MLSTM_HEADS), 0.1)
    f_b = jnp.linspace(3.0, 6.0, MLSTM_HEADS, dtype=F32) + nrm(ks[19], (N_ODD, 2, 1, MLSTM_HEADS), 0.1)
    odd_gate_b = jnp.concatenate([i_b, f_b], axis=2).reshape(N_ODD, 4 * MLSTM_HEADS)
    odd_norm_g = 1.0 + nrm(ks[20], (N_ODD, D), 0.02)
    odd_w_out = nrm(ks[21], (N_ODD, D, D), DN_BETA * D ** -0.5)
    router_w = nrm(ks[22], (DEPTH, D, N_EXPERTS), D ** -0.5)
    expert_w_gu = nrm(ks[23], (DEPTH, N_EXPERTS, D, 2 * D_EXPERT), D ** -0.5)
    expert_w_down = nrm(ks[24], (DEPTH, N_EXPERTS, D_EXPERT, D), DN_BETA * D_EXPERT ** -0.5)
    return {"x": x, "c": c, "ctx": ctx, "c_ctx": c_ctx, "ada_w": ada_w, "ada_b": ada_b,
            "ln_g": ln_g, "ln_b": ln_b, "even_w_in": even_w_in, "even_conv_w": even_conv_w,
            "even_conv_b": even_conv_b, "lru_gate_w": lru_gate_w, "lru_gate_b": lru_gate_b,
            "lru_lambda": lru_lambda, "pool_w": pool_w, "pool_scale": pool_scale,
            "even_w_out": even_w_out, "odd_w_in": odd_w_in, "odd_gate_b": odd_gate_b,
            "odd_norm_g": odd_norm_g, "odd_w_out": odd_w_out, "router_w": router_w,
            "expert_w_gu": expert_w_gu, "expert_w_down": expert_w_down}


def reference(x, c, ctx, c_ctx, ada_w, ada_b, ln_g, ln_b, even_w_in, even_conv_w, even_conv_b,
              lru_gate_w, lru_gate_b, lru_lambda, pool_w, pool_scale, even_w_out, odd_w_in,
              odd_gate_b, odd_norm_g, odd_w_out, router_w, expert_w_gu, expert_w_down):
    h_lat = layer_norm(x)
    h_ctx = layer_norm(ctx)
    for layer in range(DEPTH):
        j = layer // 2
        need_ctx = layer < DEPTH - 1
        sh_l, sc_l, g_l, sh2_l, sc2_l, g2_l = adaln(c, ada_w[layer], ada_b[layer])
        sh_c, sc_c, g_c, sh2_c, sc2_c, g2_c = adaln(c_ctx, ada_w[layer], ada_b[layer])
        u_lat = h_lat * (1.0 + sc_l) + sh_l
        u_ctx = h_ctx * (1.0 + sc_c) + sh_c
        if layer % 2 == 0:
            y_ctx, y_lat = even_mixer(u_ctx, u_lat, even_w_in[j], even_conv_w[j], even_conv_b[j],
                                      lru_gate_w[j], lru_gate_b[j], lru_lambda[j], pool_w[j],
                                      pool_scale[j], even_w_out[j], need_ctx)
        else:
            y_ctx, y_lat = odd_mixer(u_ctx, u_lat, odd_w_in[j], odd_gate_b[j], odd_norm_g[j],
                                     odd_w_out[j], need_ctx)
        h_lat = layer_norm(DN_ALPHA * h_lat + g_l * y_lat, ln_g[layer, 0], ln_b[layer, 0])
        m_lat = expert_choice_moe(h_lat * (1.0 + sc2_l) + sh2_l, router_w[layer], expert_w_gu[layer], expert_w_down[layer])
        h_lat = layer_norm(DN_ALPHA * h_lat + g2_l * m_lat, ln_g[layer, 1], ln_b[layer, 1])
        if need_ctx:
            h_ctx = layer_norm(DN_ALPHA * h_ctx + g_c * y_ctx, ln_g[layer, 0], ln_b[layer, 0])
            m_ctx = expert_choice_moe(h_ctx * (1.0 + sc2_c) + sh2_c, router_w[layer], expert_w_gu[layer], expert_w_down[layer])
            h_ctx = layer_norm(DN_ALPHA * h_ctx + g2_c * m_ctx, ln_g[layer, 1], ln_b[layer, 1])
    return h_lat
```

```python
import numpy as np
import concourse.bass as bass
import concourse.mybir as mybir
from concourse.bass_utils import run_bass_kernel_spmd

F32 = mybir.dt.float32
AF = mybir.ActivationFunctionType
ALU = mybir.AluOpType
ENGS = ("tensor", "vector", "scalar", "gpsimd", "sync")
N_DMA_SEMS = 40


def _key(x):
    if isinstance(x, (str, tuple)):
        return x
    t = getattr(x, "tensor", x)
    return t.name


class Sync:
    def __init__(self, nc):
        self.guards = []
        self.esem = {}
        for e in ENGS:
            g = nc.semaphore("c_" + e)
            self.esem[e] = g.__enter__()
            self.guards.append(g)
        self.dsem = []
        for j in range(N_DMA_SEMS):
            g = nc.semaphore("d_%d" % j)
            self.dsem.append(g.__enter__())
            self.guards.append(g)
        self.ecount = {e: 0 for e in ENGS}
        self.ndma = 0

    def close(self):
        for g in reversed(self.guards):
            g.__exit__(None, None, None)


class Prog:
    def __init__(self, nc, sy):
        self.nc = nc
        self.sy = sy
        self.ops = []
        self.stack = []
        sy.phase = getattr(sy, "phase", 0) + 1
        self.pfx = "p%d_" % sy.phase

    def sb(self, name, shape, dt=F32):
        g = self.nc.sbuf_tensor(self.pfx + name, list(shape), dt)
        t = g.__enter__()
        self.stack.append(g)
        return t

    def ps(self, name, shape, dt=F32):
        g = self.nc.psum_tensor(self.pfx + name, list(shape), dt)
        t = g.__enter__()
        self.stack.append(g)
        return t

    def op(self, eng, fn, reads=(), writes=(), dma=False):
        self.ops.append(dict(eng=eng, fn=fn, r=[_key(k) for k in reads], w=[_key(k) for k in writes], dma=dma))

    def dma(self, out, in_, eng="sync", reads=None, writes=None, **kw):
        self.op(eng, lambda e: e.dma_start(out=out, in_=in_, **kw),
                reads=[in_] if reads is None else reads, writes=[out] if writes is None else writes, dma=True)

    def mm(self, out, lhsT, rhs, start=True, stop=True, reads=None, writes=None):
        self.op("tensor", lambda e: e.matmul(out, lhsT, rhs, start=start, stop=stop),
                reads=[lhsT, rhs] if reads is None else reads, writes=[out] if writes is None else writes)

    def tr(self, out, in_, ident):
        self.op("tensor", lambda e: e.transpose(out, in_, ident), reads=[in_, ident], writes=[out])

    def act(self, out, in_, func, bias=None, scale=1.0, eng="scalar"):
        r = [in_] + [x for x in (bias, scale) if not isinstance(x, (int, float, type(None)))]
        kw = {}
        if bias is not None:
            kw["bias"] = bias
        self.op(eng, lambda e: e.activation(out=out, in_=in_, func=func, scale=scale, **kw), reads=r, writes=[out])

    def tt(self, out, in0, in1, op, eng="vector"):
        self.op(eng, lambda e: e.tensor_tensor(out=out, in0=in0, in1=in1, op=op), reads=[in0, in1], writes=[out])

    def ts(self, out, in0, s1, s2, op0, op1=ALU.bypass, eng="vector"):
        r = [in0] + [x for x in (s1, s2) if not isinstance(x, (int, float, type(None)))]
        self.op(eng, lambda e: e.tensor_scalar(out=out, in0=in0, scalar1=s1, scalar2=s2, op0=op0, op1=op1),
                reads=r, writes=[out])

    def stt(self, out, in0, scalar, in1, op0, op1):
        r = [in0, in1] + ([scalar] if not isinstance(scalar, (int, float)) else [])
        self.op("vector", lambda e: e.scalar_tensor_tensor(out=out, in0=in0, scalar=scalar, in1=in1, op0=op0, op1=op1),
                reads=r, writes=[out])

    def copy(self, out, in_, eng="vector"):
        if eng == "scalar":
            self.op(eng, lambda e: e.copy(out=out, in_=in_), reads=[in_], writes=[out])
        else:
            self.op(eng, lambda e: e.tensor_copy(out=out, in_=in_), reads=[in_], writes=[out])

    def memset(self, ap, val, eng="vector"):
        self.op(eng, lambda e: e.memset(ap, val), reads=[], writes=[ap])

    def emit(self):
        nc, sy, ops = self.nc, self.sy, self.ops
        wr, rd = {}, {}
        nsem = N_DMA_SEMS
        for i, o in enumerate(ops):
            deps = set()
            for k in o["r"]:
                deps.update(wr.get(k, ()))
            for k in o["w"]:
                deps.update(wr.get(k, ()))
                deps.update(rd.get(k, ()))
            deps.discard(i)
            o["deps"] = deps
            for k in o["w"]:
                if rd.get(k):
                    wr[k] = [i]
                    rd[k] = []
                else:
                    wr.setdefault(k, []).append(i)
            for k in o["r"]:
                rd.setdefault(k, []).append(i)
            if o["dma"]:
                o["dma_idx"] = sy.ndma
                sy.ndma += 1
            else:
                sy.ecount[o["eng"]] += 1
                o["eidx"] = sy.ecount[o["eng"]]
        final_e = dict(sy.ecount)
        final_nd = sy.ndma
        esem, dsem = sy.esem, sy.dsem

        def gen(engname):
            def body(eng):
                known = {}

                def need(key, sem, val):
                    if val <= 0 or known.get(key, 0) >= val:
                        return
                    eng.wait_ge(sem, val)
                    known[key] = val

                for o in ops:
                    if o["eng"] != engname:
                        continue
                    req = {}
                    for d in o["deps"]:
                        p = ops[d]
                        if p["dma"]:
                            j = p["dma_idx"]
                            k_, s_, v_ = ("d", j % nsem), dsem[j % nsem], 16 * (j // nsem + 1)
                        else:
                            k_, s_, v_ = ("c", p["eng"]), esem[p["eng"]], p["eidx"]
                        if k_ not in req or req[k_][1] < v_:
                            req[k_] = (s_, v_)
                    for k_, (s_, v_) in req.items():
                        need(k_, s_, v_)
                    if o["dma"]:
                        j = o["dma_idx"]
                        if j >= nsem:
                            need(("d", j % nsem), dsem[j % nsem], 16 * (j // nsem))
                        o["fn"](eng).then_inc(dsem[j % nsem], 16)
                    else:
                        o["fn"](eng).then_inc(esem[engname], 1)
                for e2 in ENGS:
                    need(("c", e2), esem[e2], final_e[e2])
                for j in range(max(0, final_nd - nsem), final_nd):
                    need(("d", j % nsem), dsem[j % nsem], 16 * (j // nsem + 1))
            return body

        with nc.Block() as block:
            block.tensor(gen("tensor"))
            block.vector(gen("vector"))
            block.scalar(gen("scalar"))
            block.gpsimd(gen("gpsimd"))
            block.sync(gen("sync"))
        for g in reversed(self.stack):
            g.__exit__(None, None, None)
        self.stack = []


CFG_FULL = dict(D=4096, T=4096, TC=256, GW=64, MH=8, E=16)
POOL_WINDOWS = (2, 4, 8, 16)
LN_EPS = 1e-5
DEPTH = 2
DN_ALPHA = (2 * DEPTH) ** 0.25


class Builder:
    def __init__(self, cfg):
        self.cfg = cfg
        D, T, TC = cfg["D"], cfg["T"], cfg["TC"]
        self.D, self.T, self.TC, self.GW, self.MH, self.E = D, T, TC, cfg["GW"], cfg["MH"], cfg["E"]
        self.TT = T + TC
        self.KC = D // 128
        self.NT = self.TT // 128
        self.NTC = TC // 128
        self.DL = D // 2
        self.LH = self.DL // 128
        self.DP = D // 2
        self.PG = self.DP // 4
        self.DV = D // self.MH
        self.DQK = self.DV // 2
        self.QK = self.MH * self.DQK
        self.DE = D // 4
        self.CAP = 2 * T // self.E
        self.CAPC = 2 * TC // self.E
        self.nc = nc = bass.Bass("TRN2", target_bir_lowering=False)
        self.sy = None
        self.inp = {}
        self.scr = {}

    def din(self, name, shape):
        self.inp[name] = self.nc.dram_tensor(name, list(shape), F32, kind="ExternalInput").ap()
        return self.inp[name]

    def dscr(self, name, shape):
        kind = "ExternalOutput" if name in self.cfg.get("debug", ()) else "Internal"
        self.scr[name] = self.nc.dram_tensor(name, list(shape), F32, kind=kind).ap()
        return self.scr[name]

    def prog(self):
        return Prog(self.nc, self.sy)

    def bank_list(self, p, n=8):
        return [p.ps("bank%d" % i, [128, 512]) for i in range(n)]

    def ln_core(self, p, src, dst, st, mv, rstd, nmr, sfx, eng_apply="scalar"):
        D = src.shape[1]
        nch = D // 512 if D >= 512 else 1
        w = D // nch
        for c in range(nch):
            p.op("vector", lambda e, c=c: e.bn_stats(out=st[:, c, :], in_=src[:, c * w:(c + 1) * w]), reads=[src], writes=[st])
        p.op("vector", lambda e: e.bn_aggr(out=mv[:], in_=st[:, 0:nch, :].rearrange("p a b -> p (a b)")), reads=[st], writes=[mv])
        p.ts(rstd[:], mv[:, 1:2], LN_EPS, None, ALU.add)
        p.act(rstd[:], rstd[:], AF.Ln)
        p.act(rstd[:], rstd[:], AF.Exp, scale=-0.5)
        p.stt(nmr[:], mv[:, 0:1], -1.0, rstd[:], ALU.mult, ALU.mult)
        p.act(dst, src, AF.Identity, bias=nmr[:], scale=rstd[:])

    def transpose_tile(self, p, src, dst, ident, banks, nblk, bi0=0):
        g = 0
        b = 0
        while b < nblk:
            n = min(4, nblk - b)
            bank = banks[(bi0 + g) % len(banks)]
            for j in range(n):
                p.tr(bank[:, j * 128:(j + 1) * 128], src[:, (b + j) * 128:(b + j + 1) * 128], ident[:])
            o = dst[:, b:b + n, :]
            i_ = bank[:, 0:n * 128].rearrange("p (a t) -> p a t", t=128)
            if g % 2 == 0:
                p.copy(o, i_, eng="vector")
            else:
                p.copy(o, i_, eng="scalar")
            b += n
            g += 1

    def load_bcast(self, p, dst, row_ap):
        p.dma(dst, row_ap.partition_broadcast(128))

    def gemm_fm(self, W, inT, outT, K, N, tcols, scale=None, row_scale=None, TS=256):
        p = self.prog()
        kc = K // 128
        banks = self.bank_list(p)
        xin = [p.sb("gx%d" % i, [128, kc, TS]) for i in range(2)]
        wch = [p.sb("gw%d" % i, [128, kc, 128]) for i in range(3)]
        ob = [p.sb("go%d" % i, [128, TS]) for i in range(3)]
        rs = None
        if row_scale is not None:
            rs = p.sb("grs", [128, (N + 127) // 128])
            p.dma(rs[:], row_scale, allow_slow_non_contiguous=True)
        it = 0
        for ti, (t0, tsz) in enumerate(tcols):
            x = xin[ti % 2]
            p.dma(x[:, :, 0:tsz], inT[:, t0:t0 + tsz].rearrange("(c p) t -> p c t", p=128))
            for nb in range((N + 127) // 128):
                m = min(128, N - nb * 128)
                w = wch[it % 3]
                p.dma(w[:, :, 0:m], W[:, nb * 128:nb * 128 + m].rearrange("(c p) m -> p c m", p=128))
                bank = banks[it % 8]
                for c in range(kc):
                    p.mm(bank[0:m, 0:tsz], w[:, c, 0:m], x[:, c, 0:tsz], start=(c == 0), stop=(c == kc - 1))
                o = ob[it % 3]
                if rs is not None:
                    p.ts(o[0:m, 0:tsz], bank[0:m, 0:tsz], rs[0:m, nb:nb + 1], None, ALU.mult)
                elif scale is not None:
                    p.act(o[0:m, 0:tsz], bank[0:m, 0:tsz], AF.Copy, scale=scale)
                elif it % 2 == 0:
                    p.copy(o[0:m, 0:tsz], bank[0:m, 0:tsz], eng="vector")
                else:
                    p.copy(o[0:m, 0:tsz], bank[0:m, 0:tsz], eng="scalar")
                p.dma(outT[nb * 128:nb * 128 + m, t0:t0 + tsz], o[0:m, 0:tsz], eng="gpsimd",
                      writes=[(_key(outT), ti, nb)])
                it += 1
        p.emit()

    def gemm_tm(self, inT, W, out, K, N, tiles):
        p = self.prog()
        kc = K // 128
        banks = self.bank_list(p)
        wb = p.sb("hw", [128, kc, 512])
        xin = [p.sb("hx%d" % i, [128, kc, 128]) for i in range(2)]
        ob = [p.sb("ho%d" % i, [128, 512]) for i in range(3)]
        it = 0
        for nb in range((N + 511) // 512):
            n = min(512, N - nb * 512)
            for c0 in range(0, kc, 8):
                c1 = min(kc, c0 + 8)
                p.dma(wb[:, c0:c1, 0:n], W[c0 * 128:c1 * 128, nb * 512:nb * 512 + n].rearrange("(c p) n -> p c n", p=128))
            for i in tiles:
                x = xin[it % 2]
                p.dma(x[:], inT[:, i * 128:(i + 1) * 128].rearrange("(c p) t -> p c t", p=128))
                bank = banks[it % 8]
                for c in range(kc):
                    p.mm(bank[:, 0:n], x[:, c, :], wb[:, c, 0:n], start=(c == 0), stop=(c == kc - 1))
                o = ob[it % 3]
                p.copy(o[:, 0:n], bank[:, 0:n], eng="vector" if it % 2 == 0 else "scalar")
                p.dma(out[i * 128:(i + 1) * 128, nb * 512:nb * 512 + n], o[:, 0:n], eng="gpsimd",
                      writes=[(_key(out), i, nb)])
                it += 1
        p.emit()

    def phase_adaln(self):
        D, KC = self.D, self.KC
        p = self.prog()
        banks = self.bank_list(p)
        cT = p.sb("cT", [128, KC, 2])
        sil = p.sb("sil", [128, KC, 2])
        p.dma(cT[:], self.inp["condT"])
        p.act(sil[:], cT[:], AF.Silu)
        NB = 2048
        wbuf = [p.sb("aw%d" % i, [128, NB]) for i in range(3)]
        bb = [p.sb("ab%d" % i, [2, NB]) for i in range(2)]
        obuf = [p.sb("ao%d" % i, [2, NB]) for i in range(2)]
        it = 0
        blk = 0
        for l in range(2):
            for nb in range(6 * D // NB):
                bset = banks[0:4] if blk % 2 == 0 else banks[4:8]
                b_ = bb[blk % 2]
                p.dma(b_[:], self.inp["ada_b"][l:l + 1, nb * NB:(nb + 1) * NB].partition_broadcast(2))
                for c in range(KC):
                    w = wbuf[it % 3]
                    it += 1
                    p.dma(w[:], self.inp["ada_w"][l, c * 128:(c + 1) * 128, nb * NB:(nb + 1) * NB])
                    for j in range(4):
                        p.mm(bset[j][0:2, :], sil[:, c, :], w[:, j * 512:(j + 1) * 512], start=(c == 0), stop=(c == KC - 1))
                o = obuf[blk % 2]
                for j in range(4):
                    p.tt(o[:, j * 512:(j + 1) * 512], bset[j][0:2, :], b_[:, j * 512:(j + 1) * 512], ALU.add)
                p.dma(self.scr["mods"][l, :, nb * NB:(nb + 1) * NB], o[:], eng="gpsimd", writes=[("mods", l, nb)])
                blk += 1
        p.emit()

    def mod_row(self, l, cnd, seg):
        D = self.D
        return self.scr["mods"][l, cnd:cnd + 1, seg * D:(seg + 1) * D]

    def phase_ln0(self):
        D = self.D
        p = self.prog()
        xb = [p.sb("xb%d" % i, [128, D]) for i in range(2)]
        hb = [p.sb("hb%d" % i, [128, D]) for i in range(2)]
        st = p.sb("st", [128, 8, 6]); mv = p.sb("mv", [128, 2]); rstd = p.sb("rstd", [128, 1]); nmr = p.sb("nmr", [128, 1])
        for i in range(self.NT):
            x = xb[i % 2]; h = hb[i % 2]
            p.dma(x[:], self.inp["xin"][i * 128:(i + 1) * 128, :])
            self.ln_core(p, x[:], h[:], st, mv, rstd, nmr, "")
            p.dma(self.scr["h"][i * 128:(i + 1) * 128, :], h[:], eng="gpsimd", writes=[("h", i)])
        p.emit()

    def phase_u(self, l):
        D, KC = self.D, self.KC
        p = self.prog()
        banks = self.bank_list(p)
        ident = p.sb("ident", [128, 128]); p.dma(ident[:], self.inp["ident"])
        rows = {}
        for cnd in range(2):
            sc = p.sb("sc%d" % cnd, [128, D]); sh = p.sb("sh%d" % cnd, [128, D])
            self.load_bcast(p, sc[:], self.mod_row(l, cnd, 1))
            self.load_bcast(p, sh[:], self.mod_row(l, cnd, 0))
            p.ts(sc[:], sc[:], 1.0, None, ALU.add)
            rows[cnd] = (sc, sh)
        hb = [p.sb("uh%d" % i, [128, D]) for i in range(2)]
        ub = [p.sb("uu%d" % i, [128, D]) for i in range(2)]
        ut = [p.sb("ut%d" % i, [128, KC, 128]) for i in range(2)]
        for i in range(self.NT):
            cnd = 1 if i < self.NTC else 0
            sc, sh = rows[cnd]
            h = hb[i % 2]; u = ub[i % 2]; t = ut[i % 2]
            p.dma(h[:], self.scr["h"][i * 128:(i + 1) * 128, :])
            p.tt(u[:], h[:], sc[:], ALU.mult, eng="gpsimd")
            p.tt(u[:], u[:], sh[:], ALU.add, eng="vector")
            self.transpose_tile(p, u, t, ident, banks, KC, bi0=i * 3)
            p.dma(self.scr["uT"][:, i * 128:(i + 1) * 128].rearrange("(c p) t -> p c t", p=128), t[:], eng="gpsimd",
                  writes=[("uT", i)])
        p.emit()

    def tcols(self, TS=256, lat_only=False):
        out = []
        t = self.TC if lat_only else 0
        while t < self.TT:
            s = min(TS, self.TT - t)
            out.append((t, s))
            t += s
        return out

    def phase_lru(self):
        TT, TC, T, LH, DL = self.TT, self.TC, self.T, self.LH, self.DL
        pT, moT, pp = self.scr["pT"], self.scr["moT"], self.inp["ppar"]
        p = self.prog()
        banks = self.bank_list(p)
        par = p.sb("par", [128, pp.shape[1]]); p.dma(par[:], pp)
        o_cw, o_cb, o_gb, o_lam = 0, 4 * LH, 5 * LH, 9 * LH
        c8 = p.sb("c8", [128, 2 * LH])
        p.act(c8[:], par[:, o_lam:o_lam + 2 * LH], AF.Exp, scale=-1.0)
        p.ts(c8[:], c8[:], 1.0, None, ALU.add)
        p.act(c8[:], c8[:], AF.Ln)
        p.ts(c8[:], c8[:], -8.0, None, ALU.mult)
        xs = [p.sb("xs%d" % i, [128, TT]) for i in range(1)]
        zs = [p.sb("zs%d" % i, [128, TT]) for i in range(1)]
        y = p.sb("y", [128, TT])
        gw = [p.sb("gw%d" % i, [128, 4, 128]) for i in range(2)]
        aa = [p.sb("aa%d" % d, [128, TT]) for d in range(2)]
        bx = [p.sb("bx%d" % d, [128, TT]) for d in range(2)]
        hh = [p.sb("hh%d" % d, [128, TT]) for d in range(2)]
        t1 = [p.sb("t1_%d" % i, [128, 512]) for i in range(2)]
        t2 = [p.sb("t2_%d" % i, [128, 512]) for i in range(2)]
        tcs = self.tcols(512)
        it = 0
        for hd in range(LH):
            x = xs[0]; z = zs[0]; g = gw[hd % 2]
            p.dma(x[:], pT[hd * 128:(hd + 1) * 128, :])
            p.dma(z[:], pT[DL + hd * 128:DL + (hd + 1) * 128, :])
            p.dma(g[:], self.inp["lru_gate_w"][:, :, hd, :, :].rearrange("d g i j -> i (d g) j"))
            cw = lambda k: par[:, o_cw + hd * 4 + k:o_cw + hd * 4 + k + 1]
            cb = par[:, o_cb + hd:o_cb + hd + 1]
            for (s0, L) in ((0, TC), (TC, T)):
                p.ts(y[:, s0:s0 + L], x[:, s0:s0 + L], cw(1), cb, ALU.mult, ALU.add)
                p.stt(y[:, s0 + 1:s0 + L], x[:, s0:s0 + L - 1], cw(0), y[:, s0 + 1:s0 + L], ALU.mult, ALU.add)
                p.stt(y[:, s0:s0 + L - 1], x[:, s0 + 1:s0 + L], cw(2), y[:, s0:s0 + L - 1], ALU.mult, ALU.add)
                p.stt(y[:, s0:s0 + L - 2], x[:, s0 + 2:s0 + L], cw(3), y[:, s0:s0 + L - 2], ALU.mult, ALU.add)
            for d in range(2):
                gbr = par[:, o_gb + hd * 4 + d * 2:o_gb + hd * 4 + d * 2 + 1]
                gbi = par[:, o_gb + hd * 4 + d * 2 + 1:o_gb + hd * 4 + d * 2 + 2]
                c8d = c8[:, hd * 2 + d:hd * 2 + d + 1]
                for (t0, ts_) in tcs:
                    br = banks[it % 8]; bi = banks[(it + 1) % 8]
                    a1 = t1[(it // 2) % 2]; a2 = t2[(it // 2) % 2]
                    it += 2
                    p.mm(br[:, 0:ts_], g[:, d * 2, :], y[:, t0:t0 + ts_])
                    p.mm(bi[:, 0:ts_], g[:, d * 2 + 1, :], y[:, t0:t0 + ts_])
                    av = aa[d][:, t0:t0 + ts_]
                    p.act(a1[:, 0:ts_], br[:, 0:ts_], AF.Sigmoid, bias=gbr)
                    p.act(a2[:, 0:ts_], bi[:, 0:ts_], AF.Sigmoid, bias=gbi)
                    p.act(av, a1[:, 0:ts_], AF.Exp, scale=c8d)
                    p.tt(a1[:, 0:ts_], av, av, ALU.mult)
                    p.ts(a1[:, 0:ts_], a1[:, 0:ts_], -1.0, 1.0, ALU.mult, ALU.add)
                    p.act(a1[:, 0:ts_], a1[:, 0:ts_], AF.Sqrt)
                    p.tt(a2[:, 0:ts_], a2[:, 0:ts_], y[:, t0:t0 + ts_], ALU.mult, eng="gpsimd")
                    p.tt(bx[d][:, t0:t0 + ts_], a1[:, 0:ts_], a2[:, 0:ts_], ALU.mult)
            p.op("vector", lambda e: e.tensor_tensor_scan(out=hh[0][:], data0=aa[0][:], data1=bx[0][:], initial=0.0,
                                                          op0=ALU.mult, op1=ALU.add), reads=[aa[0], bx[0]], writes=[hh[0]])
            p.op("vector", lambda e: e.tensor_tensor_scan(out=hh[1][:, 0:TC][:, ::-1], data0=aa[1][:, 0:TC][:, ::-1],
                                                          data1=bx[1][:, 0:TC][:, ::-1], initial=0.0,
                                                          op0=ALU.mult, op1=ALU.add), reads=[aa[1], bx[1]], writes=[hh[1]])
            p.op("vector", lambda e: e.tensor_tensor_scan(out=hh[1][:, TC:TT][:, ::-1], data0=aa[1][:, TC:TT][:, ::-1],
                                                          data1=bx[1][:, TC:TT][:, ::-1], initial=hh[1][:, 0:1],
                                                          op0=ALU.mult, op1=ALU.add), reads=[aa[1], bx[1], hh[1]], writes=[hh[1]])
            p.tt(hh[0][:], hh[0][:], hh[1][:], ALU.add, eng="gpsimd")
            p.tt(y[:], z[:], z[:], ALU.mult)
            p.ts(y[:], y[:], 0.044715, 1.0, ALU.mult, ALU.add)
            p.tt(y[:], y[:], z[:], ALU.mult)
            p.act(y[:], y[:], AF.Sigmoid, scale=1.5957691216057308)
            p.tt(y[:], y[:], z[:], ALU.mult, eng="gpsimd")
            p.tt(y[:], y[:], hh[0][:], ALU.mult)
            p.dma(moT[hd * 128:(hd + 1) * 128, :], y[:], eng="gpsimd", writes=[("moT", hd)])
        p.emit()

    def phase_pool(self):
        TT, TC, T, GW, PG, DL = self.TT, self.TC, self.T, self.GW, self.PG, self.DL
        GH = T // GW
        pT = self.scr["pT"]
        p = self.prog()
        xs = [p.sb("px%d" % i, [128, TT]) for i in range(2)]
        inv = p.sb("inv", [128, TT])
        P1 = p.sb("P1", [128, GH, GW + 16]); P2 = p.sb("P2", [128, GH + 16, GW])
        Sa = p.sb("Sa", [128, GH * (GW + 16)]); Sb = p.sb("Sb", [128, GH * (GW + 16)])
        P3 = p.sb("P3", [128, TC + 16]); S3a = p.sb("S3a", [128, TC + 16]); S3b = p.sb("S3b", [128, TC + 16])
        dd = [p.sb("dd%d" % i, [128, TT]) for i in range(2)]
        p.memset(P1[:], 0.0); p.memset(P2[:], 0.0, eng="gpsimd"); p.memset(P3[:], 0.0)

        def stages(src, tmpa, tmpb, dst, nst, ax, L):
            def sl(t, lo, hi):
                if len(t.shape) == 2:
                    return t[:, 8 + lo:8 + hi]
                if ax == 1:
                    return t[:, 8 + lo:8 + hi, :]
                return t[:, :, 8 + lo:8 + hi]
            cur = src
            shifts = [(-1, 0), (-1, 1), (-2, 2), (-4, 4)]
            rng = [(-7, L + 8), (-6, L + 7), (-4, L + 5), (0, L)]
            bufs = [tmpa, tmpb, tmpa, tmpb]
            for s_ in range(nst):
                lo, hi = rng[s_]
                if s_ == nst - 1:
                    lo, hi = 0, L
                    o = dst
                else:
                    o = sl(bufs[s_], lo, hi)
                a_, b_ = shifts[s_]
                p.tt(o, sl(cur, lo + a_, hi + a_), sl(cur, lo + b_, hi + b_), ALU.add)
                cur = bufs[s_]

        for g, w in enumerate(POOL_WINDOWS):
            nst = {2: 1, 4: 2, 8: 3, 16: 4}[w]
            self.load_bcast(p, inv[:], self.inp["invcnt"][g:g + 1, :])
            for c in range(PG // 128):
                idx = g * (PG // 128) + c
                x = xs[idx % 2]; d_ = dd[idx % 2]
                row = 2 * DL + g * PG + c * 128
                p.dma(x[:], pT[row:row + 128, :])
                p.copy(P1[:, :, 8:8 + GW], x[:, TC:TT].rearrange("p (a b) -> p a b", b=GW), eng="gpsimd")
                stages(P1[:], Sa[:].rearrange("p (a b) -> p a b", b=GW + 16), Sb[:].rearrange("p (a b) -> p a b", b=GW + 16),
                       P2[:, 8:8 + GH, :], nst, 2, GW)
                stages(P2[:], Sa[:].rearrange("p (a b) -> p a b", b=GW), Sb[:].rearrange("p (a b) -> p a b", b=GW),
                       d_[:, TC:TT].rearrange("p (a b) -> p a b", b=GW), nst, 1, GH)
                p.copy(P3[:, 8:8 + TC], x[:, 0:TC], eng="gpsimd")
                stages(P3[:], S3a[:], S3b[:], d_[:, 0:TC], nst, 1, TC)
                p.tt(d_[:], d_[:], inv[:], ALU.mult)
                p.tt(d_[:], d_[:], x[:], ALU.subtract, eng="gpsimd")
                p.dma(self.scr["dT"][idx * 128:(idx + 1) * 128, :], d_[:], eng="gpsimd", writes=[("dT", idx)])
        p.emit()

    def phase_post(self, l, which, tiles, with_router):
        D, KC, E = self.D, self.KC, self.E
        p = self.prog()
        banks = self.bank_list(p)
        ident = p.sb("ident", [128, 128]); p.dma(ident[:], self.inp["ident"])
        gseg = 2 if which == 0 else 5
        ysrc = self.scr["y"] if which == 0 else self.scr["m"]
        lg = p.sb("lg", [128, D]); lb = p.sb("lb", [128, D])
        self.load_bcast(p, lg[:], self.inp["ln_g"][l, which:which + 1, :])
        self.load_bcast(p, lb[:], self.inp["ln_b"][l, which:which + 1, :])
        gr = p.sb("gr", [128, D])
        hb = [p.sb("ph%d" % i, [128, D]) for i in range(2)]
        yb = [p.sb("py%d" % i, [128, D]) for i in range(2)]
        st = p.sb("st", [128, 8, 6]); mv = p.sb("mv", [128, 2]); rstd = p.sb("rstd", [128, 1]); nmr = p.sb("nmr", [128, 1])
        if with_router:
            s2 = p.sb("s2", [128, D]); h2 = p.sb("h2", [128, D])
            wr = p.sb("wr", [128, KC, E]); p.dma(wr[:], self.inp["router_w"][l].rearrange("(c p) e -> p c e", p=128))
            xt = p.sb("xt", [128, KC, 128])
            lgt = p.sb("lgt", [128, E]); mx = p.sb("mx", [128, 1]); sm = p.sb("sm", [128, 1])
            affT = p.sb("affT", [E, self.TT])
        n_ = 0
        for cnd in (1, 0):
            grp = [i for i in tiles if (1 if i < self.NTC else 0) == cnd]
            if not grp:
                continue
            self.load_bcast(p, gr[:], self.mod_row(l, cnd, gseg))
            if with_router:
                self.load_bcast(p, s2[:], self.mod_row(l, cnd, 4))
                p.ts(s2[:], s2[:], 1.0, None, ALU.add)
                self.load_bcast(p, h2[:], self.mod_row(l, cnd, 3))
            for i in grp:
                h = hb[n_ % 2]; y = yb[n_ % 2]
                p.dma(h[:], self.scr["h"][i * 128:(i + 1) * 128, :])
                p.dma(y[:], ysrc[i * 128:(i + 1) * 128, :])
                p.tt(y[:], y[:], gr[:], ALU.mult, eng="gpsimd")
                p.stt(y[:], h[:], DN_ALPHA, y[:], ALU.mult, ALU.add)
                self.ln_core(p, y[:], h[:], st, mv, rstd, nmr, "")
                p.tt(h[:], h[:], lg[:], ALU.mult, eng="gpsimd")
                p.tt(h[:], h[:], lb[:], ALU.add)
                p.dma(self.scr["h"][i * 128:(i + 1) * 128, :], h[:], eng="gpsimd")
                if with_router:
                    p.tt(y[:], h[:], s2[:], ALU.mult, eng="gpsimd")
                    p.tt(y[:], y[:], h2[:], ALU.add)
                    p.dma(self.scr["xm"][i * 128:(i + 1) * 128, :], y[:], eng="gpsimd", writes=[("xm", i)])
                    self.transpose_tile(p, y, xt, ident, banks[0:6], KC, bi0=n_)
                    lb_ = banks[6]
                    for c in range(KC):
                        p.mm(lb_[:, 0:E], xt[:, c, :], wr[:, c, :], start=(c == 0), stop=(c == KC - 1))
                    p.op("vector", lambda e, lb_=lb_: e.reduce_max(out=mx[:], in_=lb_[:, 0:E], axis=mybir.AxisListType.X),
                         reads=[lb_], writes=[mx])
                    p.ts(mx[:], mx[:], -1.0, None, ALU.mult)
                    p.act(lgt[:], lb_[:, 0:E], AF.Exp, bias=mx[:])
                    p.op("vector", lambda e: e.reduce_sum(out=sm[:], in_=lgt[:], axis=mybir.AxisListType.X), reads=[lgt], writes=[sm])
                    p.op("vector", lambda e: e.reciprocal(out=sm[:], in_=sm[:]), reads=[sm], writes=[sm])
                    p.ts(lgt[:], lgt[:], sm[:], None, ALU.mult)
                    p.dma(self.scr["aff"][i * 128:(i + 1) * 128, :], lgt[:], eng="gpsimd", writes=[("aff", i)])
                    tb = banks[7]
                    p.tr(tb[0:E, 0:128], lgt[:], ident[:])
                    p.copy(affT[:, i * 128:(i + 1) * 128], tb[0:E, 0:128])
                n_ += 1
        if with_router:
            p.dma(self.scr["affT"], affT[:], eng="gpsimd")
        p.emit()

    def phase_route(self, t0, n, cap, tag):
        E = self.E
        SC = (cap + 127) // 128
        p = self.prog()
        banks = self.bank_list(p)
        ident = p.sb("ident", [128, 128]); p.dma(ident[:], self.inp["ident"])
        a = p.sb("ra", [E, n]); w = [p.sb("rw%d" % i, [E, n]) for i in range(2)]
        m8 = p.sb("m8", [E, 8]); ones = p.sb("ones", [E, n]); cs = p.sb("cs", [E, n]); msk = p.sb("msk", [E, n])
        idxE = p.sb("idxE", [E, SC * 128])
        p.dma(a[:], self.scr["affT"][:, t0:t0 + n])
        p.memset(ones[:], 1.0, eng="gpsimd")
        cur = a
        for r in range(cap // 8):
            p.op("vector", lambda e, cur=cur: e.max(out=m8[:], in_=cur[:]), reads=[cur], writes=[m8])
            if r < cap // 8 - 1:
                nxt = w[r % 2]
                p.op("vector", lambda e, cur=cur, nxt=nxt: e.match_replace(out=nxt[:], in_to_replace=m8[:], in_values=cur[:],
                                                                          imm_value=-1.0), reads=[cur, m8], writes=[nxt])
                cur = nxt
        p.ts(msk[:], a[:], m8[:, 7:8], None, ALU.is_ge)
        p.op("vector", lambda e: e.tensor_tensor_scan(out=cs[:], data0=ones[:], data1=msk[:], initial=0.0,
                                                      op0=ALU.mult, op1=ALU.add), reads=[ones, msk], writes=[cs])
        junk = w[0]
        io = p.sb("io", [E, 512]); p.dma(io[:], self.inp["iota"][0:E, :])
        p.ts(io[:], io[:], -1.0, -0.5, ALU.mult, ALU.add)
        for s_ in range(SC * 128):
            p.op("scalar", lambda e, s_=s_: e.activation(out=junk[:], in_=cs[:], func=AF.Sign, bias=io[:, s_:s_ + 1], scale=1.0,
                                                         accum_out=idxE[:, s_:s_ + 1]),
                 reads=[cs, io], writes=[junk, (_key(idxE), s_)])
        p.op("vector", lambda e: e.tensor_scalar(out=idxE[:], in0=idxE[:], scalar1=-0.5, scalar2=float(n) / 2 + float(t0),
                                                 op0=ALU.mult, op1=ALU.add),
             reads=[idxE] + [(_key(idxE), s_) for s_ in range(SC * 128)], writes=[idxE])
        it_ = p.sb("it", [128, SC, E])
        for i in range(SC):
            b = banks[i % 8]
            p.tr(b[:, 0:E], idxE[:, i * 128:(i + 1) * 128], ident[0:E, 0:E])
            p.copy(it_[:, i, :], b[:, 0:E])
        p.dma(self.scr["idx_" + tag], it_[:], eng="gpsimd")
        p.emit()

    def phase_moe(self, l, t0, n, cap, tag):
        D, KC, E, DE, TT = self.D, self.KC, self.E, self.DE, self.TT
        nt = n // 128
        FC = DE // 128
        SC = (cap + 127) // 128
        CW = SC * 128
        xm, mT, aff = self.scr["xm"], self.scr["m"], self.scr["aff"]
        p = self.prog()
        banks = self.bank_list(p)
        ident = p.sb("ident", [128, 128]); p.dma(ident[:], self.inp["ident"])
        idf = p.sb("idf", [128, SC, E]); p.dma(idf[:], self.scr["idx_" + tag])
        idx = p.sb("idx", [128, SC, E], mybir.dt.int32)
        ixh = [p.sb("ixh%d" % h, [128, SC, E], mybir.dt.int32) for h in range(2)]
        idh = p.sb("idh", [128, SC, E])
        p.copy(idx[:], idf[:])
        for h in range(2):
            p.ts(idh[:], idf[:], 2.0, float(h), ALU.mult, ALU.add)
            p.copy(ixh[h][:], idh[:])
        HW = D // 2
        xm2 = xm.rearrange("t (h c) -> (t h) c", h=2)
        m2 = mT.rearrange("t (h c) -> (t h) c", h=2)
        xg = [p.sb("xg%d" % i, [128, D]) for i in range(1)]
        gs = [p.sb("gs%d" % i, [128, E]) for i in range(SC)]
        for t_ in xg + gs:
            p.memset(t_[:], 0.0, eng="gpsimd")
        for i in range(nt):
            p.dma(mT[t0 + i * 128:t0 + (i + 1) * 128, :], xg[0][:], eng="gpsimd", writes=["m"])
        xe = p.sb("xe", [128, KC, CW])
        aT = p.sb("aT", [128, FC, CW])
        wg = [p.sb("wg%d" % i, [128, KC, 128]) for i in range(2)]
        wd = p.sb("wd", [128, FC, 256])
        sg = [p.sb("sg%d" % i, [128, CW]) for i in range(2)]
        ye = [p.sb("ye%d" % i, [128, D]) for i in range(SC)]
        bound = t0 + n - 1
        breg = {}

        def bnd(g, k, v):
            if k not in breg:
                breg[k] = g.to_reg(v)
            return breg[k]
        it = 0
        for e in range(E):
            for s in range(SC):
                x_ = xg[0]
                ix = idx[:, s, e:e + 1]
                for h in range(2):
                    ixh_ = ixh[h][:, s, e:e + 1]
                    p.op("gpsimd", lambda g, x_=x_, ixh_=ixh_, h=h: g.indirect_dma_start(
                        out=x_[:, h * HW:(h + 1) * HW], out_offset=None, in_=xm2[:, :],
                        in_offset=bass.IndirectOffsetOnAxis(ap=ixh_, axis=0),
                        bounds_check=bnd(g, "b2", 2 * bound + 1), oob_is_err=False), reads=[xm, ixh[h]], writes=[x_], dma=True)
                p.op("gpsimd", lambda g, g_=gs[s], ix=ix: g.indirect_dma_start(
                    out=g_[:, :], out_offset=None, in_=aff[:, :], in_offset=bass.IndirectOffsetOnAxis(ap=ix, axis=0),
                    bounds_check=bnd(g, "b1", bound), oob_is_err=False), reads=[aff, idx], writes=[gs[s]], dma=True)
                for c0 in range(0, KC, 4):
                    nb_ = min(4, KC - c0)
                    bk = banks[it % 4]
                    it += 1
                    for j in range(nb_):
                        p.tr(bk[:, j * 128:(j + 1) * 128], x_[:, (c0 + j) * 128:(c0 + j + 1) * 128], ident[:])
                    p.copy(xe[:, c0:c0 + nb_, s * 128:(s + 1) * 128], bk[:, 0:nb_ * 128].rearrange("p (a t) -> p a t", t=128),
                           eng="vector" if it % 2 == 0 else "scalar")
            for f in range(FC):
                wg_ = wg[0]; wu_ = wg[1]
                p.dma(wg_[:], self.inp["expert_w_gu"][l, e, :, f * 128:(f + 1) * 128].rearrange("(c p) m -> p c m", p=128))
                p.dma(wu_[:], self.inp["expert_w_gu"][l, e, :, DE + f * 128:DE + (f + 1) * 128].rearrange("(c p) m -> p c m", p=128))
                bg = banks[4 + (f % 2) * 2]; bu = banks[5 + (f % 2) * 2]
                for c in range(KC):
                    p.mm(bg[:, 0:CW], wg_[:, c, :], xe[:, c, :], start=(c == 0), stop=(c == KC - 1))
                for c in range(KC):
                    p.mm(bu[:, 0:CW], wu_[:, c, :], xe[:, c, :], start=(c == 0), stop=(c == KC - 1))
                s_ = sg[f % 2]
                p.act(s_[:], bg[:, 0:CW], AF.Silu)
                p.tt(aT[:, f, :], s_[:], bu[:, 0:CW], ALU.mult)
            for nb in range(D // 256):
                p.dma(wd[:], self.inp["expert_w_down"][l, e, :, nb * 256:(nb + 1) * 256].rearrange("(c p) n -> p c n", p=128))
                for s in range(SC):
                    bk = banks[it % 4]
                    it += 1
                    for f in range(FC):
                        p.mm(bk[:, 0:256], aT[:, f, s * 128:(s + 1) * 128], wd[:, f, :], start=(f == 0), stop=(f == FC - 1))
                    p.ts(ye[s][:, nb * 256:(nb + 1) * 256], bk[:, 0:256], gs[s][:, e:e + 1], None, ALU.mult,
                         eng="vector")
            for s in range(SC):
                for h in range(2):
                    ixh_ = ixh[h][:, s, e:e + 1]
                    p.op("gpsimd", lambda g, y_=ye[s], ixh_=ixh_, h=h: g.indirect_dma_start(
                        out=m2[:, :], out_offset=bass.IndirectOffsetOnAxis(ap=ixh_, axis=0), in_=y_[:, h * HW:(h + 1) * HW],
                        in_offset=None, compute_op=ALU.add), reads=[ye[s], ixh[h], "m"], writes=["m"], dma=True)
        p.emit()

    def phase_out(self):
        p = self.prog()
        hb = [p.sb("fo%d" % i, [128, self.D]) for i in range(2)]
        for n_, i in enumerate(range(self.NTC, self.NT)):
            h = hb[n_ % 2]
            p.dma(h[:], self.scr["h"][i * 128:(i + 1) * 128, :])
            p.dma(self.out[(i - self.NTC) * 128:(i - self.NTC + 1) * 128, :], h[:], eng="gpsimd")
        p.emit()

    def phase_mlstm_gates(self):
        MH, TT, TC, NT, NTC = self.MH, self.TT, self.TC, self.NT, self.NTC
        p = self.prog()
        banks = self.bank_list(p)
        ident = p.sb("ident", [128, 128]); p.dma(ident[:], self.inp["ident"])
        gb = p.sb("gb", [MH, 4]); p.dma(gb[:], self.inp["ogb"])
        ones = p.sb("ones", [MH, TT])
        zc = p.sb("zc", [MH, 1])
        p.memset(ones[:], 1.0); p.memset(zc[:], 0.0)
        ig = p.sb("ig", [MH, TT]); fg = p.sb("fg", [MH, TT])
        Fc = p.sb("Fc", [MH, TT]); R = p.sb("R", [MH, TT])
        arr = [p.sb("q%d" % k, [MH, TT]) for k in range(5)] + [fg]
        dsc = p.sb("dsc", [MH, NT]); nre = p.sb("nre", [MH, NT])
        S = p.sb("S", [128, NT, 6, MH])
        for d in range(2):
            p.dma(ig[:], self.scr["gT"][(d * 2) * MH:(d * 2 + 1) * MH, :])
            p.dma(fg[:], self.scr["gT"][(d * 2 + 1) * MH:(d * 2 + 2) * MH, :])
            p.ts(ig[:], ig[:], gb[:, d * 2:d * 2 + 1], None, ALU.add)
            p.ts(fg[:], fg[:], gb[:, d * 2 + 1:d * 2 + 2], None, ALU.add)
            p.act(fg[:], fg[:], AF.Exp, scale=-1.0)
            p.ts(fg[:], fg[:], 1.0, None, ALU.add)
            p.act(fg[:], fg[:], AF.Ln)
            p.ts(fg[:], fg[:], -1.0, None, ALU.mult)

            def scan(out, d1, op1, segs):
                prev = None
                for (s0, s1) in segs:
                    def v(t):
                        x = t[:, s0:s1]
                        return x[:, ::-1] if d == 1 else x
                    init = 0.0 if prev is None else prev
                    p.op("vector", lambda e, o=v(out), a_=v(ones), b_=v(d1), init=init: e.tensor_tensor_scan(
                        out=o, data0=a_, data1=b_, initial=init, op0=ALU.mult, op1=op1),
                        reads=[ones, d1, out], writes=[out])
                    prev = out[:, s1 - 1:s1] if d == 0 else out[:, s0:s0 + 1]
            segs = [(0, TC), (TC, TT)]
            scan(Fc, fg, ALU.add, segs)
            p.tt(ig[:], ig[:], Fc[:], ALU.subtract)
            scan(R, ig, ALU.max, segs)
            p.tt(fg[:], Fc[:], R[:], ALU.add)
            p.act(fg[:], fg[:], AF.Exp, scale=-1.0)
            order = list(range(NT)) if d == 0 else (list(range(NTC - 1, -1, -1)) + list(range(NT - 1, NTC - 1, -1)))
            prev_idx = None
            for c in order:
                ch = slice(c * 128, (c + 1) * 128)
                ei = c * 128 + 127 if d == 0 else c * 128
                Fp = zc[:] if prev_idx is None else Fc[:, prev_idx:prev_idx + 1]
                Rp = zc[:] if prev_idx is None else R[:, prev_idx:prev_idx + 1]
                p.ts(nre[:, c:c + 1], R[:, ei:ei + 1], -1.0, None, ALU.mult)
                p.tt(dsc[:, c:c + 1], Rp, R[:, ei:ei + 1], ALU.subtract)
                p.ts(arr[0][:, ch], R[:, ch], Fp, None, ALU.add)
                p.act(arr[0][:, ch], arr[0][:, ch], AF.Exp, scale=-1.0)
                p.act(arr[1][:, ch], ig[:, ch], AF.Exp, bias=Fp)
                p.act(arr[2][:, ch], R[:, ch], AF.Exp, scale=-1.0, bias=Rp)
                p.act(arr[3][:, ch], ig[:, ch], AF.Exp, bias=nre[:, c:c + 1])
                p.act(arr[4][:, ch], R[:, ch], AF.Exp, scale=0.0, bias=dsc[:, c:c + 1])
                prev_idx = ei
            for c in range(NT):
                b = banks[c % 8]
                for k in range(6):
                    p.tr(b[:, k * MH:(k + 1) * MH], arr[k][:, c * 128:(c + 1) * 128], ident[0:MH, 0:MH])
                p.copy(S[:, c, :, :], b[:, 0:6 * MH].rearrange("p (k h) -> p k h", h=MH), eng="vector" if c % 2 == 0 else "scalar")
            p.dma(self.scr["mS"][d], S[:], eng="gpsimd")
        p.emit()

    def phase_mlstm(self):
        MH, TT, TC, NT, NTC, DV, DQK, QK, D = self.MH, self.TT, self.TC, self.NT, self.NTC, self.DV, self.DQK, self.QK, self.D
        NLC = NT - NTC
        DC = DQK // 128
        qkT, kvo, moT = self.scr["qkT"], self.scr["kvo"], self.scr["moT"]
        p = self.prog()
        ps_s = p.ps("ps_s", [128, 512]); ps_a = p.ps("ps_a", [128, 512]); ps_b = p.ps("ps_b", [128, 512])
        ps_d = p.ps("ps_d", [128, 512]); ps_u = [p.ps("ps_u%d" % i, [128, 512]) for i in range(2)]
        ps_n = p.ps("ps_n", [128, 512]); ps_t = p.ps("ps_t", [128, 512])
        ident = p.sb("ident", [128, 128]); p.dma(ident[:], self.inp["ident"])
        mask = p.sb("mask", [128, 2, 128]); p.dma(mask[:], self.inp["mmask"])
        S = [p.sb("S%d" % d, [128, NT, 6, MH]) for d in range(2)]
        for d in range(2):
            p.dma(S[d][:], self.scr["mS"][d])
        ng = p.sb("ng", [128, DV])
        hbuf = p.sb("hbuf", [128, NLC, DV])
        Cst = p.sb("Cst", [128, DC, DV + 1])
        NB = 3
        qT = [p.sb("qT%d" % i, [128, DC, 128]) for i in range(NB)]
        kT = [p.sb("kT%d" % i, [128, DC, 128]) for i in range(NB)]
        kt = [p.sb("kt%d" % i, [128, DQK]) for i in range(NB)]
        vt = [p.sb("vt%d" % i, [128, DV + 1]) for i in range(NB)]
        for i in range(NB):
            p.memset(vt[i][:, DV:DV + 1], 1.0)
        S0 = [p.sb("S0_%d" % i, [128, 128]) for i in range(2)]
        kw = [p.sb("kw%d" % i, [128, DQK]) for i in range(2)]
        n1 = [p.sb("n1_%d" % i, [128, DV]) for i in range(2)]
        n2 = [p.sb("n2_%d" % i, [128, DV]) for i in range(2)]
        sm = [p.sb("sm%d" % i, [128, 4]) for i in range(2)]
        ot = [p.sb("ot%d" % i, [128, DV]) for i in range(2)]
        st = p.sb("st", [128, 8, 6]); mv = p.sb("mv", [128, 2]); rstd = p.sb("rstd", [128, 1]); nmr = p.sb("nmr", [128, 1])
        tT = [p.sb("tT%d" % i, [128, DV // 128, 128]) for i in range(2)]
        it = 0
        for hd in range(MH):
            self.load_bcast(p, ng[:], self.inp["odd_norm_g"][0:1, hd * DV:(hd + 1) * DV])
            for d in range(2):
                p.memset(Cst[:], 0.0)
                order = list(range(NT)) if d == 0 else (list(range(NTC - 1, -1, -1)) + list(range(NT - 1, NTC - 1, -1)))
                for c in order:
                    j = it % NB
                    it += 1
                    cs_ = slice(c * 128, (c + 1) * 128)
                    sc = lambda k: S[d][:, c, k, hd:hd + 1]
                    lat = c >= NTC
                    p.dma(kt[j][:], kvo[cs_, hd * DQK:(hd + 1) * DQK])
                    p.dma(vt[j][:, 0:DV], kvo[cs_, QK + hd * DV:QK + (hd + 1) * DV])
                    if lat:
                        p.dma(qT[j][:], qkT[hd * DQK:(hd + 1) * DQK, cs_].rearrange("(c p) t -> p c t", p=128))
                        p.dma(kT[j][:], qkT[QK + hd * DQK:QK + (hd + 1) * DQK, cs_].rearrange("(c p) t -> p c t", p=128))
                        for dc in range(DC):
                            p.mm(ps_s[:, 0:128], kT[j][:, dc, :], qT[j][:, dc, :], start=(dc == 0), stop=(dc == DC - 1))
                        s0 = S0[it % 2]
                        p.stt(s0[:], ps_s[:, 0:128], sc(1), mask[:, d, :], ALU.mult, ALU.mult)
                        p.mm(ps_a[:, 0:DV], s0[:], vt[j][:, 0:DV])
                        p.mm(ps_d[:, 0:1], s0[:], vt[j][:, DV:DV + 1])
                        for dc in range(DC):
                            p.mm(ps_b[:, 0:DV], qT[j][:, dc, :], Cst[:, dc, 0:DV], start=(dc == 0), stop=(dc == DC - 1))
                        for dc in range(DC):
                            p.mm(ps_d[:, 1:2], qT[j][:, dc, :], Cst[:, dc, DV:DV + 1], start=(dc == 0), stop=(dc == DC - 1))
                        a1 = n1[it % 2]; a2 = n2[it % 2]; q_ = sm[it % 2]
                        p.act(a1[:], ps_a[:, 0:DV], AF.Copy, scale=sc(0))
                        p.stt(a2[:], ps_b[:, 0:DV], sc(2), a1[:], ALU.mult, ALU.add)
                        p.ts(q_[:, 0:1], ps_d[:, 0:1], sc(0), None, ALU.mult)
                        p.stt(q_[:, 1:2], ps_d[:, 1:2], sc(2), q_[:, 0:1], ALU.mult, ALU.add)
                        p.ts(q_[:, 2:3], q_[:, 1:2], -1.0, None, ALU.mult)
                        p.tt(q_[:, 2:3], q_[:, 2:3], q_[:, 1:2], ALU.max)
                        p.ts(q_[:, 2:3], q_[:, 2:3], sc(5), None, ALU.max)
                        p.op("vector", lambda e, q_=q_: e.reciprocal(out=q_[:, 3:4], in_=q_[:, 2:3]), reads=[q_], writes=[q_])
                        hv = hbuf[:, c - NTC, :]
                        if d == 0:
                            p.ts(hv, a2[:], q_[:, 3:4], None, ALU.mult)
                        else:
                            p.stt(hv, a2[:], q_[:, 3:4], hv, ALU.mult, ALU.add)
                    k_ = kw[it % 2]
                    p.ts(k_[:], kt[j][:], sc(3), None, ALU.mult, eng="gpsimd")
                    for dc in range(DC):
                        p.mm(ps_u[dc % 2][:, 0:DV], k_[:, dc * 128:(dc + 1) * 128], vt[j][:, 0:DV])
                        p.mm(ps_n[:, dc:dc + 1], k_[:, dc * 128:(dc + 1) * 128], vt[j][:, DV:DV + 1])
                        p.stt(Cst[:, dc, 0:DV], Cst[:, dc, 0:DV], sc(4), ps_u[dc % 2][:, 0:DV], ALU.mult, ALU.add)
                        p.stt(Cst[:, dc, DV:DV + 1], Cst[:, dc, DV:DV + 1], sc(4), ps_n[:, dc:dc + 1], ALU.mult, ALU.add)
            for c in range(NLC):
                i = c + NTC
                o_ = ot[c % 2]; t_ = tT[c % 2]; a1 = n1[c % 2]
                p.dma(o_[:], kvo[i * 128:(i + 1) * 128, QK + D + hd * DV:QK + D + (hd + 1) * DV])
                p.act(o_[:], o_[:], AF.Sigmoid)
                self.ln_core(p, hbuf[:, c, :], a1[:], st, mv, rstd, nmr, "")
                p.tt(a1[:], a1[:], ng[:], ALU.mult, eng="gpsimd")
                p.tt(a1[:], a1[:], o_[:], ALU.mult)
                for b_ in range(DV // 128):
                    p.tr(ps_t[:, b_ * 128:(b_ + 1) * 128], a1[:, b_ * 128:(b_ + 1) * 128], ident[:])
                p.copy(t_[:], ps_t[:, 0:DV].rearrange("p (a t) -> p a t", t=128), eng="scalar")
                p.dma(moT[hd * DV:(hd + 1) * DV, i * 128:(i + 1) * 128].rearrange("(a p) t -> p a t", p=128), t_[:],
                      eng="gpsimd", writes=[("moT", hd, c)])
        p.emit()

    def build(self):
        D, TT, T, TC, KC, E, MH = self.D, self.TT, self.T, self.TC, self.KC, self.E, self.MH
        QK, DL, DP, PG, LH = self.QK, self.DL, self.DP, self.PG, self.LH
        nc = self.nc
        self.din("xin", [TT, D]); self.din("condT", [128, KC, 2])
        self.din("ada_w", [2, D, 6 * D]); self.din("ada_b", [2, 6 * D])
        self.din("ln_g", [2, 2, D]); self.din("ln_b", [2, 2, D])
        self.din("even_w_in", [D, 2 * DL + DP]); self.din("lru_gate_w", [2, 2, LH, 128, 128])
        self.din("ppar", [128, 11 * LH + DP // 128]); self.din("pool_w", [4, PG, PG]); self.din("even_w_out", [D, D])
        self.din("odd_w_in", [D, 2 * QK + 2 * D + 4 * MH]); self.din("ogb", [MH, 4]); self.din("odd_norm_g", [1, D])
        self.din("odd_w_out", [D, D]); self.din("router_w", [2, D, E])
        self.din("expert_w_gu", [2, E, D, 2 * self.DE]); self.din("expert_w_down", [2, E, self.DE, D])
        self.din("ident", [128, 128]); self.din("iota", [128, 512]); self.din("pidx", [128, 4])
        self.din("invcnt", [4, TT]); self.din("mmask", [128, 2, 128])
        self.out = nc.dram_tensor("out", [T, D], F32, kind="ExternalOutput").ap()
        self.dscr("mods", [2, 2, 6 * D]); self.dscr("h", [TT, D]); self.dscr("uT", [D, TT])
        self.dscr("pT", [max(2 * DL + DP, 2 * QK), TT]); self.dscr("moT", [D, TT]); self.dscr("dT", [DP, TT])
        self.dscr("y", [TT, D]); self.dscr("xm", [TT, D]); self.dscr("aff", [TT, E]); self.dscr("affT", [E, TT])
        self.dscr("idx_l", [128, (self.CAP + 127) // 128, E]); self.dscr("idx_c", [128, (self.CAPC + 127) // 128, E]); self.dscr("m", [TT, D])
        self.dscr("gT", [4 * MH, TT]); self.dscr("kvo", [TT, QK + 2 * D]); self.dscr("mS", [2, 128, self.NT, 6, MH])
        self.scr["qkT"] = self.scr["pT"]
        self.sy = Sync(nc)
        alltiles = list(range(self.NT)); lattiles = list(range(self.NTC, self.NT))
        self.phase_adaln()
        self.phase_ln0()
        self.phase_u(0)
        self.gemm_fm(self.inp["even_w_in"], self.scr["uT"], self.scr["pT"], D, 2 * DL + DP, self.tcols())
        self.phase_lru()
        self.phase_pool()
        for g in range(4):
            self.gemm_fm(self.inp["pool_w"][g], self.scr["dT"][g * PG:(g + 1) * PG, :], self.scr["moT"][DL + g * PG:DL + (g + 1) * PG, :],
                         PG, PG, self.tcols(), row_scale=self.inp["ppar"][:, 11 * LH + g * (PG // 128):11 * LH + (g + 1) * (PG // 128)])
        self.gemm_tm(self.scr["moT"], self.inp["even_w_out"], self.scr["y"], D, D, alltiles)
        self.phase_post(0, 0, alltiles, True)
        self.phase_route(TC, T, self.CAP, "l")
        self.phase_route(0, TC, self.CAPC, "c")
        self.phase_moe(0, TC, T, self.CAP, "l")
        self.phase_moe(0, 0, TC, self.CAPC, "c")
        self.phase_post(0, 1, alltiles, False)
        self.phase_u(1)
        wi = self.inp["odd_w_in"]
        self.gemm_fm(wi[:, 0:QK], self.scr["uT"], self.scr["qkT"][0:QK, :], D, QK, self.tcols(lat_only=True), scale=float(self.DQK) ** -0.5)
        self.gemm_fm(wi[:, QK:2 * QK], self.scr["uT"], self.scr["qkT"][QK:2 * QK, :], D, QK, self.tcols(lat_only=True))
        self.gemm_fm(wi[:, 2 * QK + 2 * D:2 * QK + 2 * D + 4 * MH], self.scr["uT"], self.scr["gT"], D, 4 * MH, self.tcols())
        self.gemm_tm(self.scr["uT"], wi[:, QK:2 * QK + 2 * D], self.scr["kvo"], D, QK + 2 * D, alltiles)
        self.phase_mlstm_gates()
        self.phase_mlstm()
        self.gemm_tm(self.scr["moT"], self.inp["odd_w_out"], self.scr["y"], D, D, lattiles)
        self.phase_post(1, 0, lattiles, True)
        self.phase_route(TC, T, self.CAP, "l")
        self.phase_moe(1, TC, T, self.CAP, "l")
        self.phase_post(1, 1, lattiles, False)
        self.phase_out()
        self.sy.close()
        return nc


def host_inputs(cfg, b, x, c, ctx, c_ctx, ada_w, ada_b, ln_g, ln_b, even_w_in, even_conv_w, even_conv_b, lru_gate_w,
                lru_gate_b, lru_lambda, pool_w, pool_scale, even_w_out, odd_w_in, odd_gate_b, odd_norm_g, odd_w_out,
                router_w, expert_w_gu, expert_w_down):
    D, T, TC, GW, MH = cfg["D"], cfg["T"], cfg["TC"], cfg["GW"], cfg["MH"]
    KC = D // 128
    LH = D // 256
    f = lambda a: np.ascontiguousarray(np.asarray(a, dtype=np.float32))
    m = {}
    m["xin"] = f(np.concatenate([ctx[b], x[b]], axis=0))
    m["condT"] = f(np.stack([c[b], c_ctx], axis=-1).reshape(KC, 128, 2).transpose(1, 0, 2))
    m["ada_w"] = f(ada_w); m["ada_b"] = f(ada_b); m["ln_g"] = f(ln_g); m["ln_b"] = f(ln_b)
    m["even_w_in"] = f(even_w_in[0]); m["lru_gate_w"] = f(lru_gate_w[0])
    cw = np.asarray(even_conv_w[0]).reshape(4, LH, 128).transpose(2, 1, 0).reshape(128, LH * 4)
    cb = np.asarray(even_conv_b[0]).reshape(LH, 128).T
    gb = np.asarray(lru_gate_b[0]).reshape(2, 2, LH, 128).transpose(3, 2, 0, 1).reshape(128, LH * 4)
    lam = np.asarray(lru_lambda[0]).reshape(2, LH, 128).transpose(2, 1, 0).reshape(128, LH * 2)
    psc = np.asarray(pool_scale[0]).reshape(-1, 128).T
    m["ppar"] = f(np.concatenate([cw, cb, gb, lam, psc], axis=1))
    m["pool_w"] = f(pool_w[0]); m["even_w_out"] = f(even_w_out[0])
    m["odd_w_in"] = f(odd_w_in[0])
    m["ogb"] = f(np.asarray(odd_gate_b[0]).reshape(2, 2, MH).transpose(2, 0, 1).reshape(MH, 4))
    m["odd_norm_g"] = f(odd_norm_g); m["odd_w_out"] = f(odd_w_out[0]); m["router_w"] = f(router_w)
    m["expert_w_gu"] = f(expert_w_gu); m["expert_w_down"] = f(expert_w_down)
    m["ident"] = np.eye(128, dtype=np.float32)
    m["iota"] = f(np.tile(np.arange(512, dtype=np.float32)[None, :], (128, 1)))
    m["pidx"] = f(np.arange(128, dtype=np.float32)[:, None] + 128.0 * np.arange(4, dtype=np.float32)[None, :])
    inv = np.zeros((4, TC + T), np.float32)
    GH = T // GW

    def cnt(L, w):
        pos = np.arange(L)
        lo = np.clip(pos - w // 2, 0, L); hi = np.clip(pos + w - w // 2, 0, L)
        return (hi - lo).astype(np.float64)
    for g, w in enumerate(POOL_WINDOWS):
        inv[g, :TC] = 1.0 / cnt(TC, w)
        inv[g, TC:] = (1.0 / (cnt(GH, w)[:, None] * cnt(GW, w)[None, :])).reshape(-1)
    m["invcnt"] = inv
    s_, t_ = np.meshgrid(np.arange(128), np.arange(128), indexing="ij")
    m["mmask"] = f(np.stack([(s_ <= t_), (s_ >= t_)], axis=1).astype(np.float32))
    return m


_NC_CACHE = {}


def run(cfg, inputs, n_samples=2):
    key = tuple(sorted((k, v) for k, v in cfg.items() if k != 'debug')) + (tuple(cfg.get('debug', ())),)
    if key not in _NC_CACHE:
        _NC_CACHE[key] = Builder(cfg).build()
    nc = _NC_CACHE[key]
    in_maps = [host_inputs(cfg, b, **inputs) for b in range(n_samples)]
    res = run_bass_kernel_spmd(nc, in_maps, core_ids=list(range(n_samples)))
    return np.stack([np.asarray(r["out"]) for r in res.results], axis=0).astype(np.float32)


def kernel(**inputs):
    inputs = {k: np.asarray(v) for k, v in inputs.items()}
    return run(CFG_FULL, inputs, n_samples=2)
```

```python
import numpy as np
import concourse.bass as bass
import concourse.mybir as mybir
from concourse.bass_utils import run_bass_kernel_spmd

F32 = mybir.dt.float32
BF16 = mybir.dt.bfloat16
AF = mybir.ActivationFunctionType
ALU = mybir.AluOpType
ENGS = ("tensor", "vector", "scalar", "gpsimd", "sync")
N_DMA_SEMS = 40


def _key(x):
    if isinstance(x, (str, tuple)):
        return x
    t = getattr(x, "tensor", x)
    return t.name


class Sync:
    def __init__(self, nc):
        self.guards = []
        self.esem = {}
        for e in ENGS:
            g = nc.semaphore("c_" + e)
            self.esem[e] = g.__enter__()
            self.guards.append(g)
        self.dsem = []
        for j in range(N_DMA_SEMS):
            g = nc.semaphore("d_%d" % j)
            self.dsem.append(g.__enter__())
            self.guards.append(g)
        self.ecount = {e: 0 for e in ENGS}
        self.ndma = 0

    def close(self):
        for g in reversed(self.guards):
            g.__exit__(None, None, None)


class Prog:
    def __init__(self, nc, sy):
        self.nc = nc
        self.sy = sy
        self.ops = []
        self.stack = []
        sy.phase = getattr(sy, "phase", 0) + 1
        self.pfx = "p%d_" % sy.phase

    def sb(self, name, shape, dt=F32):
        g = self.nc.sbuf_tensor(self.pfx + name, list(shape), dt)
        t = g.__enter__()
        self.stack.append(g)
        return t

    def ps(self, name, shape, dt=F32):
        g = self.nc.psum_tensor(self.pfx + name, list(shape), dt)
        t = g.__enter__()
        self.stack.append(g)
        return t

    def op(self, eng, fn, reads=(), writes=(), dma=False):
        self.ops.append(dict(eng=eng, fn=fn, r=[_key(k) for k in reads], w=[_key(k) for k in writes], dma=dma))

    def dma(self, out, in_, eng="sync", reads=None, writes=None, **kw):
        self.op(eng, lambda e: e.dma_start(out=out, in_=in_, **kw),
                reads=[in_] if reads is None else reads, writes=[out] if writes is None else writes, dma=True)

    def mm(self, out, lhsT, rhs, start=True, stop=True, reads=None, writes=None):
        self.op("tensor", lambda e: e.matmul(out, lhsT, rhs, start=start, stop=stop),
                reads=[lhsT, rhs] if reads is None else reads, writes=[out] if writes is None else writes)

    def tr(self, out, in_, ident):
        self.op("tensor", lambda e: e.transpose(out, in_, ident), reads=[in_, ident], writes=[out])

    def act(self, out, in_, func, bias=None, scale=1.0, eng="scalar"):
        r = [in_] + [x for x in (bias, scale) if not isinstance(x, (int, float, type(None)))]
        kw = {}
        if bias is not None:
            kw["bias"] = bias
        self.op(eng, lambda e: e.activation(out=out, in_=in_, func=func, scale=scale, **kw), reads=r, writes=[out])

    def tt(self, out, in0, in1, op, eng="vector"):
        self.op(eng, lambda e: e.tensor_tensor(out=out, in0=in0, in1=in1, op=op), reads=[in0, in1], writes=[out])

    def ts(self, out, in0, s1, s2, op0, op1=ALU.bypass, eng="vector"):
        r = [in0] + [x for x in (s1, s2) if not isinstance(x, (int, float, type(None)))]
        self.op(eng, lambda e: e.tensor_scalar(out=out, in0=in0, scalar1=s1, scalar2=s2, op0=op0, op1=op1),
                reads=r, writes=[out])

    def stt(self, out, in0, scalar, in1, op0, op1):
        r = [in0, in1] + ([scalar] if not isinstance(scalar, (int, float)) else [])
        self.op("vector", lambda e: e.scalar_tensor_tensor(out=out, in0=in0, scalar=scalar, in1=in1, op0=op0, op1=op1),
                reads=r, writes=[out])

    def copy(self, out, in_, eng="vector"):
        if eng == "scalar":
            self.op(eng, lambda e: e.copy(out=out, in_=in_), reads=[in_], writes=[out])
        else:
            self.op(eng, lambda e: e.tensor_copy(out=out, in_=in_), reads=[in_], writes=[out])

    def memset(self, ap, val, eng="vector"):
        self.op(eng, lambda e: e.memset(ap, val), reads=[], writes=[ap])

    def emit(self):
        nc, sy, ops = self.nc, self.sy, self.ops
        wr, rd = {}, {}
        nsem = N_DMA_SEMS
        for i, o in enumerate(ops):
            deps = set()
            for k in o["r"]:
                deps.update(wr.get(k, ()))
            for k in o["w"]:
                deps.update(wr.get(k, ()))
                deps.update(rd.get(k, ()))
            deps.discard(i)
            o["deps"] = deps
            for k in o["w"]:
                if rd.get(k):
                    wr[k] = [i]
                    rd[k] = []
                else:
                    wr.setdefault(k, []).append(i)
            for k in o["r"]:
                rd.setdefault(k, []).append(i)
            if o["dma"]:
                o["dma_idx"] = sy.ndma
                sy.ndma += 1
            else:
                sy.ecount[o["eng"]] += 1
                o["eidx"] = sy.ecount[o["eng"]]
        final_e = dict(sy.ecount)
        final_nd = sy.ndma
        esem, dsem = sy.esem, sy.dsem

        def gen(engname):
            def body(eng):
                known = {}

                def need(key, sem, val):
                    if val <= 0 or known.get(key, 0) >= val:
                        return
                    eng.wait_ge(sem, val)
                    known[key] = val

                for o in ops:
                    if o["eng"] != engname:
                        continue
                    req = {}
                    for d in o["deps"]:
                        p = ops[d]
                        if p["dma"]:
                            j = p["dma_idx"]
                            k_, s_, v_ = ("d", j % nsem), dsem[j % nsem], 16 * (j // nsem + 1)
                        else:
                            k_, s_, v_ = ("c", p["eng"]), esem[p["eng"]], p["eidx"]
                        if k_ not in req or req[k_][1] < v_:
                            req[k_] = (s_, v_)
                    for k_, (s_, v_) in req.items():
                        need(k_, s_, v_)
                    if o["dma"]:
                        j = o["dma_idx"]
                        if j >= nsem:
                            need(("d", j % nsem), dsem[j % nsem], 16 * (j // nsem))
                        o["fn"](eng).then_inc(dsem[j % nsem], 16)
                    else:
                        o["fn"](eng).then_inc(esem[engname], 1)
                for e2 in ENGS:
                    need(("c", e2), esem[e2], final_e[e2])
                for j in range(max(0, final_nd - nsem), final_nd):
                    need(("d", j % nsem), dsem[j % nsem], 16 * (j // nsem + 1))
            return body

        with nc.Block() as block:
            block.tensor(gen("tensor"))
            block.vector(gen("vector"))
            block.scalar(gen("scalar"))
            block.gpsimd(gen("gpsimd"))
            block.sync(gen("sync"))
        for g in reversed(self.stack):
            g.__exit__(None, None, None)
        self.stack = []


CFG_FULL = dict(D=4096, T=4096, TC=256, GW=64, MH=8, E=16)
POOL_WINDOWS = (2, 4, 8, 16)
LN_EPS = 1e-5
DEPTH = 2
DN_ALPHA = (2 * DEPTH) ** 0.25


class Builder:
    def __init__(self, cfg):
        self.cfg = cfg
        D, T, TC = cfg["D"], cfg["T"], cfg["TC"]
        self.D, self.T, self.TC, self.GW, self.MH, self.E = D, T, TC, cfg["GW"], cfg["MH"], cfg["E"]
        self.TT = T + TC
        self.KC = D // 128
        self.NT = self.TT // 128
        self.NTC = TC // 128
        self.DL = D // 2
        self.LH = self.DL // 128
        self.DP = D // 2
        self.PG = self.DP // 4
        self.DV = D // self.MH
        self.DQK = self.DV // 2
        self.QK = self.MH * self.DQK
        self.DE = D // 4
        self.CAP = 2 * T // self.E
        self.CAPC = 2 * TC // self.E
        self.nc = nc = bass.Bass("TRN2", target_bir_lowering=False)
        self.sy = None
        self.inp = {}
        self.scr = {}

    def din(self, name, shape):
        self.inp[name] = self.nc.dram_tensor(name, list(shape), F32, kind="ExternalInput").ap()
        return self.inp[name]

    def dscr(self, name, shape, dt=F32):
        kind = "ExternalOutput" if name in self.cfg.get("debug", ()) else "Internal"
        self.scr[name] = self.nc.dram_tensor(name, list(shape), dt, kind=kind).ap()
        return self.scr[name]

    def prog(self):
        return Prog(self.nc, self.sy)

    def bank_list(self, p, n=8):
        return [p.ps("bank%d" % i, [128, 512]) for i in range(n)]

    def ln_core(self, p, src, dst, st, mv, rstd, nmr, sfx, eng_apply="scalar"):
        D = src.shape[1]
        nch = D // 512 if D >= 512 else 1
        w = D // nch
        for c in range(nch):
            p.op("vector", lambda e, c=c: e.bn_stats(out=st[:, c, :], in_=src[:, c * w:(c + 1) * w]), reads=[src], writes=[st])
        p.op("vector", lambda e: e.bn_aggr(out=mv[:], in_=st[:, 0:nch, :].rearrange("p a b -> p (a b)")), reads=[st], writes=[mv])
        p.ts(rstd[:], mv[:, 1:2], LN_EPS, None, ALU.add)
        p.act(rstd[:], rstd[:], AF.Ln)
        p.act(rstd[:], rstd[:], AF.Exp, scale=-0.5)
        p.stt(nmr[:], mv[:, 0:1], -1.0, rstd[:], ALU.mult, ALU.mult)
        p.act(dst, src, AF.Identity, bias=nmr[:], scale=rstd[:])

    def transpose_tile(self, p, src, dst, ident, banks, nblk, bi0=0):
        g = 0
        b = 0
        while b < nblk:
            n = min(4, nblk - b)
            bank = banks[(bi0 + g) % len(banks)]
            for j in range(n):
                p.tr(bank[:, j * 128:(j + 1) * 128], src[:, (b + j) * 128:(b + j + 1) * 128], ident[:])
            o = dst[:, b:b + n, :]
            i_ = bank[:, 0:n * 128].rearrange("p (a t) -> p a t", t=128)
            if g % 2 == 0:
                p.copy(o, i_, eng="vector")
            else:
                p.copy(o, i_, eng="scalar")
            b += n
            g += 1

    def load_bcast(self, p, dst, row_ap):
        p.dma(dst, row_ap.partition_broadcast(128))

    def gemm_fm(self, W, inT, outT, K, N, tcols, scale=None, row_scale=None, TS=512, odt=F32):
        p = self.prog()
        kc = K // 128
        banks = self.bank_list(p)
        xin = [p.sb("gx%d" % i, [128, kc, TS], BF16) for i in range(2)]
        wst = [p.sb("gws%d" % i, [128, kc, 128]) for i in range(2)]
        wch = [p.sb("gw%d" % i, [128, kc, 128], BF16) for i in range(2)]
        ob = [p.sb("go%d" % i, [128, TS], odt) for i in range(3)]
        rs = None
        if row_scale is not None:
            rs = p.sb("grs", [128, (N + 127) // 128])
            p.dma(rs[:], row_scale, allow_slow_non_contiguous=True)
        NB = (N + 127) // 128
        items = [(ti, nb) for ti in range(len(tcols)) for nb in range(NB)]

        def load_x(ti):
            t0, tsz = tcols[ti]
            p.dma(xin[ti % 2][:, :, 0:tsz], inT[:, t0:t0 + tsz].rearrange("(c p) t -> p c t", p=128))

        def load_w(k):
            nb = items[k][1]
            m = min(128, N - nb * 128)
            p.dma(wst[k % 2][:, :, 0:m], W[:, nb * 128:nb * 128 + m].rearrange("(c p) m -> p c m", p=128))
            p.copy(wch[k % 2][:, :, 0:m], wst[k % 2][:, :, 0:m], eng="gpsimd")
        load_x(0)
        load_w(0)
        for k, (ti, nb) in enumerate(items):
            t0, tsz = tcols[ti]
            m = min(128, N - nb * 128)
            if k + 1 < len(items):
                load_w(k + 1)
            if nb == 0 and ti + 1 < len(tcols):
                load_x(ti + 1)
            x = xin[ti % 2]; w = wch[k % 2]
            bank = banks[k % 8]
            for c in range(kc):
                p.mm(bank[0:m, 0:tsz], w[:, c, 0:m], x[:, c, 0:tsz], start=(c == 0), stop=(c == kc - 1))
            o = ob[k % 3]
            if rs is not None:
                p.ts(o[0:m, 0:tsz], bank[0:m, 0:tsz], rs[0:m, nb:nb + 1], None, ALU.mult)
            elif scale is not None:
                p.act(o[0:m, 0:tsz], bank[0:m, 0:tsz], AF.Copy, scale=scale)
            elif k % 2 == 0:
                p.copy(o[0:m, 0:tsz], bank[0:m, 0:tsz], eng="vector")
            else:
                p.copy(o[0:m, 0:tsz], bank[0:m, 0:tsz], eng="scalar")
            p.dma(outT[nb * 128:nb * 128 + m, t0:t0 + tsz], o[0:m, 0:tsz], eng="sync", writes=[(_key(outT), ti, nb)])
        p.emit()

    def gemm_tm(self, inT, W, out, K, N, tiles):
        p = self.prog()
        kc = K // 128
        banks = self.bank_list(p)
        wb = p.sb("hw", [128, kc, 512], BF16)
        wst = [p.sb("hws%d" % i, [128, 8, 512]) for i in range(2)]
        xin = [p.sb("hx%d" % i, [128, kc, 128], BF16) for i in range(2)]
        ob = [p.sb("ho%d" % i, [128, 512]) for i in range(3)]
        it = 0
        g_ = 0
        for nb in range((N + 511) // 512):
            n = min(512, N - nb * 512)
            for c0 in range(0, kc, 8):
                c1 = min(kc, c0 + 8)
                ws = wst[g_ % 2]
                p.dma(ws[:, 0:c1 - c0, 0:n], W[c0 * 128:c1 * 128, nb * 512:nb * 512 + n].rearrange("(c p) n -> p c n", p=128))
                p.copy(wb[:, c0:c1, 0:n], ws[:, 0:c1 - c0, 0:n], eng="gpsimd" if g_ % 2 == 0 else "vector")
                g_ += 1
            for q_, i in enumerate(tiles):
                if q_ == 0:
                    p.dma(xin[it % 2][:], inT[:, i * 128:(i + 1) * 128].rearrange("(c p) t -> p c t", p=128))
                if q_ + 1 < len(tiles):
                    i2 = tiles[q_ + 1]
                    p.dma(xin[(it + 1) % 2][:], inT[:, i2 * 128:(i2 + 1) * 128].rearrange("(c p) t -> p c t", p=128))
                x = xin[it % 2]
                bank = banks[it % 8]
                for c in range(kc):
                    p.mm(bank[:, 0:n], x[:, c, :], wb[:, c, 0:n], start=(c == 0), stop=(c == kc - 1))
                o = ob[it % 3]
                p.copy(o[:, 0:n], bank[:, 0:n], eng="vector" if it % 2 == 0 else "scalar")
                p.dma(out[i * 128:(i + 1) * 128, nb * 512:nb * 512 + n], o[:, 0:n], eng="sync",
                      writes=[(_key(out), i, nb)])
                it += 1
        p.emit()

    def phase_adaln(self):
        D, KC = self.D, self.KC
        p = self.prog()
        banks = self.bank_list(p)
        cT = p.sb("cT", [128, KC, 2])
        sil = p.sb("sil", [128, KC, 2])
        p.dma(cT[:], self.inp["condT"])
        p.act(sil[:], cT[:], AF.Silu)
        NB = 2048
        wbuf = [p.sb("aw%d" % i, [128, NB]) for i in range(3)]
        bb = [p.sb("ab%d" % i, [2, NB]) for i in range(2)]
        obuf = [p.sb("ao%d" % i, [2, NB]) for i in range(2)]
        it = 0
        blk = 0
        for l in range(2):
            for nb in range(6 * D // NB):
                bset = banks[0:4] if blk % 2 == 0 else banks[4:8]
                b_ = bb[blk % 2]
                p.dma(b_[:], self.inp["ada_b"][l:l + 1, nb * NB:(nb + 1) * NB].partition_broadcast(2))
                for c in range(KC):
                    w = wbuf[it % 3]
                    it += 1
                    p.dma(w[:], self.inp["ada_w"][l, c * 128:(c + 1) * 128, nb * NB:(nb + 1) * NB])
                    for j in range(4):
                        p.mm(bset[j][0:2, :], sil[:, c, :], w[:, j * 512:(j + 1) * 512], start=(c == 0), stop=(c == KC - 1))
                o = obuf[blk % 2]
                for j in range(4):
                    p.tt(o[:, j * 512:(j + 1) * 512], bset[j][0:2, :], b_[:, j * 512:(j + 1) * 512], ALU.add)
                p.dma(self.scr["mods"][l, :, nb * NB:(nb + 1) * NB], o[:], eng="gpsimd", writes=[("mods", l, nb)])
                blk += 1
        p.emit()

    def mod_row(self, l, cnd, seg):
        D = self.D
        return self.scr["mods"][l, cnd:cnd + 1, seg * D:(seg + 1) * D]

    def phase_ln0(self):
        D = self.D
        p = self.prog()
        xb = [p.sb("xb%d" % i, [128, D]) for i in range(2)]
        hb = [p.sb("hb%d" % i, [128, D]) for i in range(2)]
        st = p.sb("st", [128, 8, 6]); mv = p.sb("mv", [128, 2]); rstd = p.sb("rstd", [128, 1]); nmr = p.sb("nmr", [128, 1])
        for i in range(self.NT):
            x = xb[i % 2]; h = hb[i % 2]
            p.dma(x[:], self.inp["xin"][i * 128:(i + 1) * 128, :])
            self.ln_core(p, x[:], h[:], st, mv, rstd, nmr, "")
            p.dma(self.scr["h"][i * 128:(i + 1) * 128, :], h[:], eng="gpsimd", writes=[("h", i)])
        p.emit()

    def phase_u(self, l):
        D, KC = self.D, self.KC
        p = self.prog()
        banks = self.bank_list(p)
        ident = p.sb("ident", [128, 128]); p.dma(ident[:], self.inp["ident"])
        rows = {}
        for cnd in range(2):
            sc = p.sb("sc%d" % cnd, [128, D]); sh = p.sb("sh%d" % cnd, [128, D])
            self.load_bcast(p, sc[:], self.mod_row(l, cnd, 1))
            self.load_bcast(p, sh[:], self.mod_row(l, cnd, 0))
            p.ts(sc[:], sc[:], 1.0, None, ALU.add)
            rows[cnd] = (sc, sh)
        hb = [p.sb("uh%d" % i, [128, D]) for i in range(2)]
        ub = [p.sb("uu%d" % i, [128, D]) for i in range(2)]
        ut = [p.sb("ut%d" % i, [128, KC, 128], BF16) for i in range(2)]
        for i in range(self.NT):
            cnd = 1 if i < self.NTC else 0
            sc, sh = rows[cnd]
            h = hb[i % 2]; u = ub[i % 2]; t = ut[i % 2]
            p.dma(h[:], self.scr["h"][i * 128:(i + 1) * 128, :])
            p.tt(u[:], h[:], sc[:], ALU.mult, eng="gpsimd")
            p.tt(u[:], u[:], sh[:], ALU.add, eng="vector")
            self.transpose_tile(p, u, t, ident, banks, KC, bi0=i * 3)
            p.dma(self.scr["uT"][:, i * 128:(i + 1) * 128].rearrange("(c p) t -> p c t", p=128), t[:], eng="gpsimd",
                  writes=[("uT", i)])
        p.emit()

    def tcols(self, TS=512, lat_only=False):
        out = []
        t = self.TC if lat_only else 0
        while t < self.TT:
            s = min(TS, self.TT - t)
            out.append((t, s))
            t += s
        return out

    def phase_lru(self):
        TT, TC, T, LH, DL = self.TT, self.TC, self.T, self.LH, self.DL
        pT, moT, pp = self.scr["pT"], self.scr["moT"], self.inp["ppar"]
        p = self.prog()
        banks = self.bank_list(p)
        par = p.sb("par", [128, pp.shape[1]]); p.dma(par[:], pp)
        o_cw, o_cb, o_gb, o_lam = 0, 4 * LH, 5 * LH, 9 * LH
        c8 = p.sb("c8", [128, 2 * LH])
        p.act(c8[:], par[:, o_lam:o_lam + 2 * LH], AF.Exp, scale=-1.0)
        p.ts(c8[:], c8[:], 1.0, None, ALU.add)
        p.act(c8[:], c8[:], AF.Ln)
        p.ts(c8[:], c8[:], -8.0, None, ALU.mult)
        xs = [p.sb("xs%d" % i, [128, TT]) for i in range(1)]
        zs = [p.sb("zs%d" % i, [128, TT]) for i in range(1)]
        y = p.sb("y", [128, TT]); yo = p.sb("yo", [128, TT], BF16)
        gw = [p.sb("gw%d" % i, [128, 4, 128]) for i in range(2)]
        aa = [p.sb("aa%d" % d, [128, TT]) for d in range(2)]
        bx = [p.sb("bx%d" % d, [128, TT]) for d in range(2)]
        hh = [p.sb("hh%d" % d, [128, TT]) for d in range(2)]
        t1 = [p.sb("t1_%d" % i, [128, 512]) for i in range(2)]
        t2 = [p.sb("t2_%d" % i, [128, 512]) for i in range(2)]
        tcs = self.tcols(512)
        it = 0
        for hd in range(LH):
            x = xs[0]; z = zs[0]; g = gw[hd % 2]
            p.dma(x[:], pT[hd * 128:(hd + 1) * 128, :])
            p.dma(z[:], pT[DL + hd * 128:DL + (hd + 1) * 128, :])
            p.dma(g[:], self.inp["lru_gate_w"][:, :, hd, :, :].rearrange("d g i j -> i (d g) j"))
            cw = lambda k: par[:, o_cw + hd * 4 + k:o_cw + hd * 4 + k + 1]
            cb = par[:, o_cb + hd:o_cb + hd + 1]
            for (s0, L) in ((0, TC), (TC, T)):
                p.ts(y[:, s0:s0 + L], x[:, s0:s0 + L], cw(1), cb, ALU.mult, ALU.add)
                p.stt(y[:, s0 + 1:s0 + L], x[:, s0:s0 + L - 1], cw(0), y[:, s0 + 1:s0 + L], ALU.mult, ALU.add)
                p.stt(y[:, s0:s0 + L - 1], x[:, s0 + 1:s0 + L], cw(2), y[:, s0:s0 + L - 1], ALU.mult, ALU.add)
                p.stt(y[:, s0:s0 + L - 2], x[:, s0 + 2:s0 + L], cw(3), y[:, s0:s0 + L - 2], ALU.mult, ALU.add)
            for d in range(2):
                gbr = par[:, o_gb + hd * 4 + d * 2:o_gb + hd * 4 + d * 2 + 1]
                gbi = par[:, o_gb + hd * 4 + d * 2 + 1:o_gb + hd * 4 + d * 2 + 2]
                c8d = c8[:, hd * 2 + d:hd * 2 + d + 1]
                for (t0, ts_) in tcs:
                    br = banks[it % 8]; bi = banks[(it + 1) % 8]
                    a1 = t1[(it // 2) % 2]; a2 = t2[(it // 2) % 2]
                    it += 2
                    p.mm(br[:, 0:ts_], g[:, d * 2, :], y[:, t0:t0 + ts_])
                    p.mm(bi[:, 0:ts_], g[:, d * 2 + 1, :], y[:, t0:t0 + ts_])
                    av = aa[d][:, t0:t0 + ts_]
                    p.act(a1[:, 0:ts_], br[:, 0:ts_], AF.Sigmoid, bias=gbr)
                    p.act(a2[:, 0:ts_], bi[:, 0:ts_], AF.Sigmoid, bias=gbi)
                    p.act(av, a1[:, 0:ts_], AF.Exp, scale=c8d)
                    p.tt(a1[:, 0:ts_], av, av, ALU.mult)
                    p.ts(a1[:, 0:ts_], a1[:, 0:ts_], -1.0, 1.0, ALU.mult, ALU.add)
                    p.act(a1[:, 0:ts_], a1[:, 0:ts_], AF.Sqrt)
                    p.tt(a2[:, 0:ts_], a2[:, 0:ts_], y[:, t0:t0 + ts_], ALU.mult, eng="gpsimd")
                    p.tt(bx[d][:, t0:t0 + ts_], a1[:, 0:ts_], a2[:, 0:ts_], ALU.mult)
            p.op("vector", lambda e: e.tensor_tensor_scan(out=hh[0][:], data0=aa[0][:], data1=bx[0][:], initial=0.0,
                                                          op0=ALU.mult, op1=ALU.add), reads=[aa[0], bx[0]], writes=[hh[0]])
            p.op("vector", lambda e: e.tensor_tensor_scan(out=hh[1][:, 0:TC][:, ::-1], data0=aa[1][:, 0:TC][:, ::-1],
                                                          data1=bx[1][:, 0:TC][:, ::-1], initial=0.0,
                                                          op0=ALU.mult, op1=ALU.add), reads=[aa[1], bx[1]], writes=[hh[1]])
            p.op("vector", lambda e: e.tensor_tensor_scan(out=hh[1][:, TC:TT][:, ::-1], data0=aa[1][:, TC:TT][:, ::-1],
                                                          data1=bx[1][:, TC:TT][:, ::-1], initial=hh[1][:, 0:1],
                                                          op0=ALU.mult, op1=ALU.add), reads=[aa[1], bx[1], hh[1]], writes=[hh[1]])
            p.tt(hh[0][:], hh[0][:], hh[1][:], ALU.add, eng="gpsimd")
            p.tt(y[:], z[:], z[:], ALU.mult)
            p.ts(y[:], y[:], 0.044715, 1.0, ALU.mult, ALU.add)
            p.tt(y[:], y[:], z[:], ALU.mult)
            p.act(y[:], y[:], AF.Sigmoid, scale=1.5957691216057308)
            p.tt(y[:], y[:], z[:], ALU.mult, eng="gpsimd")
            p.tt(yo[:], y[:], hh[0][:], ALU.mult)
            p.dma(moT[hd * 128:(hd + 1) * 128, :], yo[:], eng="gpsimd", writes=[("moT", hd)])
        p.emit()

    def phase_pool(self):
        TT, TC, T, GW, PG, DL = self.TT, self.TC, self.T, self.GW, self.PG, self.DL
        GH = T // GW
        pT = self.scr["pT"]
        p = self.prog()
        xs = [p.sb("px%d" % i, [128, TT]) for i in range(2)]
        inv = p.sb("inv", [128, TT])
        P1 = p.sb("P1", [128, GH, GW + 16]); P2 = p.sb("P2", [128, GH + 16, GW])
        Sa = p.sb("Sa", [128, GH * (GW + 16)]); Sb = p.sb("Sb", [128, GH * (GW + 16)])
        P3 = p.sb("P3", [128, TC + 16]); S3a = p.sb("S3a", [128, TC + 16]); S3b = p.sb("S3b", [128, TC + 16])
        dd = [p.sb("dd%d" % i, [128, TT]) for i in range(2)]
        db = p.sb("db", [128, TT], BF16)
        p.memset(P1[:], 0.0); p.memset(P2[:], 0.0, eng="gpsimd"); p.memset(P3[:], 0.0)

        def stages(src, tmpa, tmpb, dst, nst, ax, L):
            def sl(t, lo, hi):
                if len(t.shape) == 2:
                    return t[:, 8 + lo:8 + hi]
                if ax == 1:
                    return t[:, 8 + lo:8 + hi, :]
                return t[:, :, 8 + lo:8 + hi]
            cur = src
            shifts = [(-1, 0), (-1, 1), (-2, 2), (-4, 4)]
            rng = [(-7, L + 8), (-6, L + 7), (-4, L + 5), (0, L)]
            bufs = [tmpa, tmpb, tmpa, tmpb]
            for s_ in range(nst):
                lo, hi = rng[s_]
                if s_ == nst - 1:
                    lo, hi = 0, L
                    o = dst
                else:
                    o = sl(bufs[s_], lo, hi)
                a_, b_ = shifts[s_]
                p.tt(o, sl(cur, lo + a_, hi + a_), sl(cur, lo + b_, hi + b_), ALU.add)
                cur = bufs[s_]

        for g, w in enumerate(POOL_WINDOWS):
            nst = {2: 1, 4: 2, 8: 3, 16: 4}[w]
            self.load_bcast(p, inv[:], self.inp["invcnt"][g:g + 1, :])
            for c in range(PG // 128):
                idx = g * (PG // 128) + c
                x = xs[idx % 2]; d_ = dd[idx % 2]
                row = 2 * DL + g * PG + c * 128
                p.dma(x[:], pT[row:row + 128, :])
                p.copy(P1[:, :, 8:8 + GW], x[:, TC:TT].rearrange("p (a b) -> p a b", b=GW), eng="gpsimd")
                stages(P1[:], Sa[:].rearrange("p (a b) -> p a b", b=GW + 16), Sb[:].rearrange("p (a b) -> p a b", b=GW + 16),
                       P2[:, 8:8 + GH, :], nst, 2, GW)
                stages(P2[:], Sa[:].rearrange("p (a b) -> p a b", b=GW), Sb[:].rearrange("p (a b) -> p a b", b=GW),
                       d_[:, TC:TT].rearrange("p (a b) -> p a b", b=GW), nst, 1, GH)
                p.copy(P3[:, 8:8 + TC], x[:, 0:TC], eng="gpsimd")
                stages(P3[:], S3a[:], S3b[:], d_[:, 0:TC], nst, 1, TC)
                p.tt(d_[:], d_[:], inv[:], ALU.mult)
                p.tt(db[:], d_[:], x[:], ALU.subtract, eng="gpsimd")
                p.dma(self.scr["dT"][idx * 128:(idx + 1) * 128, :], db[:], eng="gpsimd", writes=[("dT", idx)])
        p.emit()

    def phase_post(self, l, which, tiles, with_router):
        D, KC, E = self.D, self.KC, self.E
        p = self.prog()
        banks = self.bank_list(p)
        ident = p.sb("ident", [128, 128]); p.dma(ident[:], self.inp["ident"])
        gseg = 2 if which == 0 else 5
        ysrc = self.scr["y"] if which == 0 else self.scr["m"]
        lg = p.sb("lg", [128, D]); lb = p.sb("lb", [128, D])
        self.load_bcast(p, lg[:], self.inp["ln_g"][l, which:which + 1, :])
        self.load_bcast(p, lb[:], self.inp["ln_b"][l, which:which + 1, :])
        gr = p.sb("gr", [128, D])
        hb = [p.sb("ph%d" % i, [128, D]) for i in range(2)]
        yb = [p.sb("py%d" % i, [128, D]) for i in range(2)]
        st = p.sb("st", [128, 8, 6]); mv = p.sb("mv", [128, 2]); rstd = p.sb("rstd", [128, 1]); nmr = p.sb("nmr", [128, 1])
        if with_router:
            s2 = p.sb("s2", [128, D]); h2 = p.sb("h2", [128, D])
            wr = p.sb("wr", [128, KC, E]); p.dma(wr[:], self.inp["router_w"][l].rearrange("(c p) e -> p c e", p=128))
            xt = p.sb("xt", [128, KC, 128])
            lgt = p.sb("lgt", [128, E]); mx = p.sb("mx", [128, 1]); sm = p.sb("sm", [128, 1])
            affT = p.sb("affT", [E, self.TT])
        n_ = 0
        for cnd in (1, 0):
            grp = [i for i in tiles if (1 if i < self.NTC else 0) == cnd]
            if not grp:
                continue
            self.load_bcast(p, gr[:], self.mod_row(l, cnd, gseg))
            if with_router:
                self.load_bcast(p, s2[:], self.mod_row(l, cnd, 4))
                p.ts(s2[:], s2[:], 1.0, None, ALU.add)
                self.load_bcast(p, h2[:], self.mod_row(l, cnd, 3))
            for i in grp:
                h = hb[n_ % 2]; y = yb[n_ % 2]
                p.dma(h[:], self.scr["h"][i * 128:(i + 1) * 128, :])
                p.dma(y[:], ysrc[i * 128:(i + 1) * 128, :])
                p.tt(y[:], y[:], gr[:], ALU.mult, eng="gpsimd")
                p.stt(y[:], h[:], DN_ALPHA, y[:], ALU.mult, ALU.add)
                self.ln_core(p, y[:], h[:], st, mv, rstd, nmr, "")
                p.tt(h[:], h[:], lg[:], ALU.mult, eng="gpsimd")
                p.tt(h[:], h[:], lb[:], ALU.add)
                p.dma(self.scr["h"][i * 128:(i + 1) * 128, :], h[:], eng="gpsimd")
                if with_router:
                    p.tt(y[:], h[:], s2[:], ALU.mult, eng="gpsimd")
                    p.tt(y[:], y[:], h2[:], ALU.add)
                    p.dma(self.scr["xm"][i * 128:(i + 1) * 128, :], y[:], eng="gpsimd", writes=[("xm", i)])
                    self.transpose_tile(p, y, xt, ident, banks[0:6], KC, bi0=n_)
                    lb_ = banks[6]
                    for c in range(KC):
                        p.mm(lb_[:, 0:E], xt[:, c, :], wr[:, c, :], start=(c == 0), stop=(c == KC - 1))
                    p.op("vector", lambda e, lb_=lb_: e.reduce_max(out=mx[:], in_=lb_[:, 0:E], axis=mybir.AxisListType.X),
                         reads=[lb_], writes=[mx])
                    p.ts(mx[:], mx[:], -1.0, None, ALU.mult)
                    p.act(lgt[:], lb_[:, 0:E], AF.Exp, bias=mx[:])
                    p.op("vector", lambda e: e.reduce_sum(out=sm[:], in_=lgt[:], axis=mybir.AxisListType.X), reads=[lgt], writes=[sm])
                    p.op("vector", lambda e: e.reciprocal(out=sm[:], in_=sm[:]), reads=[sm], writes=[sm])
                    p.ts(lgt[:], lgt[:], sm[:], None, ALU.mult)
                    p.dma(self.scr["aff"][i * 128:(i + 1) * 128, :], lgt[:], eng="gpsimd", writes=[("aff", i)])
                    tb = banks[7]
                    p.tr(tb[0:E, 0:128], lgt[:], ident[:])
                    p.copy(affT[:, i * 128:(i + 1) * 128], tb[0:E, 0:128])
                n_ += 1
        if with_router:
            p.dma(self.scr["affT"], affT[:], eng="gpsimd")
        p.emit()

    def phase_route(self, t0, n, cap, tag):
        E = self.E
        SC = (cap + 127) // 128
        p = self.prog()
        banks = self.bank_list(p)
        ident = p.sb("ident", [128, 128]); p.dma(ident[:], self.inp["ident"])
        a = p.sb("ra", [E, n]); w = [p.sb("rw%d" % i, [E, n]) for i in range(2)]
        m8 = p.sb("m8", [E, 8]); ones = p.sb("ones", [E, n]); cs = p.sb("cs", [E, n]); msk = p.sb("msk", [E, n])
        idxE = p.sb("idxE", [E, SC * 128])
        p.dma(a[:], self.scr["affT"][:, t0:t0 + n])
        p.memset(ones[:], 1.0, eng="gpsimd")
        cur = a
        for r in range(cap // 8):
            p.op("vector", lambda e, cur=cur: e.max(out=m8[:], in_=cur[:]), reads=[cur], writes=[m8])
            if r < cap // 8 - 1:
                nxt = w[r % 2]
                p.op("vector", lambda e, cur=cur, nxt=nxt: e.match_replace(out=nxt[:], in_to_replace=m8[:], in_values=cur[:],
                                                                          imm_value=-1.0), reads=[cur, m8], writes=[nxt])
                cur = nxt
        p.ts(msk[:], a[:], m8[:, 7:8], None, ALU.is_ge)
        p.op("vector", lambda e: e.tensor_tensor_scan(out=cs[:], data0=ones[:], data1=msk[:], initial=0.0,
                                                      op0=ALU.mult, op1=ALU.add), reads=[ones, msk], writes=[cs])
        junk = w[0]
        io = p.sb("io", [E, 512]); p.dma(io[:], self.inp["iota"][0:E, :])
        p.ts(io[:], io[:], -1.0, -0.5, ALU.mult, ALU.add)
        for s_ in range(SC * 128):
            p.op("scalar", lambda e, s_=s_: e.activation(out=junk[:], in_=cs[:], func=AF.Sign, bias=io[:, s_:s_ + 1], scale=1.0,
                                                         accum_out=idxE[:, s_:s_ + 1]),
                 reads=[cs, io], writes=[junk, (_key(idxE), s_)])
        p.op("vector", lambda e: e.tensor_scalar(out=idxE[:], in0=idxE[:], scalar1=-0.5, scalar2=float(n) / 2 + float(t0),
                                                 op0=ALU.mult, op1=ALU.add),
             reads=[idxE] + [(_key(idxE), s_) for s_ in range(SC * 128)], writes=[idxE])
        it_ = p.sb("it", [128, SC, E])
        for i in range(SC):
            b = banks[i % 8]
            p.tr(b[:, 0:E], idxE[:, i * 128:(i + 1) * 128], ident[0:E, 0:E])
            p.copy(it_[:, i, :], b[:, 0:E])
        p.dma(self.scr["idx_" + tag], it_[:], eng="gpsimd")
        p.emit()

    def phase_moe(self, l, t0, n, cap, tag):
        D, KC, E, DE, TT = self.D, self.KC, self.E, self.DE, self.TT
        nt = n // 128
        FC = DE // 128
        SC = (cap + 127) // 128
        CW = SC * 128
        xm, mT, aff = self.scr["xm"], self.scr["m"], self.scr["aff"]
        p = self.prog()
        banks = self.bank_list(p)
        ident = p.sb("ident", [128, 128]); p.dma(ident[:], self.inp["ident"])
        idf = p.sb("idf", [128, SC, E]); p.dma(idf[:], self.scr["idx_" + tag])
        idx = p.sb("idx", [128, SC, E], mybir.dt.int32)
        ixh = [p.sb("ixh%d" % h, [128, SC, E], mybir.dt.int32) for h in range(2)]
        idh = p.sb("idh", [128, SC, E])
        p.copy(idx[:], idf[:])
        for h in range(2):
            p.ts(idh[:], idf[:], 2.0, float(h), ALU.mult, ALU.add)
            p.copy(ixh[h][:], idh[:])
        HW = D // 2
        xm2 = xm.rearrange("t (h c) -> (t h) c", h=2)
        m2 = mT.rearrange("t (h c) -> (t h) c", h=2)
        xg = [p.sb("xg%d" % i, [128, D]) for i in range(1)]
        gs = [p.sb("gs%d" % i, [128, E]) for i in range(SC)]
        for t_ in xg + gs:
            p.memset(t_[:], 0.0, eng="gpsimd")
        for i in range(nt):
            p.dma(mT[t0 + i * 128:t0 + (i + 1) * 128, :], xg[0][:], eng="gpsimd", writes=["m"])
        xe = p.sb("xe", [128, KC, CW], BF16)
        aT = p.sb("aT", [128, FC, CW], BF16)
        wst = [p.sb("wst%d" % i, [128, KC, 128]) for i in range(2)]
        wgb = [p.sb("wgb%d" % i, [128, KC, 128], BF16) for i in range(4)]
        wds = p.sb("wds", [128, FC, 256])
        wd = p.sb("wd", [128, FC, 256], BF16)
        sg = [p.sb("sg%d" % i, [128, CW]) for i in range(2)]
        ye = [p.sb("ye%d" % i, [128, D]) for i in range(SC)]
        bound = t0 + n - 1
        breg = {}

        def bnd(g, k, v):
            if k not in breg:
                breg[k] = g.to_reg(v)
            return breg[k]
        it = 0
        for e in range(E):
            for s in range(SC):
                x_ = xg[0]
                ix = idx[:, s, e:e + 1]
                for h in range(2):
                    ixh_ = ixh[h][:, s, e:e + 1]
                    p.op("gpsimd", lambda g, x_=x_, ixh_=ixh_, h=h: g.indirect_dma_start(
                        out=x_[:, h * HW:(h + 1) * HW], out_offset=None, in_=xm2[:, :],
                        in_offset=bass.IndirectOffsetOnAxis(ap=ixh_, axis=0),
                        bounds_check=bnd(g, "b2", 2 * bound + 1), oob_is_err=False), reads=[xm, ixh[h]], writes=[x_], dma=True)
                p.op("gpsimd", lambda g, g_=gs[s], ix=ix: g.indirect_dma_start(
                    out=g_[:, :], out_offset=None, in_=aff[:, :], in_offset=bass.IndirectOffsetOnAxis(ap=ix, axis=0),
                    bounds_check=bnd(g, "b1", bound), oob_is_err=False), reads=[aff, idx], writes=[gs[s]], dma=True)
                for c0 in range(0, KC, 4):
                    nb_ = min(4, KC - c0)
                    bk = banks[it % 4]
                    it += 1
                    for j in range(nb_):
                        p.tr(bk[:, j * 128:(j + 1) * 128], x_[:, (c0 + j) * 128:(c0 + j + 1) * 128], ident[:])
                    p.copy(xe[:, c0:c0 + nb_, s * 128:(s + 1) * 128], bk[:, 0:nb_ * 128].rearrange("p (a t) -> p a t", t=128),
                           eng="vector" if it % 2 == 0 else "scalar")
            def load_gu(k):
                f, part = k // 2, k % 2
                col = part * DE + f * 128
                p.dma(wst[k % 2][:], self.inp["expert_w_gu"][l, e, :, col:col + 128].rearrange("(c p) m -> p c m", p=128))
                p.copy(wgb[k % 4][:], wst[k % 2][:], eng="gpsimd")
            load_gu(0)
            for k in range(2 * FC):
                if k + 1 < 2 * FC:
                    load_gu(k + 1)
                f, part = k // 2, k % 2
                bk_ = banks[4 + (f % 2) * 2 + part]
                for c in range(KC):
                    p.mm(bk_[:, 0:CW], wgb[k % 4][:, c, :], xe[:, c, :], start=(c == 0), stop=(c == KC - 1))
                if part == 1:
                    bg = banks[4 + (f % 2) * 2]; bu = banks[5 + (f % 2) * 2]
                    s_ = sg[f % 2]
                    p.act(s_[:], bg[:, 0:CW], AF.Silu)
                    p.tt(aT[:, f, :], s_[:], bu[:, 0:CW], ALU.mult)
            for nb in range(D // 256):
                p.dma(wds[:], self.inp["expert_w_down"][l, e, :, nb * 256:(nb + 1) * 256].rearrange("(c p) n -> p c n", p=128))
                p.copy(wd[:], wds[:], eng="gpsimd")
                for s in range(SC):
                    bk = banks[it % 4]
                    it += 1
                    for f in range(FC):
                        p.mm(bk[:, 0:256], aT[:, f, s * 128:(s + 1) * 128], wd[:, f, :], start=(f == 0), stop=(f == FC - 1))
                    p.ts(ye[s][:, nb * 256:(nb + 1) * 256], bk[:, 0:256], gs[s][:, e:e + 1], None, ALU.mult,
                         eng="vector")
            for s in range(SC):
                for h in range(2):
                    ixh_ = ixh[h][:, s, e:e + 1]
                    p.op("gpsimd", lambda g, y_=ye[s], ixh_=ixh_, h=h: g.indirect_dma_start(
                        out=m2[:, :], out_offset=bass.IndirectOffsetOnAxis(ap=ixh_, axis=0), in_=y_[:, h * HW:(h + 1) * HW],
                        in_offset=None, compute_op=ALU.add), reads=[ye[s], ixh[h], "m"], writes=["m"], dma=True)
        p.emit()

    def phase_out(self):
        p = self.prog()
        hb = [p.sb("fo%d" % i, [128, self.D]) for i in range(2)]
        for n_, i in enumerate(range(self.NTC, self.NT)):
            h = hb[n_ % 2]
            p.dma(h[:], self.scr["h"][i * 128:(i + 1) * 128, :])
            p.dma(self.out[(i - self.NTC) * 128:(i - self.NTC + 1) * 128, :], h[:], eng="gpsimd")
        p.emit()

    def phase_mlstm_gates(self):
        MH, TT, TC, NT, NTC = self.MH, self.TT, self.TC, self.NT, self.NTC
        p = self.prog()
        banks = self.bank_list(p)
        ident = p.sb("ident", [128, 128]); p.dma(ident[:], self.inp["ident"])
        gb = p.sb("gb", [MH, 4]); p.dma(gb[:], self.inp["ogb"])
        ones = p.sb("ones", [MH, TT])
        zc = p.sb("zc", [MH, 1])
        p.memset(ones[:], 1.0); p.memset(zc[:], 0.0)
        ig = p.sb("ig", [MH, TT]); fg = p.sb("fg", [MH, TT])
        Fc = p.sb("Fc", [MH, TT]); R = p.sb("R", [MH, TT])
        arr = [p.sb("q%d" % k, [MH, TT]) for k in range(5)] + [fg]
        dsc = p.sb("dsc", [MH, NT]); nre = p.sb("nre", [MH, NT])
        S = p.sb("S", [128, NT, 6, MH])
        for d in range(2):
            p.dma(ig[:], self.scr["gT"][(d * 2) * MH:(d * 2 + 1) * MH, :])
            p.dma(fg[:], self.scr["gT"][(d * 2 + 1) * MH:(d * 2 + 2) * MH, :])
            p.ts(ig[:], ig[:], gb[:, d * 2:d * 2 + 1], None, ALU.add)
            p.ts(fg[:], fg[:], gb[:, d * 2 + 1:d * 2 + 2], None, ALU.add)
            p.act(fg[:], fg[:], AF.Exp, scale=-1.0)
            p.ts(fg[:], fg[:], 1.0, None, ALU.add)
            p.act(fg[:], fg[:], AF.Ln)
            p.ts(fg[:], fg[:], -1.0, None, ALU.mult)

            def scan(out, d1, op1, segs):
                prev = None
                for (s0, s1) in segs:
                    def v(t):
                        x = t[:, s0:s1]
                        return x[:, ::-1] if d == 1 else x
                    init = 0.0 if prev is None else prev
                    p.op("vector", lambda e, o=v(out), a_=v(ones), b_=v(d1), init=init: e.tensor_tensor_scan(
                        out=o, data0=a_, data1=b_, initial=init, op0=ALU.mult, op1=op1),
                        reads=[ones, d1, out], writes=[out])
                    prev = out[:, s1 - 1:s1] if d == 0 else out[:, s0:s0 + 1]
            segs = [(0, TC), (TC, TT)]
            scan(Fc, fg, ALU.add, segs)
            p.tt(ig[:], ig[:], Fc[:], ALU.subtract)
            scan(R, ig, ALU.max, segs)
            p.tt(fg[:], Fc[:], R[:], ALU.add)
            p.act(fg[:], fg[:], AF.Exp, scale=-1.0)
            order = list(range(NT)) if d == 0 else (list(range(NTC - 1, -1, -1)) + list(range(NT - 1, NTC - 1, -1)))
            prev_idx = None
            for c in order:
                ch = slice(c * 128, (c + 1) * 128)
                ei = c * 128 + 127 if d == 0 else c * 128
                Fp = zc[:] if prev_idx is None else Fc[:, prev_idx:prev_idx + 1]
                Rp = zc[:] if prev_idx is None else R[:, prev_idx:prev_idx + 1]
                p.ts(nre[:, c:c + 1], R[:, ei:ei + 1], -1.0, None, ALU.mult)
                p.tt(dsc[:, c:c + 1], Rp, R[:, ei:ei + 1], ALU.subtract)
                p.ts(arr[0][:, ch], R[:, ch], Fp, None, ALU.add)
                p.act(arr[0][:, ch], arr[0][:, ch], AF.Exp, scale=-1.0)
                p.act(arr[1][:, ch], ig[:, ch], AF.Exp, bias=Fp)
                p.act(arr[2][:, ch], R[:, ch], AF.Exp, scale=-1.0, bias=Rp)
                p.act(arr[3][:, ch], ig[:, ch], AF.Exp, bias=nre[:, c:c + 1])
                p.act(arr[4][:, ch], R[:, ch], AF.Exp, scale=0.0, bias=dsc[:, c:c + 1])
                prev_idx = ei
            for c in range(NT):
                b = banks[c % 8]
                for k in range(6):
                    p.tr(b[:, k * MH:(k + 1) * MH], arr[k][:, c * 128:(c + 1) * 128], ident[0:MH, 0:MH])
                p.copy(S[:, c, :, :], b[:, 0:6 * MH].rearrange("p (k h) -> p k h", h=MH), eng="vector" if c % 2 == 0 else "scalar")
            p.dma(self.scr["mS"][d], S[:], eng="gpsimd")
        p.emit()

    def phase_mlstm(self):
        MH, TT, TC, NT, NTC, DV, DQK, QK, D = self.MH, self.TT, self.TC, self.NT, self.NTC, self.DV, self.DQK, self.QK, self.D
        NLC = NT - NTC
        DC = DQK // 128
        qkT, kvo, moT = self.scr["qkT"], self.scr["kvo"], self.scr["moT"]
        p = self.prog()
        ps_s = p.ps("ps_s", [128, 512]); ps_a = p.ps("ps_a", [128, 512]); ps_b = p.ps("ps_b", [128, 512])
        ps_d = p.ps("ps_d", [128, 512]); ps_u = [p.ps("ps_u%d" % i, [128, 512]) for i in range(2)]
        ps_n = p.ps("ps_n", [128, 512]); ps_t = p.ps("ps_t", [128, 512])
        ident = p.sb("ident", [128, 128]); p.dma(ident[:], self.inp["ident"])
        mask = p.sb("mask", [128, 2, 128]); p.dma(mask[:], self.inp["mmask"])
        S = [p.sb("S%d" % d, [128, NT, 6, MH]) for d in range(2)]
        for d in range(2):
            p.dma(S[d][:], self.scr["mS"][d])
        ng = p.sb("ng", [128, DV])
        hbuf = p.sb("hbuf", [128, NLC, DV])
        Cst = p.sb("Cst", [128, DC, DV + 1])
        NB = 3
        qT = [p.sb("qT%d" % i, [128, DC, 128]) for i in range(NB)]
        kT = [p.sb("kT%d" % i, [128, DC, 128]) for i in range(NB)]
        kt = [p.sb("kt%d" % i, [128, DQK]) for i in range(NB)]
        vt = [p.sb("vt%d" % i, [128, DV + 1]) for i in range(NB)]
        for i in range(NB):
            p.memset(vt[i][:, DV:DV + 1], 1.0)
        S0 = [p.sb("S0_%d" % i, [128, 128]) for i in range(2)]
        kw = [p.sb("kw%d" % i, [128, DQK]) for i in range(2)]
        n1 = [p.sb("n1_%d" % i, [128, DV]) for i in range(2)]
        n2 = [p.sb("n2_%d" % i, [128, DV]) for i in range(2)]
        sm = [p.sb("sm%d" % i, [128, 4]) for i in range(2)]
        ot = [p.sb("ot%d" % i, [128, DV]) for i in range(2)]
        st = p.sb("st", [128, 8, 6]); mv = p.sb("mv", [128, 2]); rstd = p.sb("rstd", [128, 1]); nmr = p.sb("nmr", [128, 1])
        tT = [p.sb("tT%d" % i, [128, DV // 128, 128], BF16) for i in range(2)]
        it = 0
        for hd in range(MH):
            self.load_bcast(p, ng[:], self.inp["odd_norm_g"][0:1, hd * DV:(hd + 1) * DV])
            for d in range(2):
                p.memset(Cst[:], 0.0)
                order = list(range(NT)) if d == 0 else (list(range(NTC - 1, -1, -1)) + list(range(NT - 1, NTC - 1, -1)))
                for c in order:
                    j = it % NB
                    it += 1
                    cs_ = slice(c * 128, (c + 1) * 128)
                    sc = lambda k: S[d][:, c, k, hd:hd + 1]
                    lat = c >= NTC
                    p.dma(kt[j][:], kvo[cs_, hd * DQK:(hd + 1) * DQK])
                    p.dma(vt[j][:, 0:DV], kvo[cs_, QK + hd * DV:QK + (hd + 1) * DV])
                    if lat:
                        p.dma(qT[j][:], qkT[hd * DQK:(hd + 1) * DQK, cs_].rearrange("(c p) t -> p c t", p=128))
                        p.dma(kT[j][:], qkT[QK + hd * DQK:QK + (hd + 1) * DQK, cs_].rearrange("(c p) t -> p c t", p=128))
                        for dc in range(DC):
                            p.mm(ps_s[:, 0:128], kT[j][:, dc, :], qT[j][:, dc, :], start=(dc == 0), stop=(dc == DC - 1))
                        s0 = S0[it % 2]
                        p.stt(s0[:], ps_s[:, 0:128], sc(1), mask[:, d, :], ALU.mult, ALU.mult)
                        p.mm(ps_a[:, 0:DV], s0[:], vt[j][:, 0:DV])
                        p.mm(ps_d[:, 0:1], s0[:], vt[j][:, DV:DV + 1])
                        for dc in range(DC):
                            p.mm(ps_b[:, 0:DV], qT[j][:, dc, :], Cst[:, dc, 0:DV], start=(dc == 0), stop=(dc == DC - 1))
                        for dc in range(DC):
                            p.mm(ps_d[:, 1:2], qT[j][:, dc, :], Cst[:, dc, DV:DV + 1], start=(dc == 0), stop=(dc == DC - 1))
                        a1 = n1[it % 2]; a2 = n2[it % 2]; q_ = sm[it % 2]
                        p.act(a1[:], ps_a[:, 0:DV], AF.Copy, scale=sc(0))
                        p.stt(a2[:], ps_b[:, 0:DV], sc(2), a1[:], ALU.mult, ALU.add)
                        p.ts(q_[:, 0:1], ps_d[:, 0:1], sc(0), None, ALU.mult)
                        p.stt(q_[:, 1:2], ps_d[:, 1:2], sc(2), q_[:, 0:1], ALU.mult, ALU.add)
                        p.ts(q_[:, 2:3], q_[:, 1:2], -1.0, None, ALU.mult)
                        p.tt(q_[:, 2:3], q_[:, 2:3], q_[:, 1:2], ALU.max)
                        p.ts(q_[:, 2:3], q_[:, 2:3], sc(5), None, ALU.max)
                        p.op("vector", lambda e, q_=q_: e.reciprocal(out=q_[:, 3:4], in_=q_[:, 2:3]), reads=[q_], writes=[q_])
                        hv = hbuf[:, c - NTC, :]
                        if d == 0:
                            p.ts(hv, a2[:], q_[:, 3:4], None, ALU.mult)
                        else:
                            p.stt(hv, a2[:], q_[:, 3:4], hv, ALU.mult, ALU.add)
                    k_ = kw[it % 2]
                    p.ts(k_[:], kt[j][:], sc(3), None, ALU.mult, eng="gpsimd")
                    for dc in range(DC):
                        p.mm(ps_u[dc % 2][:, 0:DV], k_[:, dc * 128:(dc + 1) * 128], vt[j][:, 0:DV])
                        p.mm(ps_n[:, dc:dc + 1], k_[:, dc * 128:(dc + 1) * 128], vt[j][:, DV:DV + 1])
                        p.stt(Cst[:, dc, 0:DV], Cst[:, dc, 0:DV], sc(4), ps_u[dc % 2][:, 0:DV], ALU.mult, ALU.add)
                        p.stt(Cst[:, dc, DV:DV + 1], Cst[:, dc, DV:DV + 1], sc(4), ps_n[:, dc:dc + 1], ALU.mult, ALU.add)
            for c in range(NLC):
                i = c + NTC
                o_ = ot[c % 2]; t_ = tT[c % 2]; a1 = n1[c % 2]
                p.dma(o_[:], kvo[i * 128:(i + 1) * 128, QK + D + hd * DV:QK + D + (hd + 1) * DV])
                p.act(o_[:], o_[:], AF.Sigmoid)
                self.ln_core(p, hbuf[:, c, :], a1[:], st, mv, rstd, nmr, "")
                p.tt(a1[:], a1[:], ng[:], ALU.mult, eng="gpsimd")
                p.tt(a1[:], a1[:], o_[:], ALU.mult)
                for b_ in range(DV // 128):
                    p.tr(ps_t[:, b_ * 128:(b_ + 1) * 128], a1[:, b_ * 128:(b_ + 1) * 128], ident[:])
                p.copy(t_[:], ps_t[:, 0:DV].rearrange("p (a t) -> p a t", t=128), eng="scalar")
                p.dma(moT[hd * DV:(hd + 1) * DV, i * 128:(i + 1) * 128].rearrange("(a p) t -> p a t", p=128), t_[:],
                      eng="gpsimd", writes=[("moT", hd, c)])
        p.emit()

    def build(self):
        D, TT, T, TC, KC, E, MH = self.D, self.TT, self.T, self.TC, self.KC, self.E, self.MH
        QK, DL, DP, PG, LH = self.QK, self.DL, self.DP, self.PG, self.LH
        nc = self.nc
        self.din("xin", [TT, D]); self.din("condT", [128, KC, 2])
        self.din("ada_w", [2, D, 6 * D]); self.din("ada_b", [2, 6 * D])
        self.din("ln_g", [2, 2, D]); self.din("ln_b", [2, 2, D])
        self.din("even_w_in", [D, 2 * DL + DP]); self.din("lru_gate_w", [2, 2, LH, 128, 128])
        self.din("ppar", [128, 11 * LH + DP // 128]); self.din("pool_w", [4, PG, PG]); self.din("even_w_out", [D, D])
        self.din("odd_w_in", [D, 2 * QK + 2 * D + 4 * MH]); self.din("ogb", [MH, 4]); self.din("odd_norm_g", [1, D])
        self.din("odd_w_out", [D, D]); self.din("router_w", [2, D, E])
        self.din("expert_w_gu", [2, E, D, 2 * self.DE]); self.din("expert_w_down", [2, E, self.DE, D])
        self.din("ident", [128, 128]); self.din("iota", [128, 512]); self.din("pidx", [128, 4])
        self.din("invcnt", [4, TT]); self.din("mmask", [128, 2, 128])
        self.out = nc.dram_tensor("out", [T, D], F32, kind="ExternalOutput").ap()
        self.dscr("mods", [2, 2, 6 * D]); self.dscr("h", [TT, D]); self.dscr("uT", [D, TT], BF16)
        self.dscr("pT", [max(2 * DL + DP, 2 * QK), TT]); self.dscr("moT", [D, TT], BF16); self.dscr("dT", [DP, TT], BF16)
        self.dscr("y", [TT, D]); self.dscr("xm", [TT, D]); self.dscr("aff", [TT, E]); self.dscr("affT", [E, TT])
        self.dscr("idx_l", [128, (self.CAP + 127) // 128, E]); self.dscr("idx_c", [128, (self.CAPC + 127) // 128, E]); self.dscr("m", [TT, D])
        self.dscr("gT", [4 * MH, TT]); self.dscr("kvo", [TT, QK + 2 * D]); self.dscr("mS", [2, 128, self.NT, 6, MH])
        self.scr["qkT"] = self.scr["pT"]
        self.sy = Sync(nc)
        alltiles = list(range(self.NT)); lattiles = list(range(self.NTC, self.NT))
        self.phase_adaln()
        self.phase_ln0()
        self.phase_u(0)
        self.gemm_fm(self.inp["even_w_in"], self.scr["uT"], self.scr["pT"], D, 2 * DL + DP, self.tcols())
        self.phase_lru()
        self.phase_pool()
        for g in range(4):
            self.gemm_fm(self.inp["pool_w"][g], self.scr["dT"][g * PG:(g + 1) * PG, :], self.scr["moT"][DL + g * PG:DL + (g + 1) * PG, :],
                         PG, PG, self.tcols(), row_scale=self.inp["ppar"][:, 11 * LH + g * (PG // 128):11 * LH + (g + 1) * (PG // 128)], odt=BF16)
        self.gemm_tm(self.scr["moT"], self.inp["even_w_out"], self.scr["y"], D, D, alltiles)
        self.phase_post(0, 0, alltiles, True)
        self.phase_route(TC, T, self.CAP, "l")
        self.phase_route(0, TC, self.CAPC, "c")
        self.phase_moe(0, TC, T, self.CAP, "l")
        self.phase_moe(0, 0, TC, self.CAPC, "c")
        self.phase_post(0, 1, alltiles, False)
        self.phase_u(1)
        wi = self.inp["odd_w_in"]
        self.gemm_fm(wi[:, 0:QK], self.scr["uT"], self.scr["qkT"][0:QK, :], D, QK, self.tcols(lat_only=True), scale=float(self.DQK) ** -0.5)
        self.gemm_fm(wi[:, QK:2 * QK], self.scr["uT"], self.scr["qkT"][QK:2 * QK, :], D, QK, self.tcols(lat_only=True))
        self.gemm_fm(wi[:, 2 * QK + 2 * D:2 * QK + 2 * D + 4 * MH], self.scr["uT"], self.scr["gT"], D, 4 * MH, self.tcols())
        self.gemm_tm(self.scr["uT"], wi[:, QK:2 * QK + 2 * D], self.scr["kvo"], D, QK + 2 * D, alltiles)
        self.phase_mlstm_gates()
        self.phase_mlstm()
        self.gemm_tm(self.scr["moT"], self.inp["odd_w_out"], self.scr["y"], D, D, lattiles)
        self.phase_post(1, 0, lattiles, True)
        self.phase_route(TC, T, self.CAP, "l")
        self.phase_moe(1, TC, T, self.CAP, "l")
        self.phase_post(1, 1, lattiles, False)
        self.phase_out()
        self.sy.close()
        return nc


def host_inputs(cfg, b, x, c, ctx, c_ctx, ada_w, ada_b, ln_g, ln_b, even_w_in, even_conv_w, even_conv_b, lru_gate_w,
                lru_gate_b, lru_lambda, pool_w, pool_scale, even_w_out, odd_w_in, odd_gate_b, odd_norm_g, odd_w_out,
                router_w, expert_w_gu, expert_w_down):
    D, T, TC, GW, MH = cfg["D"], cfg["T"], cfg["TC"], cfg["GW"], cfg["MH"]
    KC = D // 128
    LH = D // 256
    f = lambda a: np.ascontiguousarray(np.asarray(a, dtype=np.float32))
    m = {}
    m["xin"] = f(np.concatenate([ctx[b], x[b]], axis=0))
    m["condT"] = f(np.stack([c[b], c_ctx], axis=-1).reshape(KC, 128, 2).transpose(1, 0, 2))
    m["ada_w"] = f(ada_w); m["ada_b"] = f(ada_b); m["ln_g"] = f(ln_g); m["ln_b"] = f(ln_b)
    m["even_w_in"] = f(even_w_in[0]); m["lru_gate_w"] = f(lru_gate_w[0])
    cw = np.asarray(even_conv_w[0]).reshape(4, LH, 128).transpose(2, 1, 0).reshape(128, LH * 4)
    cb = np.asarray(even_conv_b[0]).reshape(LH, 128).T
    gb = np.asarray(lru_gate_b[0]).reshape(2, 2, LH, 128).transpose(3, 2, 0, 1).reshape(128, LH * 4)
    lam = np.asarray(lru_lambda[0]).reshape(2, LH, 128).transpose(2, 1, 0).reshape(128, LH * 2)
    psc = np.asarray(pool_scale[0]).reshape(-1, 128).T
    m["ppar"] = f(np.concatenate([cw, cb, gb, lam, psc], axis=1))
    m["pool_w"] = f(pool_w[0]); m["even_w_out"] = f(even_w_out[0])
    m["odd_w_in"] = f(odd_w_in[0])
    m["ogb"] = f(np.asarray(odd_gate_b[0]).reshape(2, 2, MH).transpose(2, 0, 1).reshape(MH, 4))
    m["odd_norm_g"] = f(odd_norm_g); m["odd_w_out"] = f(odd_w_out[0]); m["router_w"] = f(router_w)
    m["expert_w_gu"] = f(expert_w_gu); m["expert_w_down"] = f(expert_w_down)
    m["ident"] = np.eye(128, dtype=np.float32)
    m["iota"] = f(np.tile(np.arange(512, dtype=np.float32)[None, :], (128, 1)))
    m["pidx"] = f(np.arange(128, dtype=np.float32)[:, None] + 128.0 * np.arange(4, dtype=np.float32)[None, :])
    inv = np.zeros((4, TC + T), np.float32)
    GH = T // GW

    def cnt(L, w):
        pos = np.arange(L)
        lo = np.clip(pos - w // 2, 0, L); hi = np.clip(pos + w - w // 2, 0, L)
        return (hi - lo).astype(np.float64)
    for g, w in enumerate(POOL_WINDOWS):
        inv[g, :TC] = 1.0 / cnt(TC, w)
        inv[g, TC:] = (1.0 / (cnt(GH, w)[:, None] * cnt(GW, w)[None, :])).reshape(-1)
    m["invcnt"] = inv
    s_, t_ = np.meshgrid(np.arange(128), np.arange(128), indexing="ij")
    m["mmask"] = f(np.stack([(s_ <= t_), (s_ >= t_)], axis=1).astype(np.float32))
    return m


_NC_CACHE = {}


def run(cfg, inputs, n_samples=2):
    key = tuple(sorted((k, v) for k, v in cfg.items() if k != 'debug')) + (tuple(cfg.get('debug', ())),)
    if key not in _NC_CACHE:
        _NC_CACHE[key] = Builder(cfg).build()
    nc = _NC_CACHE[key]
    in_maps = [host_inputs(cfg, b, **inputs) for b in range(n_samples)]
    res = run_bass_kernel_spmd(nc, in_maps, core_ids=list(range(n_samples)))
    return np.stack([np.asarray(r["out"]) for r in res.results], axis=0).astype(np.float32)


def kernel(**inputs):
    inputs = {k: np.asarray(v) for k, v in inputs.items()}
    return run(CFG_FULL, inputs, n_samples=2)
```

```python
import numpy as np
import concourse.bass as bass
import concourse.mybir as mybir
from concourse.bass_utils import run_bass_kernel_spmd

F32 = mybir.dt.float32
BF16 = mybir.dt.bfloat16
AF = mybir.ActivationFunctionType
ALU = mybir.AluOpType
ENGS = ("tensor", "vector", "scalar", "gpsimd", "sync")
N_DMA_SEMS = 40


def _key(x):
    if isinstance(x, (str, tuple)):
        return x
    t = getattr(x, "tensor", x)
    return t.name


class Sync:
    def __init__(self, nc):
        self.guards = []
        self.esem = {}
        for e in ENGS:
            g = nc.semaphore("c_" + e)
            self.esem[e] = g.__enter__()
            self.guards.append(g)
        self.dsem = []
        for j in range(N_DMA_SEMS):
            g = nc.semaphore("d_%d" % j)
            self.dsem.append(g.__enter__())
            self.guards.append(g)
        self.ecount = {e: 0 for e in ENGS}
        self.ndma = 0

    def close(self):
        for g in reversed(self.guards):
            g.__exit__(None, None, None)


class Prog:
    def __init__(self, nc, sy):
        self.nc = nc
        self.sy = sy
        self.ops = []
        self.stack = []
        sy.phase = getattr(sy, "phase", 0) + 1
        self.pfx = "p%d_" % sy.phase

    def sb(self, name, shape, dt=F32):
        g = self.nc.sbuf_tensor(self.pfx + name, list(shape), dt)
        t = g.__enter__()
        self.stack.append(g)
        return t

    def ps(self, name, shape, dt=F32):
        g = self.nc.psum_tensor(self.pfx + name, list(shape), dt)
        t = g.__enter__()
        self.stack.append(g)
        return t

    def op(self, eng, fn, reads=(), writes=(), dma=False):
        self.ops.append(dict(eng=eng, fn=fn, r=[_key(k) for k in reads], w=[_key(k) for k in writes], dma=dma))

    def dma(self, out, in_, eng="sync", reads=None, writes=None, **kw):
        self.op(eng, lambda e: e.dma_start(out=out, in_=in_, **kw),
                reads=[in_] if reads is None else reads, writes=[out] if writes is None else writes, dma=True)

    def mm(self, out, lhsT, rhs, start=True, stop=True, reads=None, writes=None):
        self.op("tensor", lambda e: e.matmul(out, lhsT, rhs, start=start, stop=stop),
                reads=[lhsT, rhs] if reads is None else reads, writes=[out] if writes is None else writes)

    def tr(self, out, in_, ident):
        self.op("tensor", lambda e: e.transpose(out, in_, ident), reads=[in_, ident], writes=[out])

    def act(self, out, in_, func, bias=None, scale=1.0, eng="scalar"):
        r = [in_] + [x for x in (bias, scale) if not isinstance(x, (int, float, type(None)))]
        kw = {}
        if bias is not None:
            kw["bias"] = bias
        self.op(eng, lambda e: e.activation(out=out, in_=in_, func=func, scale=scale, **kw), reads=r, writes=[out])

    def tt(self, out, in0, in1, op, eng="vector"):
        self.op(eng, lambda e: e.tensor_tensor(out=out, in0=in0, in1=in1, op=op), reads=[in0, in1], writes=[out])

    def ts(self, out, in0, s1, s2, op0, op1=ALU.bypass, eng="vector"):
        r = [in0] + [x for x in (s1, s2) if not isinstance(x, (int, float, type(None)))]
        self.op(eng, lambda e: e.tensor_scalar(out=out, in0=in0, scalar1=s1, scalar2=s2, op0=op0, op1=op1),
                reads=r, writes=[out])

    def stt(self, out, in0, scalar, in1, op0, op1):
        r = [in0, in1] + ([scalar] if not isinstance(scalar, (int, float)) else [])
        self.op("vector", lambda e: e.scalar_tensor_tensor(out=out, in0=in0, scalar=scalar, in1=in1, op0=op0, op1=op1),
                reads=r, writes=[out])

    def copy(self, out, in_, eng="vector"):
        if eng == "scalar":
            self.op(eng, lambda e: e.copy(out=out, in_=in_), reads=[in_], writes=[out])
        else:
            self.op(eng, lambda e: e.tensor_copy(out=out, in_=in_), reads=[in_], writes=[out])

    def memset(self, ap, val, eng="vector"):
        self.op(eng, lambda e: e.memset(ap, val), reads=[], writes=[ap])

    def emit(self):
        nc, sy, ops = self.nc, self.sy, self.ops
        wr, rd = {}, {}
        nsem = N_DMA_SEMS
        for i, o in enumerate(ops):
            deps = set()
            for k in o["r"]:
                deps.update(wr.get(k, ()))
            for k in o["w"]:
                deps.update(wr.get(k, ()))
                deps.update(rd.get(k, ()))
            deps.discard(i)
            o["deps"] = deps
            for k in o["w"]:
                if rd.get(k):
                    wr[k] = [i]
                    rd[k] = []
                else:
                    wr.setdefault(k, []).append(i)
            for k in o["r"]:
                rd.setdefault(k, []).append(i)
            if o["dma"]:
                o["dma_idx"] = sy.ndma
                sy.ndma += 1
            else:
                sy.ecount[o["eng"]] += 1
                o["eidx"] = sy.ecount[o["eng"]]
        final_e = dict(sy.ecount)
        final_nd = sy.ndma
        esem, dsem = sy.esem, sy.dsem

        def gen(engname):
            def body(eng):
                known = {}

                def need(key, sem, val):
                    if val <= 0 or known.get(key, 0) >= val:
                        return
                    eng.wait_ge(sem, val)
                    known[key] = val

                for o in ops:
                    if o["eng"] != engname:
                        continue
                    req = {}
                    for d in o["deps"]:
                        p = ops[d]
                        if p["dma"]:
                            j = p["dma_idx"]
                            k_, s_, v_ = ("d", j % nsem), dsem[j % nsem], 16 * (j // nsem + 1)
                        else:
                            k_, s_, v_ = ("c", p["eng"]), esem[p["eng"]], p["eidx"]
                        if k_ not in req or req[k_][1] < v_:
                            req[k_] = (s_, v_)
                    for k_, (s_, v_) in req.items():
                        need(k_, s_, v_)
                    if o["dma"]:
                        j = o["dma_idx"]
                        if j >= nsem:
                            need(("d", j % nsem), dsem[j % nsem], 16 * (j // nsem))
                        o["fn"](eng).then_inc(dsem[j % nsem], 16)
                    else:
                        o["fn"](eng).then_inc(esem[engname], 1)
                for e2 in ENGS:
                    need(("c", e2), esem[e2], final_e[e2])
                for j in range(max(0, final_nd - nsem), final_nd):
                    need(("d", j % nsem), dsem[j % nsem], 16 * (j // nsem + 1))
            return body

        with nc.Block() as block:
            block.tensor(gen("tensor"))
            block.vector(gen("vector"))
            block.scalar(gen("scalar"))
            block.gpsimd(gen("gpsimd"))
            block.sync(gen("sync"))
        for g in reversed(self.stack):
            g.__exit__(None, None, None)
        self.stack = []


CFG_FULL = dict(D=4096, T=4096, TC=256, GW=64, MH=8, E=16)
POOL_WINDOWS = (2, 4, 8, 16)
LN_EPS = 1e-5
DEPTH = 2
DN_ALPHA = (2 * DEPTH) ** 0.25


class Builder:
    def __init__(self, cfg):
        self.cfg = cfg
        D, T, TC = cfg["D"], cfg["T"], cfg["TC"]
        self.D, self.T, self.TC, self.GW, self.MH, self.E = D, T, TC, cfg["GW"], cfg["MH"], cfg["E"]
        self.TT = T + TC
        self.KC = D // 128
        self.NT = self.TT // 128
        self.NTC = TC // 128
        self.DL = D // 2
        self.LH = self.DL // 128
        self.DP = D // 2
        self.PG = self.DP // 4
        self.DV = D // self.MH
        self.DQK = self.DV // 2
        self.QK = self.MH * self.DQK
        self.DE = D // 4
        self.CAP = 2 * T // self.E
        self.CAPC = 2 * TC // self.E
        self.nc = nc = bass.Bass("TRN2", target_bir_lowering=False)
        self.sy = None
        self.inp = {}
        self.scr = {}

    def din(self, name, shape):
        self.inp[name] = self.nc.dram_tensor(name, list(shape), F32, kind="ExternalInput").ap()
        return self.inp[name]

    def dscr(self, name, shape, dt=F32):
        kind = "ExternalOutput" if name in self.cfg.get("debug", ()) else "Internal"
        self.scr[name] = self.nc.dram_tensor(name, list(shape), dt, kind=kind).ap()
        return self.scr[name]

    def prog(self):
        return Prog(self.nc, self.sy)

    def bank_list(self, p, n=8):
        return [p.ps("bank%d" % i, [128, 512]) for i in range(n)]

    def ln_core(self, p, src, dst, st, mv, rstd, nmr, sfx, eng_apply="scalar"):
        D = src.shape[1]
        nch = D // 512 if D >= 512 else 1
        w = D // nch
        for c in range(nch):
            p.op("vector", lambda e, c=c: e.bn_stats(out=st[:, c, :], in_=src[:, c * w:(c + 1) * w]), reads=[src], writes=[st])
        p.op("vector", lambda e: e.bn_aggr(out=mv[:], in_=st[:, 0:nch, :].rearrange("p a b -> p (a b)")), reads=[st], writes=[mv])
        p.ts(rstd[:], mv[:, 1:2], LN_EPS, None, ALU.add)
        p.act(rstd[:], rstd[:], AF.Ln)
        p.act(rstd[:], rstd[:], AF.Exp, scale=-0.5)
        p.stt(nmr[:], mv[:, 0:1], -1.0, rstd[:], ALU.mult, ALU.mult)
        p.act(dst, src, AF.Identity, bias=nmr[:], scale=rstd[:])

    def transpose_tile(self, p, src, dst, ident, banks, nblk, bi0=0):
        g = 0
        b = 0
        while b < nblk:
            n = min(4, nblk - b)
            bank = banks[(bi0 + g) % len(banks)]
            for j in range(n):
                p.tr(bank[:, j * 128:(j + 1) * 128], src[:, (b + j) * 128:(b + j + 1) * 128], ident[:])
            o = dst[:, b:b + n, :]
            i_ = bank[:, 0:n * 128].rearrange("p (a t) -> p a t", t=128)
            if g % 2 == 0:
                p.copy(o, i_, eng="vector")
            else:
                p.copy(o, i_, eng="scalar")
            b += n
            g += 1

    def load_bcast(self, p, dst, row_ap):
        p.dma(dst, row_ap.partition_broadcast(128))

    def gemm_fm(self, W, inT, outT, K, N, tcols, scale=None, row_scale=None, TS=512, odt=F32):
        p = self.prog()
        kc = K // 128
        banks = self.bank_list(p)
        xin = [p.sb("gx%d" % i, [128, kc, TS], BF16) for i in range(2)]
        wst = [p.sb("gws%d" % i, [128, kc, 128]) for i in range(2)]
        wch = [p.sb("gw%d" % i, [128, kc, 128], BF16) for i in range(2)]
        ob = [p.sb("go%d" % i, [128, TS], odt) for i in range(3)]
        rs = None
        if row_scale is not None:
            rs = p.sb("grs", [128, (N + 127) // 128])
            p.dma(rs[:], row_scale, allow_slow_non_contiguous=True)
        NB = (N + 127) // 128
        items = [(ti, nb) for ti in range(len(tcols)) for nb in range(NB)]

        def load_x(ti):
            t0, tsz = tcols[ti]
            p.dma(xin[ti % 2][:, :, 0:tsz], inT[:, t0:t0 + tsz].rearrange("(c p) t -> p c t", p=128))

        def load_w(k):
            nb = items[k][1]
            m = min(128, N - nb * 128)
            p.dma(wst[k % 2][:, :, 0:m], W[:, nb * 128:nb * 128 + m].rearrange("(c p) m -> p c m", p=128))
            p.copy(wch[k % 2][:, :, 0:m], wst[k % 2][:, :, 0:m], eng="gpsimd")
        load_x(0)
        load_w(0)
        for k, (ti, nb) in enumerate(items):
            t0, tsz = tcols[ti]
            m = min(128, N - nb * 128)
            if k + 1 < len(items):
                load_w(k + 1)
            if nb == 0 and ti + 1 < len(tcols):
                load_x(ti + 1)
            x = xin[ti % 2]; w = wch[k % 2]
            bank = banks[k % 8]
            for c in range(kc):
                p.mm(bank[0:m, 0:tsz], w[:, c, 0:m], x[:, c, 0:tsz], start=(c == 0), stop=(c == kc - 1))
            o = ob[k % 3]
            if rs is not None:
                p.ts(o[0:m, 0:tsz], bank[0:m, 0:tsz], rs[0:m, nb:nb + 1], None, ALU.mult)
            elif scale is not None:
                p.act(o[0:m, 0:tsz], bank[0:m, 0:tsz], AF.Copy, scale=scale)
            elif k % 2 == 0:
                p.copy(o[0:m, 0:tsz], bank[0:m, 0:tsz], eng="vector")
            else:
                p.copy(o[0:m, 0:tsz], bank[0:m, 0:tsz], eng="scalar")
            p.dma(outT[nb * 128:nb * 128 + m, t0:t0 + tsz], o[0:m, 0:tsz], eng="sync", writes=[(_key(outT), ti, nb)])
        p.emit()

    def gemm_tm(self, inT, W, out, K, N, tiles):
        p = self.prog()
        kc = K // 128
        banks = self.bank_list(p)
        wb = p.sb("hw", [128, kc, 512], BF16)
        wst = [p.sb("hws%d" % i, [128, 8, 512]) for i in range(2)]
        xin = [p.sb("hx%d" % i, [128, kc, 128], BF16) for i in range(2)]
        ob = [p.sb("ho%d" % i, [128, 512]) for i in range(3)]
        it = 0
        g_ = 0
        for nb in range((N + 511) // 512):
            n = min(512, N - nb * 512)
            for c0 in range(0, kc, 8):
                c1 = min(kc, c0 + 8)
                ws = wst[g_ % 2]
                p.dma(ws[:, 0:c1 - c0, 0:n], W[c0 * 128:c1 * 128, nb * 512:nb * 512 + n].rearrange("(c p) n -> p c n", p=128))
                p.copy(wb[:, c0:c1, 0:n], ws[:, 0:c1 - c0, 0:n], eng="gpsimd" if g_ % 2 == 0 else "vector")
                g_ += 1
            for q_, i in enumerate(tiles):
                if q_ == 0:
                    p.dma(xin[it % 2][:], inT[:, i * 128:(i + 1) * 128].rearrange("(c p) t -> p c t", p=128))
                if q_ + 1 < len(tiles):
                    i2 = tiles[q_ + 1]
                    p.dma(xin[(it + 1) % 2][:], inT[:, i2 * 128:(i2 + 1) * 128].rearrange("(c p) t -> p c t", p=128))
                x = xin[it % 2]
                bank = banks[it % 8]
                for c in range(kc):
                    p.mm(bank[:, 0:n], x[:, c, :], wb[:, c, 0:n], start=(c == 0), stop=(c == kc - 1))
                o = ob[it % 3]
                p.copy(o[:, 0:n], bank[:, 0:n], eng="vector" if it % 2 == 0 else "scalar")
                p.dma(out[i * 128:(i + 1) * 128, nb * 512:nb * 512 + n], o[:, 0:n], eng="sync",
                      writes=[(_key(out), i, nb)])
                it += 1
        p.emit()

    def phase_adaln(self):
        D, KC = self.D, self.KC
        p = self.prog()
        banks = self.bank_list(p)
        cT = p.sb("cT", [128, KC, 2])
        sil = p.sb("sil", [128, KC, 2])
        p.dma(cT[:], self.inp["condT"])
        p.act(sil[:], cT[:], AF.Silu)
        NB = 2048
        wbuf = [p.sb("aw%d" % i, [128, NB]) for i in range(3)]
        bb = [p.sb("ab%d" % i, [2, NB]) for i in range(2)]
        obuf = [p.sb("ao%d" % i, [2, NB]) for i in range(2)]
        it = 0
        blk = 0
        for l in range(2):
            for nb in range(6 * D // NB):
                bset = banks[0:4] if blk % 2 == 0 else banks[4:8]
                b_ = bb[blk % 2]
                p.dma(b_[:], self.inp["ada_b"][l:l + 1, nb * NB:(nb + 1) * NB].partition_broadcast(2))
                for c in range(KC):
                    w = wbuf[it % 3]
                    it += 1
                    p.dma(w[:], self.inp["ada_w"][l, c * 128:(c + 1) * 128, nb * NB:(nb + 1) * NB])
                    for j in range(4):
                        p.mm(bset[j][0:2, :], sil[:, c, :], w[:, j * 512:(j + 1) * 512], start=(c == 0), stop=(c == KC - 1))
                o = obuf[blk % 2]
                for j in range(4):
                    p.tt(o[:, j * 512:(j + 1) * 512], bset[j][0:2, :], b_[:, j * 512:(j + 1) * 512], ALU.add)
                p.dma(self.scr["mods"][l, :, nb * NB:(nb + 1) * NB], o[:], eng="gpsimd", writes=[("mods", l, nb)])
                blk += 1
        p.emit()

    def mod_row(self, l, cnd, seg):
        D = self.D
        return self.scr["mods"][l, cnd:cnd + 1, seg * D:(seg + 1) * D]

    def phase_ln0(self):
        D = self.D
        p = self.prog()
        xb = [p.sb("xb%d" % i, [128, D]) for i in range(2)]
        hb = [p.sb("hb%d" % i, [128, D]) for i in range(2)]
        st = p.sb("st", [128, 8, 6]); mv = p.sb("mv", [128, 2]); rstd = p.sb("rstd", [128, 1]); nmr = p.sb("nmr", [128, 1])
        for i in range(self.NT):
            x = xb[i % 2]; h = hb[i % 2]
            p.dma(x[:], self.inp["xin"][i * 128:(i + 1) * 128, :])
            self.ln_core(p, x[:], h[:], st, mv, rstd, nmr, "")
            p.dma(self.scr["h"][i * 128:(i + 1) * 128, :], h[:], eng="gpsimd", writes=[("h", i)])
        p.emit()

    def phase_u(self, l):
        D, KC = self.D, self.KC
        p = self.prog()
        banks = self.bank_list(p)
        ident = p.sb("ident", [128, 128]); p.dma(ident[:], self.inp["ident"])
        rows = {}
        for cnd in range(2):
            sc = p.sb("sc%d" % cnd, [128, D]); sh = p.sb("sh%d" % cnd, [128, D])
            self.load_bcast(p, sc[:], self.mod_row(l, cnd, 1))
            self.load_bcast(p, sh[:], self.mod_row(l, cnd, 0))
            p.ts(sc[:], sc[:], 1.0, None, ALU.add)
            rows[cnd] = (sc, sh)
        hb = [p.sb("uh%d" % i, [128, D]) for i in range(2)]
        ub = [p.sb("uu%d" % i, [128, D]) for i in range(2)]
        ut = [p.sb("ut%d" % i, [128, KC, 128], BF16) for i in range(2)]
        for i in range(self.NT):
            cnd = 1 if i < self.NTC else 0
            sc, sh = rows[cnd]
            h = hb[i % 2]; u = ub[i % 2]; t = ut[i % 2]
            p.dma(h[:], self.scr["h"][i * 128:(i + 1) * 128, :])
            p.tt(u[:], h[:], sc[:], ALU.mult, eng="gpsimd")
            p.tt(u[:], u[:], sh[:], ALU.add, eng="vector")
            self.transpose_tile(p, u, t, ident, banks, KC, bi0=i * 3)
            p.dma(self.scr["uT"][:, i * 128:(i + 1) * 128].rearrange("(c p) t -> p c t", p=128), t[:], eng="gpsimd",
                  writes=[("uT", i)])
        p.emit()

    def tcols(self, TS=512, lat_only=False):
        out = []
        t = self.TC if lat_only else 0
        while t < self.TT:
            s = min(TS, self.TT - t)
            out.append((t, s))
            t += s
        return out

    def phase_lru(self):
        TT, TC, T, LH, DL = self.TT, self.TC, self.T, self.LH, self.DL
        pT, moT, pp = self.scr["pT"], self.scr["moT"], self.inp["ppar"]
        p = self.prog()
        banks = self.bank_list(p)
        par = p.sb("par", [128, pp.shape[1]]); p.dma(par[:], pp)
        o_cw, o_cb, o_gb, o_lam = 0, 4 * LH, 5 * LH, 9 * LH
        c8 = p.sb("c8", [128, 2 * LH])
        p.act(c8[:], par[:, o_lam:o_lam + 2 * LH], AF.Exp, scale=-1.0)
        p.ts(c8[:], c8[:], 1.0, None, ALU.add)
        p.act(c8[:], c8[:], AF.Ln)
        p.ts(c8[:], c8[:], -8.0, None, ALU.mult)
        xs = [p.sb("xs%d" % i, [128, TT]) for i in range(1)]
        zs = [p.sb("zs%d" % i, [128, TT]) for i in range(1)]
        y = p.sb("y", [128, TT]); yo = p.sb("yo", [128, TT], BF16)
        gw = [p.sb("gw%d" % i, [128, 4, 128]) for i in range(2)]
        aa = [p.sb("aa%d" % d, [128, TT]) for d in range(2)]
        bx = [p.sb("bx%d" % d, [128, TT]) for d in range(2)]
        hh = [p.sb("hh%d" % d, [128, TT]) for d in range(2)]
        t1 = [p.sb("t1_%d" % i, [128, 512]) for i in range(2)]
        t2 = [p.sb("t2_%d" % i, [128, 512]) for i in range(2)]
        tcs = self.tcols(512)
        it = 0
        for hd in range(LH):
            x = xs[0]; z = zs[0]; g = gw[hd % 2]
            p.dma(x[:], pT[hd * 128:(hd + 1) * 128, :])
            p.dma(z[:], pT[DL + hd * 128:DL + (hd + 1) * 128, :])
            p.dma(g[:], self.inp["lru_gate_w"][:, :, hd, :, :].rearrange("d g i j -> i (d g) j"))
            cw = lambda k: par[:, o_cw + hd * 4 + k:o_cw + hd * 4 + k + 1]
            cb = par[:, o_cb + hd:o_cb + hd + 1]
            for (s0, L) in ((0, TC), (TC, T)):
                p.ts(y[:, s0:s0 + L], x[:, s0:s0 + L], cw(1), cb, ALU.mult, ALU.add)
                p.stt(y[:, s0 + 1:s0 + L], x[:, s0:s0 + L - 1], cw(0), y[:, s0 + 1:s0 + L], ALU.mult, ALU.add)
                p.stt(y[:, s0:s0 + L - 1], x[:, s0 + 1:s0 + L], cw(2), y[:, s0:s0 + L - 1], ALU.mult, ALU.add)
                p.stt(y[:, s0:s0 + L - 2], x[:, s0 + 2:s0 + L], cw(3), y[:, s0:s0 + L - 2], ALU.mult, ALU.add)
            for d in range(2):
                gbr = par[:, o_gb + hd * 4 + d * 2:o_gb + hd * 4 + d * 2 + 1]
                gbi = par[:, o_gb + hd * 4 + d * 2 + 1:o_gb + hd * 4 + d * 2 + 2]
                c8d = c8[:, hd * 2 + d:hd * 2 + d + 1]
                for (t0, ts_) in tcs:
                    br = banks[it % 8]; bi = banks[(it + 1) % 8]
                    a1 = t1[(it // 2) % 2]; a2 = t2[(it // 2) % 2]
                    it += 2
                    p.mm(br[:, 0:ts_], g[:, d * 2, :], y[:, t0:t0 + ts_])
                    p.mm(bi[:, 0:ts_], g[:, d * 2 + 1, :], y[:, t0:t0 + ts_])
                    av = aa[d][:, t0:t0 + ts_]
                    p.act(a1[:, 0:ts_], br[:, 0:ts_], AF.Sigmoid, bias=gbr)
                    p.act(a2[:, 0:ts_], bi[:, 0:ts_], AF.Sigmoid, bias=gbi)
                    p.act(av, a1[:, 0:ts_], AF.Exp, scale=c8d)
                    p.tt(a1[:, 0:ts_], av, av, ALU.mult)
                    p.ts(a1[:, 0:ts_], a1[:, 0:ts_], -1.0, 1.0, ALU.mult, ALU.add)
                    p.act(a1[:, 0:ts_], a1[:, 0:ts_], AF.Sqrt)
                    p.tt(a2[:, 0:ts_], a2[:, 0:ts_], y[:, t0:t0 + ts_], ALU.mult, eng="gpsimd")
                    p.tt(bx[d][:, t0:t0 + ts_], a1[:, 0:ts_], a2[:, 0:ts_], ALU.mult)
            p.op("vector", lambda e: e.tensor_tensor_scan(out=hh[0][:], data0=aa[0][:], data1=bx[0][:], initial=0.0,
                                                          op0=ALU.mult, op1=ALU.add), reads=[aa[0], bx[0]], writes=[hh[0]])
            p.op("vector", lambda e: e.tensor_tensor_scan(out=hh[1][:, 0:TC][:, ::-1], data0=aa[1][:, 0:TC][:, ::-1],
                                                          data1=bx[1][:, 0:TC][:, ::-1], initial=0.0,
                                                          op0=ALU.mult, op1=ALU.add), reads=[aa[1], bx[1]], writes=[hh[1]])
            p.op("vector", lambda e: e.tensor_tensor_scan(out=hh[1][:, TC:TT][:, ::-1], data0=aa[1][:, TC:TT][:, ::-1],
                                                          data1=bx[1][:, TC:TT][:, ::-1], initial=hh[1][:, 0:1],
                                                          op0=ALU.mult, op1=ALU.add), reads=[aa[1], bx[1], hh[1]], writes=[hh[1]])
            p.tt(hh[0][:], hh[0][:], hh[1][:], ALU.add, eng="gpsimd")
            p.tt(y[:], z[:], z[:], ALU.mult)
            p.ts(y[:], y[:], 0.044715, 1.0, ALU.mult, ALU.add)
            p.tt(y[:], y[:], z[:], ALU.mult)
            p.act(y[:], y[:], AF.Sigmoid, scale=1.5957691216057308)
            p.tt(y[:], y[:], z[:], ALU.mult, eng="gpsimd")
            p.tt(yo[:], y[:], hh[0][:], ALU.mult)
            p.dma(moT[hd * 128:(hd + 1) * 128, :], yo[:], eng="gpsimd", writes=[("moT", hd)])
        p.emit()

    def phase_pool(self):
        TT, TC, T, GW, PG, DL = self.TT, self.TC, self.T, self.GW, self.PG, self.DL
        GH = T // GW
        pT = self.scr["pT"]
        p = self.prog()
        xs = [p.sb("px%d" % i, [128, TT]) for i in range(2)]
        inv = p.sb("inv", [128, TT])
        P1 = p.sb("P1", [128, GH, GW + 16]); P2 = p.sb("P2", [128, GH + 16, GW])
        Sa = p.sb("Sa", [128, GH * (GW + 16)]); Sb = p.sb("Sb", [128, GH * (GW + 16)])
        P3 = p.sb("P3", [128, TC + 16]); S3a = p.sb("S3a", [128, TC + 16]); S3b = p.sb("S3b", [128, TC + 16])
        dd = [p.sb("dd%d" % i, [128, TT]) for i in range(2)]
        db = p.sb("db", [128, TT], BF16)
        p.memset(P1[:], 0.0); p.memset(P2[:], 0.0, eng="gpsimd"); p.memset(P3[:], 0.0)

        def stages(src, tmpa, tmpb, dst, nst, ax, L):
            def sl(t, lo, hi):
                if len(t.shape) == 2:
                    return t[:, 8 + lo:8 + hi]
                if ax == 1:
                    return t[:, 8 + lo:8 + hi, :]
                return t[:, :, 8 + lo:8 + hi]
            cur = src
            shifts = [(-1, 0), (-1, 1), (-2, 2), (-4, 4)]
            rng = [(-7, L + 8), (-6, L + 7), (-4, L + 5), (0, L)]
            bufs = [tmpa, tmpb, tmpa, tmpb]
            for s_ in range(nst):
                lo, hi = rng[s_]
                if s_ == nst - 1:
                    lo, hi = 0, L
                    o = dst
                else:
                    o = sl(bufs[s_], lo, hi)
                a_, b_ = shifts[s_]
                p.tt(o, sl(cur, lo + a_, hi + a_), sl(cur, lo + b_, hi + b_), ALU.add)
                cur = bufs[s_]

        for g, w in enumerate(POOL_WINDOWS):
            nst = {2: 1, 4: 2, 8: 3, 16: 4}[w]
            self.load_bcast(p, inv[:], self.inp["invcnt"][g:g + 1, :])
            for c in range(PG // 128):
                idx = g * (PG // 128) + c
                x = xs[idx % 2]; d_ = dd[idx % 2]
                row = 2 * DL + g * PG + c * 128
                p.dma(x[:], pT[row:row + 128, :])
                p.copy(P1[:, :, 8:8 + GW], x[:, TC:TT].rearrange("p (a b) -> p a b", b=GW), eng="gpsimd")
                stages(P1[:], Sa[:].rearrange("p (a b) -> p a b", b=GW + 16), Sb[:].rearrange("p (a b) -> p a b", b=GW + 16),
                       P2[:, 8:8 + GH, :], nst, 2, GW)
                stages(P2[:], Sa[:].rearrange("p (a b) -> p a b", b=GW), Sb[:].rearrange("p (a b) -> p a b", b=GW),
                       d_[:, TC:TT].rearrange("p (a b) -> p a b", b=GW), nst, 1, GH)
                p.copy(P3[:, 8:8 + TC], x[:, 0:TC], eng="gpsimd")
                stages(P3[:], S3a[:], S3b[:], d_[:, 0:TC], nst, 1, TC)
                p.tt(d_[:], d_[:], inv[:], ALU.mult)
                p.tt(db[:], d_[:], x[:], ALU.subtract, eng="gpsimd")
                p.dma(self.scr["dT"][idx * 128:(idx + 1) * 128, :], db[:], eng="gpsimd", writes=[("dT", idx)])
        p.emit()

    def phase_post(self, l, which, tiles, with_router):
        D, KC, E = self.D, self.KC, self.E
        p = self.prog()
        banks = self.bank_list(p)
        ident = p.sb("ident", [128, 128]); p.dma(ident[:], self.inp["ident"])
        gseg = 2 if which == 0 else 5
        ysrc = self.scr["y"] if which == 0 else self.scr["m"]
        lg = p.sb("lg", [128, D]); lb = p.sb("lb", [128, D])
        self.load_bcast(p, lg[:], self.inp["ln_g"][l, which:which + 1, :])
        self.load_bcast(p, lb[:], self.inp["ln_b"][l, which:which + 1, :])
        gr = p.sb("gr", [128, D])
        hb = [p.sb("ph%d" % i, [128, D]) for i in range(2)]
        yb = [p.sb("py%d" % i, [128, D]) for i in range(2)]
        st = p.sb("st", [128, 8, 6]); mv = p.sb("mv", [128, 2]); rstd = p.sb("rstd", [128, 1]); nmr = p.sb("nmr", [128, 1])
        if with_router:
            s2 = p.sb("s2", [128, D]); h2 = p.sb("h2", [128, D])
            wr = p.sb("wr", [128, KC, E]); p.dma(wr[:], self.inp["router_w"][l].rearrange("(c p) e -> p c e", p=128))
            xt = p.sb("xt", [128, KC, 128])
            lgt = p.sb("lgt", [128, E]); mx = p.sb("mx", [128, 1]); sm = p.sb("sm", [128, 1])
            affT = p.sb("affT", [E, self.TT])
        n_ = 0
        for cnd in (1, 0):
            grp = [i for i in tiles if (1 if i < self.NTC else 0) == cnd]
            if not grp:
                continue
            self.load_bcast(p, gr[:], self.mod_row(l, cnd, gseg))
            if with_router:
                self.load_bcast(p, s2[:], self.mod_row(l, cnd, 4))
                p.ts(s2[:], s2[:], 1.0, None, ALU.add)
                self.load_bcast(p, h2[:], self.mod_row(l, cnd, 3))
            for i in grp:
                h = hb[n_ % 2]; y = yb[n_ % 2]
                p.dma(h[:], self.scr["h"][i * 128:(i + 1) * 128, :])
                p.dma(y[:], ysrc[i * 128:(i + 1) * 128, :])
                p.tt(y[:], y[:], gr[:], ALU.mult, eng="gpsimd")
                p.stt(y[:], h[:], DN_ALPHA, y[:], ALU.mult, ALU.add)
                self.ln_core(p, y[:], h[:], st, mv, rstd, nmr, "")
                p.tt(h[:], h[:], lg[:], ALU.mult, eng="gpsimd")
                p.tt(h[:], h[:], lb[:], ALU.add)
                p.dma(self.scr["h"][i * 128:(i + 1) * 128, :], h[:], eng="gpsimd")
                if with_router:
                    p.tt(y[:], h[:], s2[:], ALU.mult, eng="gpsimd")
                    p.tt(y[:], y[:], h2[:], ALU.add)
                    p.dma(self.scr["xm"][i * 128:(i + 1) * 128, :], y[:], eng="gpsimd", writes=[("xm", i)])
                    self.transpose_tile(p, y, xt, ident, banks[0:6], KC, bi0=n_)
                    lb_ = banks[6]
                    for c in range(KC):
                        p.mm(lb_[:, 0:E], xt[:, c, :], wr[:, c, :], start=(c == 0), stop=(c == KC - 1))
                    p.op("vector", lambda e, lb_=lb_: e.reduce_max(out=mx[:], in_=lb_[:, 0:E], axis=mybir.AxisListType.X),
                         reads=[lb_], writes=[mx])
                    p.ts(mx[:], mx[:], -1.0, None, ALU.mult)
                    p.act(lgt[:], lb_[:, 0:E], AF.Exp, bias=mx[:])
                    p.op("vector", lambda e: e.reduce_sum(out=sm[:], in_=lgt[:], axis=mybir.AxisListType.X), reads=[lgt], writes=[sm])
                    p.op("vector", lambda e: e.reciprocal(out=sm[:], in_=sm[:]), reads=[sm], writes=[sm])
                    p.ts(lgt[:], lgt[:], sm[:], None, ALU.mult)
                    p.dma(self.scr["aff"][i * 128:(i + 1) * 128, :], lgt[:], eng="gpsimd", writes=[("aff", i)])
                    tb = banks[7]
                    p.tr(tb[0:E, 0:128], lgt[:], ident[:])
                    p.copy(affT[:, i * 128:(i + 1) * 128], tb[0:E, 0:128])
                n_ += 1
        if with_router:
            p.dma(self.scr["affT"], affT[:], eng="gpsimd")
        p.emit()

    def phase_route(self, t0, n, cap, tag):
        E = self.E
        SC = (cap + 127) // 128
        p = self.prog()
        banks = self.bank_list(p)
        ident = p.sb("ident", [128, 128]); p.dma(ident[:], self.inp["ident"])
        a = p.sb("ra", [E, n]); w = [p.sb("rw%d" % i, [E, n]) for i in range(2)]
        m8 = p.sb("m8", [E, 8]); ones = p.sb("ones", [E, n]); cs = p.sb("cs", [E, n]); msk = p.sb("msk", [E, n])
        idxE = p.sb("idxE", [E, SC * 128])
        p.dma(a[:], self.scr["affT"][:, t0:t0 + n])
        p.memset(ones[:], 1.0, eng="gpsimd")
        cur = a
        for r in range(cap // 8):
            p.op("vector", lambda e, cur=cur: e.max(out=m8[:], in_=cur[:]), reads=[cur], writes=[m8])
            if r < cap // 8 - 1:
                nxt = w[r % 2]
                p.op("vector", lambda e, cur=cur, nxt=nxt: e.match_replace(out=nxt[:], in_to_replace=m8[:], in_values=cur[:],
                                                                          imm_value=-1.0), reads=[cur, m8], writes=[nxt])
                cur = nxt
        p.ts(msk[:], a[:], m8[:, 7:8], None, ALU.is_ge)
        p.op("vector", lambda e: e.tensor_tensor_scan(out=cs[:], data0=ones[:], data1=msk[:], initial=0.0,
                                                      op0=ALU.mult, op1=ALU.add), reads=[ones, msk], writes=[cs])
        junk = w[0]
        io = p.sb("io", [E, 512]); p.dma(io[:], self.inp["iota"][0:E, :])
        p.ts(io[:], io[:], -1.0, -0.5, ALU.mult, ALU.add)
        for s_ in range(SC * 128):
            p.op("scalar", lambda e, s_=s_: e.activation(out=junk[:], in_=cs[:], func=AF.Sign, bias=io[:, s_:s_ + 1], scale=1.0,
                                                         accum_out=idxE[:, s_:s_ + 1]),
                 reads=[cs, io], writes=[junk, (_key(idxE), s_)])
        p.op("vector", lambda e: e.tensor_scalar(out=idxE[:], in0=idxE[:], scalar1=-0.5, scalar2=float(n) / 2 + float(t0),
                                                 op0=ALU.mult, op1=ALU.add),
             reads=[idxE] + [(_key(idxE), s_) for s_ in range(SC * 128)], writes=[idxE])
        it_ = p.sb("it", [128, SC, E])
        for i in range(SC):
            b = banks[i % 8]
            p.tr(b[:, 0:E], idxE[:, i * 128:(i + 1) * 128], ident[0:E, 0:E])
            p.copy(it_[:, i, :], b[:, 0:E])
        p.dma(self.scr["idx_" + tag], it_[:], eng="gpsimd")
        p.emit()

    def phase_moe(self, l, t0, n, cap, tag):
        D, KC, E, DE, TT = self.D, self.KC, self.E, self.DE, self.TT
        nt = n // 128
        FC = DE // 128
        SC = (cap + 127) // 128
        CW = SC * 128
        xm, mT, aff = self.scr["xm"], self.scr["m"], self.scr["aff"]
        p = self.prog()
        banks = self.bank_list(p)
        ident = p.sb("ident", [128, 128]); p.dma(ident[:], self.inp["ident"])
        idf = p.sb("idf", [128, SC, E]); p.dma(idf[:], self.scr["idx_" + tag])
        idx = p.sb("idx", [128, SC, E], mybir.dt.int32)
        ixh = [p.sb("ixh%d" % h, [128, SC, E], mybir.dt.int32) for h in range(2)]
        idh = p.sb("idh", [128, SC, E])
        p.copy(idx[:], idf[:])
        for h in range(2):
            p.ts(idh[:], idf[:], 2.0, float(h), ALU.mult, ALU.add)
            p.copy(ixh[h][:], idh[:])
        HW = D // 2
        xm2 = xm.rearrange("t (h c) -> (t h) c", h=2)
        m2 = mT.rearrange("t (h c) -> (t h) c", h=2)
        xg = [p.sb("xg%d" % i, [128, D]) for i in range(1)]
        gs = [p.sb("gs%d" % i, [128, E]) for i in range(SC)]
        for t_ in xg + gs:
            p.memset(t_[:], 0.0, eng="gpsimd")
        for i in range(nt):
            p.dma(mT[t0 + i * 128:t0 + (i + 1) * 128, :], xg[0][:], eng="gpsimd", writes=["m"])
        xe = p.sb("xe", [128, KC, CW], BF16)
        aT = p.sb("aT", [128, FC, CW], BF16)
        wst = [p.sb("wst%d" % i, [128, KC, 128]) for i in range(2)]
        wgb = [p.sb("wgb%d" % i, [128, KC, 128], BF16) for i in range(2)]
        wds = [p.sb("wds%d" % i, [128, FC, 256]) for i in range(2)]
        wd = [p.sb("wd%d" % i, [128, FC, 256], BF16) for i in range(2)]
        sg = [p.sb("sg%d" % i, [128, CW]) for i in range(2)]
        ye = [p.sb("ye%d" % i, [128, D]) for i in range(SC)]
        bound = t0 + n - 1
        breg = {}

        def bnd(g, k, v):
            if k not in breg:
                breg[k] = g.to_reg(v)
            return breg[k]
        it = 0
        for e in range(E):
            for s in range(SC):
                x_ = xg[0]
                ix = idx[:, s, e:e + 1]
                for h in range(2):
                    ixh_ = ixh[h][:, s, e:e + 1]
                    p.op("gpsimd", lambda g, x_=x_, ixh_=ixh_, h=h: g.indirect_dma_start(
                        out=x_[:, h * HW:(h + 1) * HW], out_offset=None, in_=xm2[:, :],
                        in_offset=bass.IndirectOffsetOnAxis(ap=ixh_, axis=0),
                        bounds_check=bnd(g, "b2", 2 * bound + 1), oob_is_err=False), reads=[xm, ixh[h]], writes=[x_], dma=True)
                p.op("gpsimd", lambda g, g_=gs[s], ix=ix: g.indirect_dma_start(
                    out=g_[:, :], out_offset=None, in_=aff[:, :], in_offset=bass.IndirectOffsetOnAxis(ap=ix, axis=0),
                    bounds_check=bnd(g, "b1", bound), oob_is_err=False), reads=[aff, idx], writes=[gs[s]], dma=True)
                for c0 in range(0, KC, 4):
                    nb_ = min(4, KC - c0)
                    bk = banks[it % 4]
                    it += 1
                    for j in range(nb_):
                        p.tr(bk[:, j * 128:(j + 1) * 128], x_[:, (c0 + j) * 128:(c0 + j + 1) * 128], ident[:])
                    p.copy(xe[:, c0:c0 + nb_, s * 128:(s + 1) * 128], bk[:, 0:nb_ * 128].rearrange("p (a t) -> p a t", t=128),
                           eng="vector" if it % 2 == 0 else "scalar")
            def load_gu(k):
                f, part = k // 2, k % 2
                col = part * DE + f * 128
                p.dma(wst[k % 2][:], self.inp["expert_w_gu"][l, e, :, col:col + 128].rearrange("(c p) m -> p c m", p=128))
                p.copy(wgb[k % 2][:], wst[k % 2][:], eng="gpsimd")
            load_gu(0)
            for k in range(2 * FC):
                if k + 1 < 2 * FC:
                    load_gu(k + 1)
                f, part = k // 2, k % 2
                bk_ = banks[4 + (f % 2) * 2 + part]
                for c in range(KC):
                    p.mm(bk_[:, 0:CW], wgb[k % 2][:, c, :], xe[:, c, :], start=(c == 0), stop=(c == KC - 1))
                if part == 1:
                    bg = banks[4 + (f % 2) * 2]; bu = banks[5 + (f % 2) * 2]
                    s_ = sg[f % 2]
                    p.act(s_[:], bg[:, 0:CW], AF.Silu)
                    p.tt(aT[:, f, :], s_[:], bu[:, 0:CW], ALU.mult)
            def load_d(nb):
                p.dma(wds[nb % 2][:], self.inp["expert_w_down"][l, e, :, nb * 256:(nb + 1) * 256].rearrange("(c p) n -> p c n", p=128))
                p.copy(wd[nb % 2][:], wds[nb % 2][:], eng="gpsimd")
            load_d(0)
            for nb in range(D // 256):
                if nb + 1 < D // 256:
                    load_d(nb + 1)
                for s in range(SC):
                    bk = banks[it % 4]
                    it += 1
                    for f in range(FC):
                        p.mm(bk[:, 0:256], aT[:, f, s * 128:(s + 1) * 128], wd[nb % 2][:, f, :], start=(f == 0), stop=(f == FC - 1))
                    p.ts(ye[s][:, nb * 256:(nb + 1) * 256], bk[:, 0:256], gs[s][:, e:e + 1], None, ALU.mult,
                         eng="vector")
            for s in range(SC):
                for h in range(2):
                    ixh_ = ixh[h][:, s, e:e + 1]
                    p.op("gpsimd", lambda g, y_=ye[s], ixh_=ixh_, h=h: g.indirect_dma_start(
                        out=m2[:, :], out_offset=bass.IndirectOffsetOnAxis(ap=ixh_, axis=0), in_=y_[:, h * HW:(h + 1) * HW],
                        in_offset=None, compute_op=ALU.add), reads=[ye[s], ixh[h], "m"], writes=["m"], dma=True)
        p.emit()

    def phase_out(self):
        p = self.prog()
        hb = [p.sb("fo%d" % i, [128, self.D]) for i in range(2)]
        for n_, i in enumerate(range(self.NTC, self.NT)):
            h = hb[n_ % 2]
            p.dma(h[:], self.scr["h"][i * 128:(i + 1) * 128, :])
            p.dma(self.out[(i - self.NTC) * 128:(i - self.NTC + 1) * 128, :], h[:], eng="gpsimd")
        p.emit()

    def phase_mlstm_gates(self):
        MH, TT, TC, NT, NTC = self.MH, self.TT, self.TC, self.NT, self.NTC
        p = self.prog()
        banks = self.bank_list(p)
        ident = p.sb("ident", [128, 128]); p.dma(ident[:], self.inp["ident"])
        gb = p.sb("gb", [MH, 4]); p.dma(gb[:], self.inp["ogb"])
        ones = p.sb("ones", [MH, TT])
        zc = p.sb("zc", [MH, 1])
        p.memset(ones[:], 1.0); p.memset(zc[:], 0.0)
        ig = p.sb("ig", [MH, TT]); fg = p.sb("fg", [MH, TT])
        Fc = p.sb("Fc", [MH, TT]); R = p.sb("R", [MH, TT])
        arr = [p.sb("q%d" % k, [MH, TT]) for k in range(5)] + [fg]
        dsc = p.sb("dsc", [MH, NT]); nre = p.sb("nre", [MH, NT])
        S = p.sb("S", [128, NT, 6, MH])
        for d in range(2):
            p.dma(ig[:], self.scr["gT"][(d * 2) * MH:(d * 2 + 1) * MH, :])
            p.dma(fg[:], self.scr["gT"][(d * 2 + 1) * MH:(d * 2 + 2) * MH, :])
            p.ts(ig[:], ig[:], gb[:, d * 2:d * 2 + 1], None, ALU.add)
            p.ts(fg[:], fg[:], gb[:, d * 2 + 1:d * 2 + 2], None, ALU.add)
            p.act(fg[:], fg[:], AF.Exp, scale=-1.0)
            p.ts(fg[:], fg[:], 1.0, None, ALU.add)
            p.act(fg[:], fg[:], AF.Ln)
            p.ts(fg[:], fg[:], -1.0, None, ALU.mult)

            def scan(out, d1, op1, segs):
                prev = None
                for (s0, s1) in segs:
                    def v(t):
                        x = t[:, s0:s1]
                        return x[:, ::-1] if d == 1 else x
                    init = 0.0 if prev is None else prev
                    p.op("vector", lambda e, o=v(out), a_=v(ones), b_=v(d1), init=init: e.tensor_tensor_scan(
                        out=o, data0=a_, data1=b_, initial=init, op0=ALU.mult, op1=op1),
                        reads=[ones, d1, out], writes=[out])
                    prev = out[:, s1 - 1:s1] if d == 0 else out[:, s0:s0 + 1]
            segs = [(0, TC), (TC, TT)]
            scan(Fc, fg, ALU.add, segs)
            p.tt(ig[:], ig[:], Fc[:], ALU.subtract)
            scan(R, ig, ALU.max, segs)
            p.tt(fg[:], Fc[:], R[:], ALU.add)
            p.act(fg[:], fg[:], AF.Exp, scale=-1.0)
            order = list(range(NT)) if d == 0 else (list(range(NTC - 1, -1, -1)) + list(range(NT - 1, NTC - 1, -1)))
            prev_idx = None
            for c in order:
                ch = slice(c * 128, (c + 1) * 128)
                ei = c * 128 + 127 if d == 0 else c * 128
                Fp = zc[:] if prev_idx is None else Fc[:, prev_idx:prev_idx + 1]
                Rp = zc[:] if prev_idx is None else R[:, prev_idx:prev_idx + 1]
                p.ts(nre[:, c:c + 1], R[:, ei:ei + 1], -1.0, None, ALU.mult)
                p.tt(dsc[:, c:c + 1], Rp, R[:, ei:ei + 1], ALU.subtract)
                p.ts(arr[0][:, ch], R[:, ch], Fp, None, ALU.add)
                p.act(arr[0][:, ch], arr[0][:, ch], AF.Exp, scale=-1.0)
                p.act(arr[1][:, ch], ig[:, ch], AF.Exp, bias=Fp)
                p.act(arr[2][:, ch], R[:, ch], AF.Exp, scale=-1.0, bias=Rp)
                p.act(arr[3][:, ch], ig[:, ch], AF.Exp, bias=nre[:, c:c + 1])
                p.act(arr[4][:, ch], R[:, ch], AF.Exp, scale=0.0, bias=dsc[:, c:c + 1])
                prev_idx = ei
            for c in range(NT):
                b = banks[c % 8]
                for k in range(6):
                    p.tr(b[:, k * MH:(k + 1) * MH], arr[k][:, c * 128:(c + 1) * 128], ident[0:MH, 0:MH])
                p.copy(S[:, c, :, :], b[:, 0:6 * MH].rearrange("p (k h) -> p k h", h=MH), eng="vector" if c % 2 == 0 else "scalar")
            p.dma(self.scr["mS"][d], S[:], eng="gpsimd")
        p.emit()

    def phase_mlstm(self):
        MH, TT, TC, NT, NTC, DV, DQK, QK, D = self.MH, self.TT, self.TC, self.NT, self.NTC, self.DV, self.DQK, self.QK, self.D
        NLC = NT - NTC
        DC = DQK // 128
        qkT, kvo, moT = self.scr["qkT"], self.scr["kvo"], self.scr["moT"]
        p = self.prog()
        pm = [p.ps("pm%d" % i, [128, 512]) for i in range(8)]
        ps_aD = [pm[0][:, 0:DV], pm[1][:, 0:DV]]
        ps_bD = [pm[2][:, 0:DV], pm[3][:, 0:DV]]
        ps_sD = [pm[4][:, 0:128], pm[5][:, 0:128]]
        ps_dD = [pm[4][:, 128:136], pm[5][:, 128:136]]
        ps_u = [pm[6], pm[7]]
        ps_uD = None
        ps_t = pm[6]
        ident = p.sb("ident", [128, 128]); p.dma(ident[:], self.inp["ident"])
        mask = p.sb("mask", [128, 2, 128]); p.dma(mask[:], self.inp["mmask"])
        S = [p.sb("S%d" % d, [128, NT, 6, MH]) for d in range(2)]
        for d in range(2):
            p.dma(S[d][:], self.scr["mS"][d])
        ng = p.sb("ng", [128, DV])
        hbuf = p.sb("hbuf", [128, NLC, DV])
        CstD = [p.sb("Cst%d" % d, [128, DC, DV + 1]) for d in range(2)]
        NB = 4
        qT = [p.sb("qT%d" % i, [128, DC, 128]) for i in range(NB)]
        kT = [p.sb("kT%d" % i, [128, DC, 128]) for i in range(NB)]
        kt = [p.sb("kt%d" % i, [128, DQK]) for i in range(NB)]
        vt = [p.sb("vt%d" % i, [128, DV + 1]) for i in range(NB)]
        for i in range(NB):
            p.memset(vt[i][:, DV:DV + 1], 1.0)
        S0 = [p.sb("S0_%d" % i, [128, 128]) for i in range(2)]
        kw = [p.sb("kw%d" % i, [128, DQK]) for i in range(2)]
        n1 = [p.sb("n1_%d" % i, [128, DV]) for i in range(2)]
        n2 = [p.sb("n2_%d" % i, [128, DV]) for i in range(2)]
        sm = [p.sb("sm%d" % i, [128, 4]) for i in range(2)]
        ot = [p.sb("ot%d" % i, [128, DV]) for i in range(2)]
        st = p.sb("st", [128, 8, 6]); mv = p.sb("mv", [128, 2]); rstd = p.sb("rstd", [128, 1]); nmr = p.sb("nmr", [128, 1])
        tT = [p.sb("tT%d" % i, [128, DV // 128, 128], BF16) for i in range(2)]
        it = 0
        for hd in range(MH):
            self.load_bcast(p, ng[:], self.inp["odd_norm_g"][0:1, hd * DV:(hd + 1) * DV])
            p.memset(hbuf[:], 0.0, eng="gpsimd")
            orders = [list(range(NT)), list(range(NTC - 1, -1, -1)) + list(range(NT - 1, NTC - 1, -1))]
            for d in range(2):
                p.memset(CstD[d][:], 0.0)
            for step in range(NT):
                for d in range(2):
                    Cst = CstD[d]
                    c = orders[d][step]
                    j = it % NB
                    it += 1
                    cs_ = slice(c * 128, (c + 1) * 128)
                    sc = lambda k, d=d, c=c: S[d][:, c, k, hd:hd + 1]
                    lat = c >= NTC
                    p.dma(kt[j][:], kvo[cs_, hd * DQK:(hd + 1) * DQK])
                    p.dma(vt[j][:, 0:DV], kvo[cs_, QK + hd * DV:QK + (hd + 1) * DV])
                    pss, psa, psb, psd = ps_sD[d], ps_aD[d], ps_bD[d], ps_dD[d]
                    if lat:
                        p.dma(qT[j][:], qkT[hd * DQK:(hd + 1) * DQK, cs_].rearrange("(c p) t -> p c t", p=128))
                        p.dma(kT[j][:], qkT[QK + hd * DQK:QK + (hd + 1) * DQK, cs_].rearrange("(c p) t -> p c t", p=128))
                        for dc in range(DC):
                            p.mm(pss, kT[j][:, dc, :], qT[j][:, dc, :], start=(dc == 0), stop=(dc == DC - 1))
                        s0 = S0[it % 2]
                        p.stt(s0[:], pss, sc(1), mask[:, d, :], ALU.mult, ALU.mult)
                        p.mm(psa, s0[:], vt[j][:, 0:DV])
                        p.mm(psd[:, 0:1], s0[:], vt[j][:, DV:DV + 1])
                        for dc in range(DC):
                            p.mm(psb, qT[j][:, dc, :], Cst[:, dc, 0:DV], start=(dc == 0), stop=(dc == DC - 1))
                        for dc in range(DC):
                            p.mm(psd[:, 1:2], qT[j][:, dc, :], Cst[:, dc, DV:DV + 1], start=(dc == 0), stop=(dc == DC - 1))
                        a1 = n1[it % 2]; a2 = n2[it % 2]; q_ = sm[it % 2]
                        p.act(a1[:], psa, AF.Copy, scale=sc(0))
                        p.stt(a2[:], psb, sc(2), a1[:], ALU.mult, ALU.add)
                        p.ts(q_[:, 0:1], psd[:, 0:1], sc(0), None, ALU.mult)
                        p.stt(q_[:, 1:2], psd[:, 1:2], sc(2), q_[:, 0:1], ALU.mult, ALU.add)
                        p.ts(q_[:, 2:3], q_[:, 1:2], -1.0, None, ALU.mult)
                        p.tt(q_[:, 2:3], q_[:, 2:3], q_[:, 1:2], ALU.max)
                        p.ts(q_[:, 2:3], q_[:, 2:3], sc(5), None, ALU.max)
                        p.op("vector", lambda e, q_=q_: e.reciprocal(out=q_[:, 3:4], in_=q_[:, 2:3]), reads=[q_], writes=[q_])
                        hv = hbuf[:, c - NTC, :]
                        p.stt(hv, a2[:], q_[:, 3:4], hv, ALU.mult, ALU.add)
                    k_ = kw[it % 2]
                    p.ts(k_[:], kt[j][:], sc(3), None, ALU.mult, eng="gpsimd")
                    for dc in range(DC):
                        pu = ps_uD[d][:, dc * DV:(dc + 1) * DV] if False else ps_u[(2 * d + dc) % len(ps_u)][:, 0:DV]
                        p.mm(pu, k_[:, dc * 128:(dc + 1) * 128], vt[j][:, 0:DV])
                        p.mm(psd[:, 2 + dc:3 + dc], k_[:, dc * 128:(dc + 1) * 128], vt[j][:, DV:DV + 1])
                        p.stt(Cst[:, dc, 0:DV], Cst[:, dc, 0:DV], sc(4), pu, ALU.mult, ALU.add)
                        p.stt(Cst[:, dc, DV:DV + 1], Cst[:, dc, DV:DV + 1], sc(4), psd[:, 2 + dc:3 + dc], ALU.mult, ALU.add)
            for c in range(NLC):
                i = c + NTC
                o_ = ot[c % 2]; t_ = tT[c % 2]; a1 = n1[c % 2]
                p.dma(o_[:], kvo[i * 128:(i + 1) * 128, QK + D + hd * DV:QK + D + (hd + 1) * DV])
                p.act(o_[:], o_[:], AF.Sigmoid)
                self.ln_core(p, hbuf[:, c, :], a1[:], st, mv, rstd, nmr, "")
                p.tt(a1[:], a1[:], ng[:], ALU.mult, eng="gpsimd")
                p.tt(a1[:], a1[:], o_[:], ALU.mult)
                for b_ in range(DV // 128):
                    p.tr(ps_t[:, b_ * 128:(b_ + 1) * 128], a1[:, b_ * 128:(b_ + 1) * 128], ident[:])
                p.copy(t_[:], ps_t[:, 0:DV].rearrange("p (a t) -> p a t", t=128), eng="scalar")
                p.dma(moT[hd * DV:(hd + 1) * DV, i * 128:(i + 1) * 128].rearrange("(a p) t -> p a t", p=128), t_[:],
                      eng="gpsimd", writes=[("moT", hd, c)])
        p.emit()

    def build(self):
        D, TT, T, TC, KC, E, MH = self.D, self.TT, self.T, self.TC, self.KC, self.E, self.MH
        QK, DL, DP, PG, LH = self.QK, self.DL, self.DP, self.PG, self.LH
        nc = self.nc
        self.din("xin", [TT, D]); self.din("condT", [128, KC, 2])
        self.din("ada_w", [2, D, 6 * D]); self.din("ada_b", [2, 6 * D])
        self.din("ln_g", [2, 2, D]); self.din("ln_b", [2, 2, D])
        self.din("even_w_in", [D, 2 * DL + DP]); self.din("lru_gate_w", [2, 2, LH, 128, 128])
        self.din("ppar", [128, 11 * LH + DP // 128]); self.din("pool_w", [4, PG, PG]); self.din("even_w_out", [D, D])
        self.din("odd_w_in", [D, 2 * QK + 2 * D + 4 * MH]); self.din("ogb", [MH, 4]); self.din("odd_norm_g", [1, D])
        self.din("odd_w_out", [D, D]); self.din("router_w", [2, D, E])
        self.din("expert_w_gu", [2, E, D, 2 * self.DE]); self.din("expert_w_down", [2, E, self.DE, D])
        self.din("ident", [128, 128]); self.din("iota", [128, 512]); self.din("pidx", [128, 4])
        self.din("invcnt", [4, TT]); self.din("mmask", [128, 2, 128])
        self.out = nc.dram_tensor("out", [T, D], F32, kind="ExternalOutput").ap()
        self.dscr("mods", [2, 2, 6 * D]); self.dscr("h", [TT, D]); self.dscr("uT", [D, TT], BF16)
        self.dscr("pT", [max(2 * DL + DP, 2 * QK), TT]); self.dscr("moT", [D, TT], BF16); self.dscr("dT", [DP, TT], BF16)
        self.dscr("y", [TT, D]); self.dscr("xm", [TT, D]); self.dscr("aff", [TT, E]); self.dscr("affT", [E, TT])
        self.dscr("idx_l", [128, (self.CAP + 127) // 128, E]); self.dscr("idx_c", [128, (self.CAPC + 127) // 128, E]); self.dscr("m", [TT, D])
        self.dscr("gT", [4 * MH, TT]); self.dscr("kvo", [TT, QK + 2 * D]); self.dscr("mS", [2, 128, self.NT, 6, MH])
        self.scr["qkT"] = self.scr["pT"]
        self.sy = Sync(nc)
        alltiles = list(range(self.NT)); lattiles = list(range(self.NTC, self.NT))
        self.phase_adaln()
        self.phase_ln0()
        self.phase_u(0)
        self.gemm_fm(self.inp["even_w_in"], self.scr["uT"], self.scr["pT"], D, 2 * DL + DP, self.tcols())
        self.phase_lru()
        self.phase_pool()
        for g in range(4):
            self.gemm_fm(self.inp["pool_w"][g], self.scr["dT"][g * PG:(g + 1) * PG, :], self.scr["moT"][DL + g * PG:DL + (g + 1) * PG, :],
                         PG, PG, self.tcols(), row_scale=self.inp["ppar"][:, 11 * LH + g * (PG // 128):11 * LH + (g + 1) * (PG // 128)], odt=BF16)
        self.gemm_tm(self.scr["moT"], self.inp["even_w_out"], self.scr["y"], D, D, alltiles)
        self.phase_post(0, 0, alltiles, True)
        self.phase_route(TC, T, self.CAP, "l")
        self.phase_route(0, TC, self.CAPC, "c")
        self.phase_moe(0, TC, T, self.CAP, "l")
        self.phase_moe(0, 0, TC, self.CAPC, "c")
        self.phase_post(0, 1, alltiles, False)
        self.phase_u(1)
        wi = self.inp["odd_w_in"]
        self.gemm_fm(wi[:, 0:QK], self.scr["uT"], self.scr["qkT"][0:QK, :], D, QK, self.tcols(lat_only=True), scale=float(self.DQK) ** -0.5)
        self.gemm_fm(wi[:, QK:2 * QK], self.scr["uT"], self.scr["qkT"][QK:2 * QK, :], D, QK, self.tcols(lat_only=True))
        self.gemm_fm(wi[:, 2 * QK + 2 * D:2 * QK + 2 * D + 4 * MH], self.scr["uT"], self.scr["gT"], D, 4 * MH, self.tcols())
        self.gemm_tm(self.scr["uT"], wi[:, QK:2 * QK + 2 * D], self.scr["kvo"], D, QK + 2 * D, alltiles)
        self.phase_mlstm_gates()
        self.phase_mlstm()
        self.gemm_tm(self.scr["moT"], self.inp["odd_w_out"], self.scr["y"], D, D, lattiles)
        self.phase_post(1, 0, lattiles, True)
        self.phase_route(TC, T, self.CAP, "l")
        self.phase_moe(1, TC, T, self.CAP, "l")
        self.phase_post(1, 1, lattiles, False)
        self.phase_out()
        self.sy.close()
        return nc


def host_inputs(cfg, b, x, c, ctx, c_ctx, ada_w, ada_b, ln_g, ln_b, even_w_in, even_conv_w, even_conv_b, lru_gate_w,
                lru_gate_b, lru_lambda, pool_w, pool_scale, even_w_out, odd_w_in, odd_gate_b, odd_norm_g, odd_w_out,
                router_w, expert_w_gu, expert_w_down):
    D, T, TC, GW, MH = cfg["D"], cfg["T"], cfg["TC"], cfg["GW"], cfg["MH"]
    KC = D // 128
    LH = D // 256
    f = lambda a: np.ascontiguousarray(np.asarray(a, dtype=np.float32))
    m = {}
    m["xin"] = f(np.concatenate([ctx[b], x[b]], axis=0))
    m["condT"] = f(np.stack([c[b], c_ctx], axis=-1).reshape(KC, 128, 2).transpose(1, 0, 2))
    m["ada_w"] = f(ada_w); m["ada_b"] = f(ada_b); m["ln_g"] = f(ln_g); m["ln_b"] = f(ln_b)
    m["even_w_in"] = f(even_w_in[0]); m["lru_gate_w"] = f(lru_gate_w[0])
    cw = np.asarray(even_conv_w[0]).reshape(4, LH, 128).transpose(2, 1, 0).reshape(128, LH * 4)
    cb = np.asarray(even_conv_b[0]).reshape(LH, 128).T
    gb = np.asarray(lru_gate_b[0]).reshape(2, 2, LH, 128).transpose(3, 2, 0, 1).reshape(128, LH * 4)
    lam = np.asarray(lru_lambda[0]).reshape(2, LH, 128).transpose(2, 1, 0).reshape(128, LH * 2)
    psc = np.asarray(pool_scale[0]).reshape(-1, 128).T
    m["ppar"] = f(np.concatenate([cw, cb, gb, lam, psc], axis=1))
    m["pool_w"] = f(pool_w[0]); m["even_w_out"] = f(even_w_out[0])
    m["odd_w_in"] = f(odd_w_in[0])
    m["ogb"] = f(np.asarray(odd_gate_b[0]).reshape(2, 2, MH).transpose(2, 0, 1).reshape(MH, 4))
    m["odd_norm_g"] = f(odd_norm_g); m["odd_w_out"] = f(odd_w_out[0]); m["router_w"] = f(router_w)
    m["expert_w_gu"] = f(expert_w_gu); m["expert_w_down"] = f(expert_w_down)
    m["ident"] = np.eye(128, dtype=np.float32)
    m["iota"] = f(np.tile(np.arange(512, dtype=np.float32)[None, :], (128, 1)))
    m["pidx"] = f(np.arange(128, dtype=np.float32)[:, None] + 128.0 * np.arange(4, dtype=np.float32)[None, :])
    inv = np.zeros((4, TC + T), np.float32)
    GH = T // GW

    def cnt(L, w):
        pos = np.arange(L)
        lo = np.clip(pos - w // 2, 0, L); hi = np.clip(pos + w - w // 2, 0, L)
        return (hi - lo).astype(np.float64)
    for g, w in enumerate(POOL_WINDOWS):
        inv[g, :TC] = 1.0 / cnt(TC, w)
        inv[g, TC:] = (1.0 / (cnt(GH, w)[:, None] * cnt(GW, w)[None, :])).reshape(-1)
    m["invcnt"] = inv
    s_, t_ = np.meshgrid(np.arange(128), np.arange(128), indexing="ij")
    m["mmask"] = f(np.stack([(s_ <= t_), (s_ >= t_)], axis=1).astype(np.float32))
    return m


_NC_CACHE = {}


def run(cfg, inputs, n_samples=2):
    key = tuple(sorted((k, v) for k, v in cfg.items() if k != 'debug')) + (tuple(cfg.get('debug', ())),)
    if key not in _NC_CACHE:
        _NC_CACHE[key] = Builder(cfg).build()
    nc = _NC_CACHE[key]
    in_maps = [host_inputs(cfg, b, **inputs) for b in range(n_samples)]
    res = run_bass_kernel_spmd(nc, in_maps, core_ids=list(range(n_samples)))
    return np.stack([np.asarray(r["out"]) for r in res.results], axis=0).astype(np.float32)


def kernel(**inputs):
    inputs = {k: np.asarray(v) for k, v in inputs.items()}
    return run(CFG_FULL, inputs, n_samples=2)
```

```python
import numpy as np
import concourse.bass as bass
import concourse.mybir as mybir
from concourse.bass_utils import run_bass_kernel_spmd

F32 = mybir.dt.float32
BF16 = mybir.dt.bfloat16
AF = mybir.ActivationFunctionType
ALU = mybir.AluOpType
ENGS = ("tensor", "vector", "scalar", "gpsimd", "sync")
N_DMA_SEMS = 40


def _key(x):
    if isinstance(x, (str, tuple)):
        return x
    t = getattr(x, "tensor", x)
    return t.name


class Sync:
    def __init__(self, nc):
        self.guards = []
        self.esem = {}
        for e in ENGS:
            g = nc.semaphore("c_" + e)
            self.esem[e] = g.__enter__()
            self.guards.append(g)
        self.dsem = []
        for j in range(N_DMA_SEMS):
            g = nc.semaphore("d_%d" % j)
            self.dsem.append(g.__enter__())
            self.guards.append(g)
        self.ecount = {e: 0 for e in ENGS}
        self.ndma = 0

    def close(self):
        for g in reversed(self.guards):
            g.__exit__(None, None, None)


class Prog:
    def __init__(self, nc, sy):
        self.nc = nc
        self.sy = sy
        self.ops = []
        self.stack = []
        sy.phase = getattr(sy, "phase", 0) + 1
        self.pfx = "p%d_" % sy.phase

    def sb(self, name, shape, dt=F32):
        g = self.nc.sbuf_tensor(self.pfx + name, list(shape), dt)
        t = g.__enter__()
        self.stack.append(g)
        return t

    def ps(self, name, shape, dt=F32):
        g = self.nc.psum_tensor(self.pfx + name, list(shape), dt)
        t = g.__enter__()
        self.stack.append(g)
        return t

    def op(self, eng, fn, reads=(), writes=(), dma=False):
        self.ops.append(dict(eng=eng, fn=fn, r=[_key(k) for k in reads], w=[_key(k) for k in writes], dma=dma))

    def dma(self, out, in_, eng="sync", reads=None, writes=None, **kw):
        self.op(eng, lambda e: e.dma_start(out=out, in_=in_, **kw),
                reads=[in_] if reads is None else reads, writes=[out] if writes is None else writes, dma=True)

    def mm(self, out, lhsT, rhs, start=True, stop=True, reads=None, writes=None):
        self.op("tensor", lambda e: e.matmul(out, lhsT, rhs, start=start, stop=stop),
                reads=[lhsT, rhs] if reads is None else reads, writes=[out] if writes is None else writes)

    def tr(self, out, in_, ident):
        self.op("tensor", lambda e: e.transpose(out, in_, ident), reads=[in_, ident], writes=[out])

    def act(self, out, in_, func, bias=None, scale=1.0, eng="scalar"):
        r = [in_] + [x for x in (bias, scale) if not isinstance(x, (int, float, type(None)))]
        kw = {}
        if bias is not None:
            kw["bias"] = bias
        self.op(eng, lambda e: e.activation(out=out, in_=in_, func=func, scale=scale, **kw), reads=r, writes=[out])

    def tt(self, out, in0, in1, op, eng="vector"):
        self.op(eng, lambda e: e.tensor_tensor(out=out, in0=in0, in1=in1, op=op), reads=[in0, in1], writes=[out])

    def ts(self, out, in0, s1, s2, op0, op1=ALU.bypass, eng="vector"):
        r = [in0] + [x for x in (s1, s2) if not isinstance(x, (int, float, type(None)))]
        self.op(eng, lambda e: e.tensor_scalar(out=out, in0=in0, scalar1=s1, scalar2=s2, op0=op0, op1=op1),
                reads=r, writes=[out])

    def stt(self, out, in0, scalar, in1, op0, op1):
        r = [in0, in1] + ([scalar] if not isinstance(scalar, (int, float)) else [])
        self.op("vector", lambda e: e.scalar_tensor_tensor(out=out, in0=in0, scalar=scalar, in1=in1, op0=op0, op1=op1),
                reads=r, writes=[out])

    def copy(self, out, in_, eng="vector"):
        if eng == "scalar":
            self.op(eng, lambda e: e.copy(out=out, in_=in_), reads=[in_], writes=[out])
        else:
            self.op(eng, lambda e: e.tensor_copy(out=out, in_=in_), reads=[in_], writes=[out])

    def memset(self, ap, val, eng="vector"):
        self.op(eng, lambda e: e.memset(ap, val), reads=[], writes=[ap])

    def emit(self):
        nc, sy, ops = self.nc, self.sy, self.ops
        wr, rd = {}, {}
        nsem = N_DMA_SEMS
        for i, o in enumerate(ops):
            deps = set()
            for k in o["r"]:
                deps.update(wr.get(k, ()))
            for k in o["w"]:
                deps.update(wr.get(k, ()))
                deps.update(rd.get(k, ()))
            deps.discard(i)
            o["deps"] = deps
            for k in o["w"]:
                if rd.get(k):
                    wr[k] = [i]
                    rd[k] = []
                else:
                    wr.setdefault(k, []).append(i)
            for k in o["r"]:
                rd.setdefault(k, []).append(i)
            if o["dma"]:
                o["dma_idx"] = sy.ndma
                sy.ndma += 1
            else:
                sy.ecount[o["eng"]] += 1
                o["eidx"] = sy.ecount[o["eng"]]
        final_e = dict(sy.ecount)
        final_nd = sy.ndma
        esem, dsem = sy.esem, sy.dsem

        def gen(engname):
            def body(eng):
                known = {}

                def need(key, sem, val):
                    if val <= 0 or known.get(key, 0) >= val:
                        return
                    eng.wait_ge(sem, val)
                    known[key] = val

                for o in ops:
                    if o["eng"] != engname:
                        continue
                    req = {}
                    for d in o["deps"]:
                        p = ops[d]
                        if p["dma"]:
                            j = p["dma_idx"]
                            k_, s_, v_ = ("d", j % nsem), dsem[j % nsem], 16 * (j // nsem + 1)
                        else:
                            k_, s_, v_ = ("c", p["eng"]), esem[p["eng"]], p["eidx"]
                        if k_ not in req or req[k_][1] < v_:
                            req[k_] = (s_, v_)
                    for k_, (s_, v_) in req.items():
                        need(k_, s_, v_)
                    if o["dma"]:
                        j = o["dma_idx"]
                        if j >= nsem:
                            need(("d", j % nsem), dsem[j % nsem], 16 * (j // nsem))
                        o["fn"](eng).then_inc(dsem[j % nsem], 16)
                    else:
                        o["fn"](eng).then_inc(esem[engname], 1)
                for e2 in ENGS:
                    need(("c", e2), esem[e2], final_e[e2])
                for j in range(max(0, final_nd - nsem), final_nd):
                    need(("d", j % nsem), dsem[j % nsem], 16 * (j // nsem + 1))
            return body

        with nc.Block() as block:
            block.tensor(gen("tensor"))
            block.vector(gen("vector"))
            block.scalar(gen("scalar"))
            block.gpsimd(gen("gpsimd"))
            block.sync(gen("sync"))
        for g in reversed(self.stack):
            g.__exit__(None, None, None)
        self.stack = []


CFG_FULL = dict(D=4096, T=4096, TC=256, GW=64, MH=8, E=16)
POOL_WINDOWS = (2, 4, 8, 16)
LN_EPS = 1e-5
DEPTH = 2
DN_ALPHA = (2 * DEPTH) ** 0.25


class Builder:
    def __init__(self, cfg):
        self.cfg = cfg
        D, T, TC = cfg["D"], cfg["T"], cfg["TC"]
        self.D, self.T, self.TC, self.GW, self.MH, self.E = D, T, TC, cfg["GW"], cfg["MH"], cfg["E"]
        self.TT = T + TC
        self.KC = D // 128
        self.NT = self.TT // 128
        self.NTC = TC // 128
        self.DL = D // 2
        self.LH = self.DL // 128
        self.DP = D // 2
        self.PG = self.DP // 4
        self.DV = D // self.MH
        self.DQK = self.DV // 2
        self.QK = self.MH * self.DQK
        self.DE = D // 4
        self.CAP = 2 * T // self.E
        self.CAPC = 2 * TC // self.E
        self.nc = nc = bass.Bass("TRN2", target_bir_lowering=False)
        self.sy = None
        self.inp = {}
        self.scr = {}

    def din(self, name, shape):
        self.inp[name] = self.nc.dram_tensor(name, list(shape), F32, kind="ExternalInput").ap()
        return self.inp[name]

    def dscr(self, name, shape, dt=F32):
        kind = "ExternalOutput" if name in self.cfg.get("debug", ()) else "Internal"
        self.scr[name] = self.nc.dram_tensor(name, list(shape), dt, kind=kind).ap()
        return self.scr[name]

    def prog(self):
        return Prog(self.nc, self.sy)

    def bank_list(self, p, n=8):
        return [p.ps("bank%d" % i, [128, 512]) for i in range(n)]

    def ln_core(self, p, src, dst, st, mv, rstd, nmr, sfx, eng_apply="scalar"):
        D = src.shape[1]
        nch = D // 512 if D >= 512 else 1
        w = D // nch
        for c in range(nch):
            p.op("vector", lambda e, c=c: e.bn_stats(out=st[:, c, :], in_=src[:, c * w:(c + 1) * w]), reads=[src], writes=[st])
        p.op("vector", lambda e: e.bn_aggr(out=mv[:], in_=st[:, 0:nch, :].rearrange("p a b -> p (a b)")), reads=[st], writes=[mv])
        p.ts(rstd[:], mv[:, 1:2], LN_EPS, None, ALU.add)
        p.act(rstd[:], rstd[:], AF.Ln)
        p.act(rstd[:], rstd[:], AF.Exp, scale=-0.5)
        p.stt(nmr[:], mv[:, 0:1], -1.0, rstd[:], ALU.mult, ALU.mult)
        p.act(dst, src, AF.Identity, bias=nmr[:], scale=rstd[:])

    def transpose_tile(self, p, src, dst, ident, banks, nblk, bi0=0):
        g = 0
        b = 0
        while b < nblk:
            n = min(4, nblk - b)
            bank = banks[(bi0 + g) % len(banks)]
            for j in range(n):
                p.tr(bank[:, j * 128:(j + 1) * 128], src[:, (b + j) * 128:(b + j + 1) * 128], ident[:])
            o = dst[:, b:b + n, :]
            i_ = bank[:, 0:n * 128].rearrange("p (a t) -> p a t", t=128)
            if g % 2 == 0:
                p.copy(o, i_, eng="vector")
            else:
                p.copy(o, i_, eng="scalar")
            b += n
            g += 1

    def load_bcast(self, p, dst, row_ap):
        p.dma(dst, row_ap.partition_broadcast(128))

    def gemm_fm(self, W, inT, outT, K, N, tcols, scale=None, row_scale=None, TS=512, odt=F32):
        p = self.prog()
        kc = K // 128
        banks = self.bank_list(p)
        xin = [p.sb("gx%d" % i, [128, kc, TS], BF16) for i in range(2)]
        wst = [p.sb("gws%d" % i, [128, kc, 128]) for i in range(2)]
        wch = [p.sb("gw%d" % i, [128, kc, 128], BF16) for i in range(2)]
        ob = [p.sb("go%d" % i, [128, TS], odt) for i in range(3)]
        rs = None
        if row_scale is not None:
            rs = p.sb("grs", [128, (N + 127) // 128])
            p.dma(rs[:], row_scale, allow_slow_non_contiguous=True)
        NB = (N + 127) // 128
        items = [(ti, nb) for ti in range(len(tcols)) for nb in range(NB)]

        def load_x(ti):
            t0, tsz = tcols[ti]
            p.dma(xin[ti % 2][:, :, 0:tsz], inT[:, t0:t0 + tsz].rearrange("(c p) t -> p c t", p=128))

        def load_w(k):
            nb = items[k][1]
            m = min(128, N - nb * 128)
            p.dma(wst[k % 2][:, :, 0:m], W[:, nb * 128:nb * 128 + m].rearrange("(c p) m -> p c m", p=128))
            p.copy(wch[k % 2][:, :, 0:m], wst[k % 2][:, :, 0:m], eng="gpsimd")
        load_x(0)
        load_w(0)
        for k, (ti, nb) in enumerate(items):
            t0, tsz = tcols[ti]
            m = min(128, N - nb * 128)
            if k + 1 < len(items):
                load_w(k + 1)
            if nb == 0 and ti + 1 < len(tcols):
                load_x(ti + 1)
            x = xin[ti % 2]; w = wch[k % 2]
            bank = banks[k % 8]
            for c in range(kc):
                p.mm(bank[0:m, 0:tsz], w[:, c, 0:m], x[:, c, 0:tsz], start=(c == 0), stop=(c == kc - 1))
            o = ob[k % 3]
            if rs is not None:
                p.ts(o[0:m, 0:tsz], bank[0:m, 0:tsz], rs[0:m, nb:nb + 1], None, ALU.mult)
            elif scale is not None:
                p.act(o[0:m, 0:tsz], bank[0:m, 0:tsz], AF.Copy, scale=scale)
            elif k % 2 == 0:
                p.copy(o[0:m, 0:tsz], bank[0:m, 0:tsz], eng="vector")
            else:
                p.copy(o[0:m, 0:tsz], bank[0:m, 0:tsz], eng="scalar")
            p.dma(outT[nb * 128:nb * 128 + m, t0:t0 + tsz], o[0:m, 0:tsz], eng="scalar", writes=[(_key(outT), ti, nb)])
        p.emit()

    def gemm_tm(self, inT, W, out, K, N, tiles):
        p = self.prog()
        kc = K // 128
        banks = self.bank_list(p)
        wb = p.sb("hw", [128, kc, 512], BF16)
        wst = [p.sb("hws%d" % i, [128, 8, 512]) for i in range(2)]
        xin = [p.sb("hx%d" % i, [128, kc, 128], BF16) for i in range(2)]
        ob = [p.sb("ho%d" % i, [128, 512]) for i in range(3)]
        it = 0
        g_ = 0
        for nb in range((N + 511) // 512):
            n = min(512, N - nb * 512)
            for c0 in range(0, kc, 8):
                c1 = min(kc, c0 + 8)
                ws = wst[g_ % 2]
                p.dma(ws[:, 0:c1 - c0, 0:n], W[c0 * 128:c1 * 128, nb * 512:nb * 512 + n].rearrange("(c p) n -> p c n", p=128))
                p.copy(wb[:, c0:c1, 0:n], ws[:, 0:c1 - c0, 0:n], eng="gpsimd" if g_ % 2 == 0 else "vector")
                g_ += 1
            for q_, i in enumerate(tiles):
                if q_ == 0:
                    p.dma(xin[it % 2][:], inT[:, i * 128:(i + 1) * 128].rearrange("(c p) t -> p c t", p=128))
                if q_ + 1 < len(tiles):
                    i2 = tiles[q_ + 1]
                    p.dma(xin[(it + 1) % 2][:], inT[:, i2 * 128:(i2 + 1) * 128].rearrange("(c p) t -> p c t", p=128))
                x = xin[it % 2]
                bank = banks[it % 8]
                for c in range(kc):
                    p.mm(bank[:, 0:n], x[:, c, :], wb[:, c, 0:n], start=(c == 0), stop=(c == kc - 1))
                o = ob[it % 3]
                p.copy(o[:, 0:n], bank[:, 0:n], eng="vector" if it % 2 == 0 else "scalar")
                p.dma(out[i * 128:(i + 1) * 128, nb * 512:nb * 512 + n], o[:, 0:n], eng="scalar",
                      writes=[(_key(out), i, nb)])
                it += 1
        p.emit()

    def phase_adaln(self):
        D, KC = self.D, self.KC
        p = self.prog()
        banks = self.bank_list(p)
        cT = p.sb("cT", [128, KC, 2])
        sil = p.sb("sil", [128, KC, 2])
        p.dma(cT[:], self.inp["condT"])
        p.act(sil[:], cT[:], AF.Silu)
        NB = 2048
        wbuf = [p.sb("aw%d" % i, [128, NB]) for i in range(3)]
        bb = [p.sb("ab%d" % i, [2, NB]) for i in range(2)]
        obuf = [p.sb("ao%d" % i, [2, NB]) for i in range(2)]
        it = 0
        blk = 0
        for l in range(2):
            for nb in range(6 * D // NB):
                bset = banks[0:4] if blk % 2 == 0 else banks[4:8]
                b_ = bb[blk % 2]
                p.dma(b_[:], self.inp["ada_b"][l:l + 1, nb * NB:(nb + 1) * NB].partition_broadcast(2))
                for c in range(KC):
                    w = wbuf[it % 3]
                    it += 1
                    p.dma(w[:], self.inp["ada_w"][l, c * 128:(c + 1) * 128, nb * NB:(nb + 1) * NB])
                    for j in range(4):
                        p.mm(bset[j][0:2, :], sil[:, c, :], w[:, j * 512:(j + 1) * 512], start=(c == 0), stop=(c == KC - 1))
                o = obuf[blk % 2]
                for j in range(4):
                    p.tt(o[:, j * 512:(j + 1) * 512], bset[j][0:2, :], b_[:, j * 512:(j + 1) * 512], ALU.add)
                p.dma(self.scr["mods"][l, :, nb * NB:(nb + 1) * NB], o[:], eng="gpsimd", writes=[("mods", l, nb)])
                blk += 1
        p.emit()

    def mod_row(self, l, cnd, seg):
        D = self.D
        return self.scr["mods"][l, cnd:cnd + 1, seg * D:(seg + 1) * D]

    def phase_ln0(self):
        D = self.D
        p = self.prog()
        xb = [p.sb("xb%d" % i, [128, D]) for i in range(2)]
        hb = [p.sb("hb%d" % i, [128, D]) for i in range(2)]
        st = p.sb("st", [128, 8, 6]); mv = p.sb("mv", [128, 2]); rstd = p.sb("rstd", [128, 1]); nmr = p.sb("nmr", [128, 1])
        for i in range(self.NT):
            x = xb[i % 2]; h = hb[i % 2]
            p.dma(x[:], self.inp["xin"][i * 128:(i + 1) * 128, :])
            self.ln_core(p, x[:], h[:], st, mv, rstd, nmr, "")
            p.dma(self.scr["h"][i * 128:(i + 1) * 128, :], h[:], eng="gpsimd", writes=[("h", i)])
        p.emit()

    def phase_u(self, l):
        D, KC = self.D, self.KC
        p = self.prog()
        banks = self.bank_list(p)
        ident = p.sb("ident", [128, 128]); p.dma(ident[:], self.inp["ident"])
        rows = {}
        for cnd in range(2):
            sc = p.sb("sc%d" % cnd, [128, D]); sh = p.sb("sh%d" % cnd, [128, D])
            self.load_bcast(p, sc[:], self.mod_row(l, cnd, 1))
            self.load_bcast(p, sh[:], self.mod_row(l, cnd, 0))
            p.ts(sc[:], sc[:], 1.0, None, ALU.add)
            rows[cnd] = (sc, sh)
        hb = [p.sb("uh%d" % i, [128, D]) for i in range(2)]
        ub = [p.sb("uu%d" % i, [128, D]) for i in range(2)]
        ut = [p.sb("ut%d" % i, [128, KC, 128], BF16) for i in range(2)]
        for i in range(self.NT):
            cnd = 1 if i < self.NTC else 0
            sc, sh = rows[cnd]
            h = hb[i % 2]; u = ub[i % 2]; t = ut[i % 2]
            p.dma(h[:], self.scr["h"][i * 128:(i + 1) * 128, :])
            p.tt(u[:], h[:], sc[:], ALU.mult, eng="gpsimd")
            p.tt(u[:], u[:], sh[:], ALU.add, eng="vector")
            self.transpose_tile(p, u, t, ident, banks, KC, bi0=i * 3)
            p.dma(self.scr["uT"][:, i * 128:(i + 1) * 128].rearrange("(c p) t -> p c t", p=128), t[:], eng="gpsimd",
                  writes=[("uT", i)])
        p.emit()

    def tcols(self, TS=512, lat_only=False):
        out = []
        t = self.TC if lat_only else 0
        while t < self.TT:
            s = min(TS, self.TT - t)
            out.append((t, s))
            t += s
        return out

    def phase_lru(self):
        TT, TC, T, LH, DL = self.TT, self.TC, self.T, self.LH, self.DL
        pT, moT, pp = self.scr["pT"], self.scr["moT"], self.inp["ppar"]
        p = self.prog()
        banks = self.bank_list(p)
        par = p.sb("par", [128, pp.shape[1]]); p.dma(par[:], pp)
        o_cw, o_cb, o_gb, o_lam = 0, 4 * LH, 5 * LH, 9 * LH
        c8 = p.sb("c8", [128, 2 * LH])
        p.act(c8[:], par[:, o_lam:o_lam + 2 * LH], AF.Exp, scale=-1.0)
        p.ts(c8[:], c8[:], 1.0, None, ALU.add)
        p.act(c8[:], c8[:], AF.Ln)
        p.ts(c8[:], c8[:], -8.0, None, ALU.mult)
        xs = [p.sb("xs%d" % i, [128, TT]) for i in range(1)]
        zs = [p.sb("zs%d" % i, [128, TT]) for i in range(1)]
        y = p.sb("y", [128, TT]); yo = p.sb("yo", [128, TT], BF16)
        gw = [p.sb("gw%d" % i, [128, 4, 128]) for i in range(2)]
        aa = [p.sb("aa%d" % d, [128, TT]) for d in range(2)]
        bx = [p.sb("bx%d" % d, [128, TT]) for d in range(2)]
        hh = [p.sb("hh%d" % d, [128, TT]) for d in range(2)]
        t1 = [p.sb("t1_%d" % i, [128, 512]) for i in range(2)]
        t2 = [p.sb("t2_%d" % i, [128, 512]) for i in range(2)]
        tcs = self.tcols(512)
        it = 0
        for hd in range(LH):
            x = xs[0]; z = zs[0]; g = gw[hd % 2]
            p.dma(x[:], pT[hd * 128:(hd + 1) * 128, :])
            p.dma(z[:], pT[DL + hd * 128:DL + (hd + 1) * 128, :])
            p.dma(g[:], self.inp["lru_gate_w"][:, :, hd, :, :].rearrange("d g i j -> i (d g) j"))
            cw = lambda k: par[:, o_cw + hd * 4 + k:o_cw + hd * 4 + k + 1]
            cb = par[:, o_cb + hd:o_cb + hd + 1]
            for (s0, L) in ((0, TC), (TC, T)):
                p.ts(y[:, s0:s0 + L], x[:, s0:s0 + L], cw(1), cb, ALU.mult, ALU.add)
                p.stt(y[:, s0 + 1:s0 + L], x[:, s0:s0 + L - 1], cw(0), y[:, s0 + 1:s0 + L], ALU.mult, ALU.add)
                p.stt(y[:, s0:s0 + L - 1], x[:, s0 + 1:s0 + L], cw(2), y[:, s0:s0 + L - 1], ALU.mult, ALU.add)
                p.stt(y[:, s0:s0 + L - 2], x[:, s0 + 2:s0 + L], cw(3), y[:, s0:s0 + L - 2], ALU.mult, ALU.add)
            for d in range(2):
                gbr = par[:, o_gb + hd * 4 + d * 2:o_gb + hd * 4 + d * 2 + 1]
                gbi = par[:, o_gb + hd * 4 + d * 2 + 1:o_gb + hd * 4 + d * 2 + 2]
                c8d = c8[:, hd * 2 + d:hd * 2 + d + 1]
                for (t0, ts_) in tcs:
                    br = banks[it % 8]; bi = banks[(it + 1) % 8]
                    a1 = t1[(it // 2) % 2]; a2 = t2[(it // 2) % 2]
                    it += 2
                    p.mm(br[:, 0:ts_], g[:, d * 2, :], y[:, t0:t0 + ts_])
                    p.mm(bi[:, 0:ts_], g[:, d * 2 + 1, :], y[:, t0:t0 + ts_])
                    av = aa[d][:, t0:t0 + ts_]
                    p.act(a1[:, 0:ts_], br[:, 0:ts_], AF.Sigmoid, bias=gbr)
                    p.act(a2[:, 0:ts_], bi[:, 0:ts_], AF.Sigmoid, bias=gbi)
                    p.act(av, a1[:, 0:ts_], AF.Exp, scale=c8d)
                    p.tt(a1[:, 0:ts_], av, av, ALU.mult)
                    p.ts(a1[:, 0:ts_], a1[:, 0:ts_], -1.0, 1.0, ALU.mult, ALU.add)
                    p.act(a1[:, 0:ts_], a1[:, 0:ts_], AF.Sqrt)
                    p.tt(a2[:, 0:ts_], a2[:, 0:ts_], y[:, t0:t0 + ts_], ALU.mult, eng="gpsimd")
                    p.tt(bx[d][:, t0:t0 + ts_], a1[:, 0:ts_], a2[:, 0:ts_], ALU.mult)
            p.op("vector", lambda e: e.tensor_tensor_scan(out=hh[0][:], data0=aa[0][:], data1=bx[0][:], initial=0.0,
                                                          op0=ALU.mult, op1=ALU.add), reads=[aa[0], bx[0]], writes=[hh[0]])
            p.op("vector", lambda e: e.tensor_tensor_scan(out=hh[1][:, 0:TC][:, ::-1], data0=aa[1][:, 0:TC][:, ::-1],
                                                          data1=bx[1][:, 0:TC][:, ::-1], initial=0.0,
                                                          op0=ALU.mult, op1=ALU.add), reads=[aa[1], bx[1]], writes=[hh[1]])
            p.op("vector", lambda e: e.tensor_tensor_scan(out=hh[1][:, TC:TT][:, ::-1], data0=aa[1][:, TC:TT][:, ::-1],
                                                          data1=bx[1][:, TC:TT][:, ::-1], initial=hh[1][:, 0:1],
                                                          op0=ALU.mult, op1=ALU.add), reads=[aa[1], bx[1], hh[1]], writes=[hh[1]])
            p.tt(hh[0][:], hh[0][:], hh[1][:], ALU.add, eng="gpsimd")
            p.tt(y[:], z[:], z[:], ALU.mult)
            p.ts(y[:], y[:], 0.044715, 1.0, ALU.mult, ALU.add)
            p.tt(y[:], y[:], z[:], ALU.mult)
            p.act(y[:], y[:], AF.Sigmoid, scale=1.5957691216057308)
            p.tt(y[:], y[:], z[:], ALU.mult, eng="gpsimd")
            p.tt(yo[:], y[:], hh[0][:], ALU.mult)
            p.dma(moT[hd * 128:(hd + 1) * 128, :], yo[:], eng="gpsimd", writes=[("moT", hd)])
        p.emit()

    def phase_pool(self):
        TT, TC, T, GW, PG, DL = self.TT, self.TC, self.T, self.GW, self.PG, self.DL
        GH = T // GW
        pT = self.scr["pT"]
        p = self.prog()
        xs = [p.sb("px%d" % i, [128, TT]) for i in range(2)]
        inv = p.sb("inv", [128, TT])
        P1 = p.sb("P1", [128, GH, GW + 16]); P2 = p.sb("P2", [128, GH + 16, GW])
        Sa = p.sb("Sa", [128, GH * (GW + 16)]); Sb = p.sb("Sb", [128, GH * (GW + 16)])
        P3 = p.sb("P3", [128, TC + 16]); S3a = p.sb("S3a", [128, TC + 16]); S3b = p.sb("S3b", [128, TC + 16])
        dd = [p.sb("dd%d" % i, [128, TT]) for i in range(2)]
        db = p.sb("db", [128, TT], BF16)
        p.memset(P1[:], 0.0); p.memset(P2[:], 0.0, eng="gpsimd"); p.memset(P3[:], 0.0)

        def stages(src, tmpa, tmpb, dst, nst, ax, L):
            def sl(t, lo, hi):
                if len(t.shape) == 2:
                    return t[:, 8 + lo:8 + hi]
                if ax == 1:
                    return t[:, 8 + lo:8 + hi, :]
                return t[:, :, 8 + lo:8 + hi]
            cur = src
            shifts = [(-1, 0), (-1, 1), (-2, 2), (-4, 4)]
            rng = [(-7, L + 8), (-6, L + 7), (-4, L + 5), (0, L)]
            bufs = [tmpa, tmpb, tmpa, tmpb]
            for s_ in range(nst):
                lo, hi = rng[s_]
                if s_ == nst - 1:
                    lo, hi = 0, L
                    o = dst
                else:
                    o = sl(bufs[s_], lo, hi)
                a_, b_ = shifts[s_]
                p.tt(o, sl(cur, lo + a_, hi + a_), sl(cur, lo + b_, hi + b_), ALU.add)
                cur = bufs[s_]

        for g, w in enumerate(POOL_WINDOWS):
            nst = {2: 1, 4: 2, 8: 3, 16: 4}[w]
            self.load_bcast(p, inv[:], self.inp["invcnt"][g:g + 1, :])
            for c in range(PG // 128):
                idx = g * (PG // 128) + c
                x = xs[idx % 2]; d_ = dd[idx % 2]
                row = 2 * DL + g * PG + c * 128
                p.dma(x[:], pT[row:row + 128, :])
                p.copy(P1[:, :, 8:8 + GW], x[:, TC:TT].rearrange("p (a b) -> p a b", b=GW), eng="gpsimd")
                stages(P1[:], Sa[:].rearrange("p (a b) -> p a b", b=GW + 16), Sb[:].rearrange("p (a b) -> p a b", b=GW + 16),
                       P2[:, 8:8 + GH, :], nst, 2, GW)
                stages(P2[:], Sa[:].rearrange("p (a b) -> p a b", b=GW), Sb[:].rearrange("p (a b) -> p a b", b=GW),
                       d_[:, TC:TT].rearrange("p (a b) -> p a b", b=GW), nst, 1, GH)
                p.copy(P3[:, 8:8 + TC], x[:, 0:TC], eng="gpsimd")
                stages(P3[:], S3a[:], S3b[:], d_[:, 0:TC], nst, 1, TC)
                p.tt(d_[:], d_[:], inv[:], ALU.mult)
                p.tt(db[:], d_[:], x[:], ALU.subtract, eng="gpsimd")
                p.dma(self.scr["dT"][idx * 128:(idx + 1) * 128, :], db[:], eng="gpsimd", writes=[("dT", idx)])
        p.emit()

    def phase_post(self, l, which, tiles, with_router):
        D, KC, E = self.D, self.KC, self.E
        p = self.prog()
        banks = self.bank_list(p)
        ident = p.sb("ident", [128, 128]); p.dma(ident[:], self.inp["ident"])
        gseg = 2 if which == 0 else 5
        ysrc = self.scr["y"] if which == 0 else self.scr["m"]
        lg = p.sb("lg", [128, D]); lb = p.sb("lb", [128, D])
        self.load_bcast(p, lg[:], self.inp["ln_g"][l, which:which + 1, :])
        self.load_bcast(p, lb[:], self.inp["ln_b"][l, which:which + 1, :])
        gr = p.sb("gr", [128, D])
        hb = [p.sb("ph%d" % i, [128, D]) for i in range(2)]
        yb = [p.sb("py%d" % i, [128, D]) for i in range(2)]
        st = p.sb("st", [128, 8, 6]); mv = p.sb("mv", [128, 2]); rstd = p.sb("rstd", [128, 1]); nmr = p.sb("nmr", [128, 1])
        if with_router:
            s2 = p.sb("s2", [128, D]); h2 = p.sb("h2", [128, D])
            wr = p.sb("wr", [128, KC, E]); p.dma(wr[:], self.inp["router_w"][l].rearrange("(c p) e -> p c e", p=128))
            xt = p.sb("xt", [128, KC, 128])
            lgt = p.sb("lgt", [128, E]); mx = p.sb("mx", [128, 1]); sm = p.sb("sm", [128, 1])
            affT = p.sb("affT", [E, self.TT])
        n_ = 0
        for cnd in (1, 0):
            grp = [i for i in tiles if (1 if i < self.NTC else 0) == cnd]
            if not grp:
                continue
            self.load_bcast(p, gr[:], self.mod_row(l, cnd, gseg))
            if with_router:
                self.load_bcast(p, s2[:], self.mod_row(l, cnd, 4))
                p.ts(s2[:], s2[:], 1.0, None, ALU.add)
                self.load_bcast(p, h2[:], self.mod_row(l, cnd, 3))
            for i in grp:
                h = hb[n_ % 2]; y = yb[n_ % 2]
                p.dma(h[:], self.scr["h"][i * 128:(i + 1) * 128, :])
                p.dma(y[:], ysrc[i * 128:(i + 1) * 128, :])
                p.tt(y[:], y[:], gr[:], ALU.mult, eng="gpsimd")
                p.stt(y[:], h[:], DN_ALPHA, y[:], ALU.mult, ALU.add)
                self.ln_core(p, y[:], h[:], st, mv, rstd, nmr, "")
                p.tt(h[:], h[:], lg[:], ALU.mult, eng="gpsimd")
                p.tt(h[:], h[:], lb[:], ALU.add)
                p.dma(self.scr["h"][i * 128:(i + 1) * 128, :], h[:], eng="gpsimd")
                if with_router:
                    p.tt(y[:], h[:], s2[:], ALU.mult, eng="gpsimd")
                    p.tt(y[:], y[:], h2[:], ALU.add)
                    p.dma(self.scr["xm"][i * 128:(i + 1) * 128, :], y[:], eng="gpsimd", writes=[("xm", i)])
                    self.transpose_tile(p, y, xt, ident, banks[0:6], KC, bi0=n_)
                    lb_ = banks[6]
                    for c in range(KC):
                        p.mm(lb_[:, 0:E], xt[:, c, :], wr[:, c, :], start=(c == 0), stop=(c == KC - 1))
                    p.op("vector", lambda e, lb_=lb_: e.reduce_max(out=mx[:], in_=lb_[:, 0:E], axis=mybir.AxisListType.X),
                         reads=[lb_], writes=[mx])
                    p.ts(mx[:], mx[:], -1.0, None, ALU.mult)
                    p.act(lgt[:], lb_[:, 0:E], AF.Exp, bias=mx[:])
                    p.op("vector", lambda e: e.reduce_sum(out=sm[:], in_=lgt[:], axis=mybir.AxisListType.X), reads=[lgt], writes=[sm])
                    p.op("vector", lambda e: e.reciprocal(out=sm[:], in_=sm[:]), reads=[sm], writes=[sm])
                    p.ts(lgt[:], lgt[:], sm[:], None, ALU.mult)
                    p.dma(self.scr["aff"][i * 128:(i + 1) * 128, :], lgt[:], eng="gpsimd", writes=[("aff", i)])
                    tb = banks[7]
                    p.tr(tb[0:E, 0:128], lgt[:], ident[:])
                    p.copy(affT[:, i * 128:(i + 1) * 128], tb[0:E, 0:128])
                n_ += 1
        if with_router:
            p.dma(self.scr["affT"], affT[:], eng="gpsimd")
        p.emit()

    def phase_route(self, t0, n, cap, tag):
        E = self.E
        SC = (cap + 127) // 128
        p = self.prog()
        banks = self.bank_list(p)
        ident = p.sb("ident", [128, 128]); p.dma(ident[:], self.inp["ident"])
        a = p.sb("ra", [E, n]); w = [p.sb("rw%d" % i, [E, n]) for i in range(2)]
        m8 = p.sb("m8", [E, 8]); ones = p.sb("ones", [E, n]); cs = p.sb("cs", [E, n]); msk = p.sb("msk", [E, n])
        idxE = p.sb("idxE", [E, SC * 128])
        p.dma(a[:], self.scr["affT"][:, t0:t0 + n])
        p.memset(ones[:], 1.0, eng="gpsimd")
        cur = a
        for r in range(cap // 8):
            p.op("vector", lambda e, cur=cur: e.max(out=m8[:], in_=cur[:]), reads=[cur], writes=[m8])
            if r < cap // 8 - 1:
                nxt = w[r % 2]
                p.op("vector", lambda e, cur=cur, nxt=nxt: e.match_replace(out=nxt[:], in_to_replace=m8[:], in_values=cur[:],
                                                                          imm_value=-1.0), reads=[cur, m8], writes=[nxt])
                cur = nxt
        p.ts(msk[:], a[:], m8[:, 7:8], None, ALU.is_ge)
        p.op("vector", lambda e: e.tensor_tensor_scan(out=cs[:], data0=ones[:], data1=msk[:], initial=0.0,
                                                      op0=ALU.mult, op1=ALU.add), reads=[ones, msk], writes=[cs])
        junk = w[0]
        io = p.sb("io", [E, 512]); p.dma(io[:], self.inp["iota"][0:E, :])
        p.ts(io[:], io[:], -1.0, -0.5, ALU.mult, ALU.add)
        for s_ in range(SC * 128):
            p.op("scalar", lambda e, s_=s_: e.activation(out=junk[:], in_=cs[:], func=AF.Sign, bias=io[:, s_:s_ + 1], scale=1.0,
                                                         accum_out=idxE[:, s_:s_ + 1]),
                 reads=[cs, io], writes=[junk, (_key(idxE), s_)])
        p.op("vector", lambda e: e.tensor_scalar(out=idxE[:], in0=idxE[:], scalar1=-0.5, scalar2=float(n) / 2 + float(t0),
                                                 op0=ALU.mult, op1=ALU.add),
             reads=[idxE] + [(_key(idxE), s_) for s_ in range(SC * 128)], writes=[idxE])
        it_ = p.sb("it", [128, SC, E])
        for i in range(SC):
            b = banks[i % 8]
            p.tr(b[:, 0:E], idxE[:, i * 128:(i + 1) * 128], ident[0:E, 0:E])
            p.copy(it_[:, i, :], b[:, 0:E])
        p.dma(self.scr["idx_" + tag], it_[:], eng="gpsimd")
        p.emit()

    def phase_moe(self, l, t0, n, cap, tag):
        D, KC, E, DE, TT = self.D, self.KC, self.E, self.DE, self.TT
        nt = n // 128
        FC = DE // 128
        SC = (cap + 127) // 128
        CW = SC * 128
        xm, mT, aff = self.scr["xm"], self.scr["m"], self.scr["aff"]
        p = self.prog()
        banks = self.bank_list(p)
        ident = p.sb("ident", [128, 128]); p.dma(ident[:], self.inp["ident"])
        idf = p.sb("idf", [128, SC, E]); p.dma(idf[:], self.scr["idx_" + tag])
        idx = p.sb("idx", [128, SC, E], mybir.dt.int32)
        ixh = [p.sb("ixh%d" % h, [128, SC, E], mybir.dt.int32) for h in range(2)]
        idh = p.sb("idh", [128, SC, E])
        p.copy(idx[:], idf[:])
        for h in range(2):
            p.ts(idh[:], idf[:], 2.0, float(h), ALU.mult, ALU.add)
            p.copy(ixh[h][:], idh[:])
        HW = D // 2
        xm2 = xm.rearrange("t (h c) -> (t h) c", h=2)
        m2 = mT.rearrange("t (h c) -> (t h) c", h=2)
        xg = [p.sb("xg%d" % i, [128, D]) for i in range(1)]
        gs = [p.sb("gs%d" % i, [128, E]) for i in range(SC)]
        for t_ in xg + gs:
            p.memset(t_[:], 0.0, eng="gpsimd")
        for i in range(nt):
            p.dma(mT[t0 + i * 128:t0 + (i + 1) * 128, :], xg[0][:], eng="gpsimd", writes=["m"])
        xe = p.sb("xe", [128, KC, CW], BF16)
        aT = p.sb("aT", [128, FC, CW], BF16)
        wst = [p.sb("wst%d" % i, [128, KC, 128]) for i in range(2)]
        wgb = [p.sb("wgb%d" % i, [128, KC, 128], BF16) for i in range(2)]
        wds = [p.sb("wds%d" % i, [128, FC, 256]) for i in range(2)]
        wd = [p.sb("wd%d" % i, [128, FC, 256], BF16) for i in range(2)]
        sg = [p.sb("sg%d" % i, [128, CW]) for i in range(2)]
        ye = [p.sb("ye%d" % i, [128, D]) for i in range(SC)]
        bound = t0 + n - 1
        breg = {}

        def bnd(g, k, v):
            if k not in breg:
                breg[k] = g.to_reg(v)
            return breg[k]
        it = 0
        for e in range(E):
            for s in range(SC):
                x_ = xg[0]
                ix = idx[:, s, e:e + 1]
                for h in range(2):
                    ixh_ = ixh[h][:, s, e:e + 1]
                    p.op("gpsimd", lambda g, x_=x_, ixh_=ixh_, h=h: g.indirect_dma_start(
                        out=x_[:, h * HW:(h + 1) * HW], out_offset=None, in_=xm2[:, :],
                        in_offset=bass.IndirectOffsetOnAxis(ap=ixh_, axis=0),
                        bounds_check=bnd(g, "b2", 2 * bound + 1), oob_is_err=False), reads=[xm, ixh[h]], writes=[x_], dma=True)
                p.op("gpsimd", lambda g, g_=gs[s], ix=ix: g.indirect_dma_start(
                    out=g_[:, :], out_offset=None, in_=aff[:, :], in_offset=bass.IndirectOffsetOnAxis(ap=ix, axis=0),
                    bounds_check=bnd(g, "b1", bound), oob_is_err=False), reads=[aff, idx], writes=[gs[s]], dma=True)
                for c0 in range(0, KC, 4):
                    nb_ = min(4, KC - c0)
                    bk = banks[it % 4]
                    it += 1
                    for j in range(nb_):
                        p.tr(bk[:, j * 128:(j + 1) * 128], x_[:, (c0 + j) * 128:(c0 + j + 1) * 128], ident[:])
                    p.copy(xe[:, c0:c0 + nb_, s * 128:(s + 1) * 128], bk[:, 0:nb_ * 128].rearrange("p (a t) -> p a t", t=128),
                           eng="vector" if it % 2 == 0 else "scalar")
            def load_gu(k):
                f, part = k // 2, k % 2
                col = part * DE + f * 128
                p.dma(wst[k % 2][:], self.inp["expert_w_gu"][l, e, :, col:col + 128].rearrange("(c p) m -> p c m", p=128))
                p.copy(wgb[k % 2][:], wst[k % 2][:], eng="gpsimd")
            load_gu(0)
            for k in range(2 * FC):
                if k + 1 < 2 * FC:
                    load_gu(k + 1)
                f, part = k // 2, k % 2
                bk_ = banks[4 + (f % 2) * 2 + part]
                for c in range(KC):
                    p.mm(bk_[:, 0:CW], wgb[k % 2][:, c, :], xe[:, c, :], start=(c == 0), stop=(c == KC - 1))
                if part == 1:
                    bg = banks[4 + (f % 2) * 2]; bu = banks[5 + (f % 2) * 2]
                    s_ = sg[f % 2]
                    p.act(s_[:], bg[:, 0:CW], AF.Silu)
                    p.tt(aT[:, f, :], s_[:], bu[:, 0:CW], ALU.mult)
            def load_d(nb):
                p.dma(wds[nb % 2][:], self.inp["expert_w_down"][l, e, :, nb * 256:(nb + 1) * 256].rearrange("(c p) n -> p c n", p=128))
                p.copy(wd[nb % 2][:], wds[nb % 2][:], eng="gpsimd")
            load_d(0)
            for nb in range(D // 256):
                if nb + 1 < D // 256:
                    load_d(nb + 1)
                for s in range(SC):
                    bk = banks[it % 4]
                    it += 1
                    for f in range(FC):
                        p.mm(bk[:, 0:256], aT[:, f, s * 128:(s + 1) * 128], wd[nb % 2][:, f, :], start=(f == 0), stop=(f == FC - 1))
                    p.ts(ye[s][:, nb * 256:(nb + 1) * 256], bk[:, 0:256], gs[s][:, e:e + 1], None, ALU.mult,
                         eng="vector")
            for s in range(SC):
                for h in range(2):
                    ixh_ = ixh[h][:, s, e:e + 1]
                    p.op("gpsimd", lambda g, y_=ye[s], ixh_=ixh_, h=h: g.indirect_dma_start(
                        out=m2[:, :], out_offset=bass.IndirectOffsetOnAxis(ap=ixh_, axis=0), in_=y_[:, h * HW:(h + 1) * HW],
                        in_offset=None, compute_op=ALU.add), reads=[ye[s], ixh[h], "m"], writes=["m"], dma=True)
        p.emit()

    def phase_out(self):
        p = self.prog()
        hb = [p.sb("fo%d" % i, [128, self.D]) for i in range(2)]
        for n_, i in enumerate(range(self.NTC, self.NT)):
            h = hb[n_ % 2]
            p.dma(h[:], self.scr["h"][i * 128:(i + 1) * 128, :])
            p.dma(self.out[(i - self.NTC) * 128:(i - self.NTC + 1) * 128, :], h[:], eng="gpsimd")
        p.emit()

    def phase_mlstm_gates(self):
        MH, TT, TC, NT, NTC = self.MH, self.TT, self.TC, self.NT, self.NTC
        p = self.prog()
        banks = self.bank_list(p)
        ident = p.sb("ident", [128, 128]); p.dma(ident[:], self.inp["ident"])
        gb = p.sb("gb", [MH, 4]); p.dma(gb[:], self.inp["ogb"])
        ones = p.sb("ones", [MH, TT])
        zc = p.sb("zc", [MH, 1])
        p.memset(ones[:], 1.0); p.memset(zc[:], 0.0)
        ig = p.sb("ig", [MH, TT]); fg = p.sb("fg", [MH, TT])
        Fc = p.sb("Fc", [MH, TT]); R = p.sb("R", [MH, TT])
        arr = [p.sb("q%d" % k, [MH, TT]) for k in range(5)] + [fg]
        dsc = p.sb("dsc", [MH, NT]); nre = p.sb("nre", [MH, NT])
        S = p.sb("S", [128, NT, 6, MH])
        for d in range(2):
            p.dma(ig[:], self.scr["gT"][(d * 2) * MH:(d * 2 + 1) * MH, :])
            p.dma(fg[:], self.scr["gT"][(d * 2 + 1) * MH:(d * 2 + 2) * MH, :])
            p.ts(ig[:], ig[:], gb[:, d * 2:d * 2 + 1], None, ALU.add)
            p.ts(fg[:], fg[:], gb[:, d * 2 + 1:d * 2 + 2], None, ALU.add)
            p.act(fg[:], fg[:], AF.Exp, scale=-1.0)
            p.ts(fg[:], fg[:], 1.0, None, ALU.add)
            p.act(fg[:], fg[:], AF.Ln)
            p.ts(fg[:], fg[:], -1.0, None, ALU.mult)

            def scan(out, d1, op1, segs):
                prev = None
                for (s0, s1) in segs:
                    def v(t):
                        x = t[:, s0:s1]
                        return x[:, ::-1] if d == 1 else x
                    init = 0.0 if prev is None else prev
                    p.op("vector", lambda e, o=v(out), a_=v(ones), b_=v(d1), init=init: e.tensor_tensor_scan(
                        out=o, data0=a_, data1=b_, initial=init, op0=ALU.mult, op1=op1),
                        reads=[ones, d1, out], writes=[out])
                    prev = out[:, s1 - 1:s1] if d == 0 else out[:, s0:s0 + 1]
            segs = [(0, TC), (TC, TT)]
            scan(Fc, fg, ALU.add, segs)
            p.tt(ig[:], ig[:], Fc[:], ALU.subtract)
            scan(R, ig, ALU.max, segs)
            p.tt(fg[:], Fc[:], R[:], ALU.add)
            p.act(fg[:], fg[:], AF.Exp, scale=-1.0)
            order = list(range(NT)) if d == 0 else (list(range(NTC - 1, -1, -1)) + list(range(NT - 1, NTC - 1, -1)))
            prev_idx = None
            for c in order:
                ch = slice(c * 128, (c + 1) * 128)
                ei = c * 128 + 127 if d == 0 else c * 128
                Fp = zc[:] if prev_idx is None else Fc[:, prev_idx:prev_idx + 1]
                Rp = zc[:] if prev_idx is None else R[:, prev_idx:prev_idx + 1]
                p.ts(nre[:, c:c + 1], R[:, ei:ei + 1], -1.0, None, ALU.mult)
                p.tt(dsc[:, c:c + 1], Rp, R[:, ei:ei + 1], ALU.subtract)
                p.ts(arr[0][:, ch], R[:, ch], Fp, None, ALU.add)
                p.act(arr[0][:, ch], arr[0][:, ch], AF.Exp, scale=-1.0)
                p.act(arr[1][:, ch], ig[:, ch], AF.Exp, bias=Fp)
                p.act(arr[2][:, ch], R[:, ch], AF.Exp, scale=-1.0, bias=Rp)
                p.act(arr[3][:, ch], ig[:, ch], AF.Exp, bias=nre[:, c:c + 1])
                p.act(arr[4][:, ch], R[:, ch], AF.Exp, scale=0.0, bias=dsc[:, c:c + 1])
                prev_idx = ei
            for c in range(NT):
                b = banks[c % 8]
                for k in range(6):
                    p.tr(b[:, k * MH:(k + 1) * MH], arr[k][:, c * 128:(c + 1) * 128], ident[0:MH, 0:MH])
                p.copy(S[:, c, :, :], b[:, 0:6 * MH].rearrange("p (k h) -> p k h", h=MH), eng="vector" if c % 2 == 0 else "scalar")
            p.dma(self.scr["mS"][d], S[:], eng="gpsimd")
        p.emit()

    def phase_mlstm(self):
        MH, TT, TC, NT, NTC, DV, DQK, QK, D = self.MH, self.TT, self.TC, self.NT, self.NTC, self.DV, self.DQK, self.QK, self.D
        NLC = NT - NTC
        DC = DQK // 128
        qkT, kvo, moT = self.scr["qkT"], self.scr["kvo"], self.scr["moT"]
        p = self.prog()
        pm = [p.ps("pm%d" % i, [128, 512]) for i in range(8)]
        ps_aD = [pm[0][:, 0:DV], pm[1][:, 0:DV]]
        ps_bD = [pm[2][:, 0:DV], pm[3][:, 0:DV]]
        ps_sD = [pm[4][:, 0:128], pm[5][:, 0:128]]
        ps_dD = [pm[4][:, 128:136], pm[5][:, 128:136]]
        ps_u = [pm[6], pm[7]]
        ps_uD = None
        ps_t = pm[6]
        ident = p.sb("ident", [128, 128]); p.dma(ident[:], self.inp["ident"])
        mask = p.sb("mask", [128, 2, 128]); p.dma(mask[:], self.inp["mmask"])
        S = [p.sb("S%d" % d, [128, NT, 6, MH]) for d in range(2)]
        for d in range(2):
            p.dma(S[d][:], self.scr["mS"][d])
        ng = p.sb("ng", [128, DV])
        hbuf = p.sb("hbuf", [128, NLC, DV])
        CstD = [p.sb("Cst%d" % d, [128, DC, DV + 1]) for d in range(2)]
        NB = 4
        qT = [p.sb("qT%d" % i, [128, DC, 128]) for i in range(NB)]
        kT = [p.sb("kT%d" % i, [128, DC, 128]) for i in range(NB)]
        kt = [p.sb("kt%d" % i, [128, DQK]) for i in range(NB)]
        vt = [p.sb("vt%d" % i, [128, DV + 1]) for i in range(NB)]
        for i in range(NB):
            p.memset(vt[i][:, DV:DV + 1], 1.0)
        S0 = [p.sb("S0_%d" % i, [128, 128]) for i in range(2)]
        kw = [p.sb("kw%d" % i, [128, DQK]) for i in range(2)]
        n1 = [p.sb("n1_%d" % i, [128, DV]) for i in range(2)]
        n2 = [p.sb("n2_%d" % i, [128, DV]) for i in range(2)]
        sm = [p.sb("sm%d" % i, [128, 4]) for i in range(2)]
        ot = [p.sb("ot%d" % i, [128, DV]) for i in range(2)]
        st = p.sb("st", [128, 8, 6]); mv = p.sb("mv", [128, 2]); rstd = p.sb("rstd", [128, 1]); nmr = p.sb("nmr", [128, 1])
        tT = [p.sb("tT%d" % i, [128, DV // 128, 128], BF16) for i in range(2)]
        it = 0
        for hd in range(MH):
            self.load_bcast(p, ng[:], self.inp["odd_norm_g"][0:1, hd * DV:(hd + 1) * DV])
            p.memset(hbuf[:], 0.0, eng="gpsimd")
            orders = [list(range(NT)), list(range(NTC - 1, -1, -1)) + list(range(NT - 1, NTC - 1, -1))]
            for d in range(2):
                p.memset(CstD[d][:], 0.0)
            for step in range(NT):
                for d in range(2):
                    Cst = CstD[d]
                    c = orders[d][step]
                    j = it % NB
                    it += 1
                    cs_ = slice(c * 128, (c + 1) * 128)
                    sc = lambda k, d=d, c=c: S[d][:, c, k, hd:hd + 1]
                    lat = c >= NTC
                    p.dma(kt[j][:], kvo[cs_, hd * DQK:(hd + 1) * DQK])
                    p.dma(vt[j][:, 0:DV], kvo[cs_, QK + hd * DV:QK + (hd + 1) * DV])
                    pss, psa, psb, psd = ps_sD[d], ps_aD[d], ps_bD[d], ps_dD[d]
                    if lat:
                        p.dma(qT[j][:], qkT[hd * DQK:(hd + 1) * DQK, cs_].rearrange("(c p) t -> p c t", p=128))
                        p.dma(kT[j][:], qkT[QK + hd * DQK:QK + (hd + 1) * DQK, cs_].rearrange("(c p) t -> p c t", p=128))
                        for dc in range(DC):
                            p.mm(pss, kT[j][:, dc, :], qT[j][:, dc, :], start=(dc == 0), stop=(dc == DC - 1))
                        s0 = S0[it % 2]
                        p.stt(s0[:], pss, sc(1), mask[:, d, :], ALU.mult, ALU.mult)
                        p.mm(psa, s0[:], vt[j][:, 0:DV])
                        p.mm(psd[:, 0:1], s0[:], vt[j][:, DV:DV + 1])
                        for dc in range(DC):
                            p.mm(psb, qT[j][:, dc, :], Cst[:, dc, 0:DV], start=(dc == 0), stop=(dc == DC - 1))
                        for dc in range(DC):
                            p.mm(psd[:, 1:2], qT[j][:, dc, :], Cst[:, dc, DV:DV + 1], start=(dc == 0), stop=(dc == DC - 1))
                        a1 = n1[it % 2]; a2 = n2[it % 2]; q_ = sm[it % 2]
                        p.act(a1[:], psa, AF.Copy, scale=sc(0))
                        p.stt(a2[:], psb, sc(2), a1[:], ALU.mult, ALU.add)
                        p.ts(q_[:, 0:1], psd[:, 0:1], sc(0), None, ALU.mult)
                        p.stt(q_[:, 1:2], psd[:, 1:2], sc(2), q_[:, 0:1], ALU.mult, ALU.add)
                        p.ts(q_[:, 2:3], q_[:, 1:2], -1.0, None, ALU.mult)
                        p.tt(q_[:, 2:3], q_[:, 2:3], q_[:, 1:2], ALU.max)
                        p.ts(q_[:, 2:3], q_[:, 2:3], sc(5), None, ALU.max)
                        p.op("vector", lambda e, q_=q_: e.reciprocal(out=q_[:, 3:4], in_=q_[:, 2:3]), reads=[q_], writes=[q_])
                        hv = hbuf[:, c - NTC, :]
                        p.stt(hv, a2[:], q_[:, 3:4], hv, ALU.mult, ALU.add)
                    k_ = kw[it % 2]
                    p.ts(k_[:], kt[j][:], sc(3), None, ALU.mult, eng="gpsimd")
                    for dc in range(DC):
                        pu = ps_uD[d][:, dc * DV:(dc + 1) * DV] if False else ps_u[(2 * d + dc) % len(ps_u)][:, 0:DV]
                        p.mm(pu, k_[:, dc * 128:(dc + 1) * 128], vt[j][:, 0:DV])
                        p.mm(psd[:, 2 + dc:3 + dc], k_[:, dc * 128:(dc + 1) * 128], vt[j][:, DV:DV + 1])
                        p.stt(Cst[:, dc, 0:DV], Cst[:, dc, 0:DV], sc(4), pu, ALU.mult, ALU.add)
                        p.stt(Cst[:, dc, DV:DV + 1], Cst[:, dc, DV:DV + 1], sc(4), psd[:, 2 + dc:3 + dc], ALU.mult, ALU.add)
            for c in range(NLC):
                i = c + NTC
                o_ = ot[c % 2]; t_ = tT[c % 2]; a1 = n1[c % 2]
                p.dma(o_[:], kvo[i * 128:(i + 1) * 128, QK + D + hd * DV:QK + D + (hd + 1) * DV])
                p.act(o_[:], o_[:], AF.Sigmoid)
                self.ln_core(p, hbuf[:, c, :], a1[:], st, mv, rstd, nmr, "")
                p.tt(a1[:], a1[:], ng[:], ALU.mult, eng="gpsimd")
                p.tt(a1[:], a1[:], o_[:], ALU.mult)
                for b_ in range(DV // 128):
                    p.tr(ps_t[:, b_ * 128:(b_ + 1) * 128], a1[:, b_ * 128:(b_ + 1) * 128], ident[:])
                p.copy(t_[:], ps_t[:, 0:DV].rearrange("p (a t) -> p a t", t=128), eng="scalar")
                p.dma(moT[hd * DV:(hd + 1) * DV, i * 128:(i + 1) * 128].rearrange("(a p) t -> p a t", p=128), t_[:],
                      eng="gpsimd", writes=[("moT", hd, c)])
        p.emit()

    def build(self):
        D, TT, T, TC, KC, E, MH = self.D, self.TT, self.T, self.TC, self.KC, self.E, self.MH
        QK, DL, DP, PG, LH = self.QK, self.DL, self.DP, self.PG, self.LH
        nc = self.nc
        self.din("xin", [TT, D]); self.din("condT", [128, KC, 2])
        self.din("ada_w", [2, D, 6 * D]); self.din("ada_b", [2, 6 * D])
        self.din("ln_g", [2, 2, D]); self.din("ln_b", [2, 2, D])
        self.din("even_w_in", [D, 2 * DL + DP]); self.din("lru_gate_w", [2, 2, LH, 128, 128])
        self.din("ppar", [128, 11 * LH + DP // 128]); self.din("pool_w", [4, PG, PG]); self.din("even_w_out", [D, D])
        self.din("odd_w_in", [D, 2 * QK + 2 * D + 4 * MH]); self.din("ogb", [MH, 4]); self.din("odd_norm_g", [1, D])
        self.din("odd_w_out", [D, D]); self.din("router_w", [2, D, E])
        self.din("expert_w_gu", [2, E, D, 2 * self.DE]); self.din("expert_w_down", [2, E, self.DE, D])
        self.din("ident", [128, 128]); self.din("iota", [128, 512]); self.din("pidx", [128, 4])
        self.din("invcnt", [4, TT]); self.din("mmask", [128, 2, 128])
        self.out = nc.dram_tensor("out", [T, D], F32, kind="ExternalOutput").ap()
        self.dscr("mods", [2, 2, 6 * D]); self.dscr("h", [TT, D]); self.dscr("uT", [D, TT], BF16)
        self.dscr("pT", [max(2 * DL + DP, 2 * QK), TT]); self.dscr("moT", [D, TT], BF16); self.dscr("dT", [DP, TT], BF16)
        self.dscr("y", [TT, D]); self.dscr("xm", [TT, D]); self.dscr("aff", [TT, E]); self.dscr("affT", [E, TT])
        self.dscr("idx_l", [128, (self.CAP + 127) // 128, E]); self.dscr("idx_c", [128, (self.CAPC + 127) // 128, E]); self.dscr("m", [TT, D])
        self.dscr("gT", [4 * MH, TT]); self.dscr("kvo", [TT, QK + 2 * D]); self.dscr("mS", [2, 128, self.NT, 6, MH])
        self.scr["qkT"] = self.scr["pT"]
        self.sy = Sync(nc)
        alltiles = list(range(self.NT)); lattiles = list(range(self.NTC, self.NT))
        self.phase_adaln()
        self.phase_ln0()
        self.phase_u(0)
        self.gemm_fm(self.inp["even_w_in"], self.scr["uT"], self.scr["pT"], D, 2 * DL + DP, self.tcols())
        self.phase_lru()
        self.phase_pool()
        for g in range(4):
            self.gemm_fm(self.inp["pool_w"][g], self.scr["dT"][g * PG:(g + 1) * PG, :], self.scr["moT"][DL + g * PG:DL + (g + 1) * PG, :],
                         PG, PG, self.tcols(), row_scale=self.inp["ppar"][:, 11 * LH + g * (PG // 128):11 * LH + (g + 1) * (PG // 128)], odt=BF16)
        self.gemm_tm(self.scr["moT"], self.inp["even_w_out"], self.scr["y"], D, D, alltiles)
        self.phase_post(0, 0, alltiles, True)
        self.phase_route(TC, T, self.CAP, "l")
        self.phase_route(0, TC, self.CAPC, "c")
        self.phase_moe(0, TC, T, self.CAP, "l")
        self.phase_moe(0, 0, TC, self.CAPC, "c")
        self.phase_post(0, 1, alltiles, False)
        self.phase_u(1)
        wi = self.inp["odd_w_in"]
        self.gemm_fm(wi[:, 0:QK], self.scr["uT"], self.scr["qkT"][0:QK, :], D, QK, self.tcols(lat_only=True), scale=float(self.DQK) ** -0.5)
        self.gemm_fm(wi[:, QK:2 * QK], self.scr["uT"], self.scr["qkT"][QK:2 * QK, :], D, QK, self.tcols(lat_only=True))
        self.gemm_fm(wi[:, 2 * QK + 2 * D:2 * QK + 2 * D + 4 * MH], self.scr["uT"], self.scr["gT"], D, 4 * MH, self.tcols())
        self.gemm_tm(self.scr["uT"], wi[:, QK:2 * QK + 2 * D], self.scr["kvo"], D, QK + 2 * D, alltiles)
        self.phase_mlstm_gates()
        self.phase_mlstm()
        self.gemm_tm(self.scr["moT"], self.inp["odd_w_out"], self.scr["y"], D, D, lattiles)
        self.phase_post(1, 0, lattiles, True)
        self.phase_route(TC, T, self.CAP, "l")
        self.phase_moe(1, TC, T, self.CAP, "l")
        self.phase_post(1, 1, lattiles, False)
        self.phase_out()
        self.sy.close()
        return nc


def host_inputs(cfg, b, x, c, ctx, c_ctx, ada_w, ada_b, ln_g, ln_b, even_w_in, even_conv_w, even_conv_b, lru_gate_w,
                lru_gate_b, lru_lambda, pool_w, pool_scale, even_w_out, odd_w_in, odd_gate_b, odd_norm_g, odd_w_out,
                router_w, expert_w_gu, expert_w_down):
    D, T, TC, GW, MH = cfg["D"], cfg["T"], cfg["TC"], cfg["GW"], cfg["MH"]
    KC = D // 128
    LH = D // 256
    f = lambda a: np.ascontiguousarray(np.asarray(a, dtype=np.float32))
    m = {}
    m["xin"] = f(np.concatenate([ctx[b], x[b]], axis=0))
    m["condT"] = f(np.stack([c[b], c_ctx], axis=-1).reshape(KC, 128, 2).transpose(1, 0, 2))
    m["ada_w"] = f(ada_w); m["ada_b"] = f(ada_b); m["ln_g"] = f(ln_g); m["ln_b"] = f(ln_b)
    m["even_w_in"] = f(even_w_in[0]); m["lru_gate_w"] = f(lru_gate_w[0])
    cw = np.asarray(even_conv_w[0]).reshape(4, LH, 128).transpose(2, 1, 0).reshape(128, LH * 4)
    cb = np.asarray(even_conv_b[0]).reshape(LH, 128).T
    gb = np.asarray(lru_gate_b[0]).reshape(2, 2, LH, 128).transpose(3, 2, 0, 1).reshape(128, LH * 4)
    lam = np.asarray(lru_lambda[0]).reshape(2, LH, 128).transpose(2, 1, 0).reshape(128, LH * 2)
    psc = np.asarray(pool_scale[0]).reshape(-1, 128).T
    m["ppar"] = f(np.concatenate([cw, cb, gb, lam, psc], axis=1))
    m["pool_w"] = f(pool_w[0]); m["even_w_out"] = f(even_w_out[0])
    m["odd_w_in"] = f(odd_w_in[0])
    m["ogb"] = f(np.asarray(odd_gate_b[0]).reshape(2, 2, MH).transpose(2, 0, 1).reshape(MH, 4))
    m["odd_norm_g"] = f(odd_norm_g); m["odd_w_out"] = f(odd_w_out[0]); m["router_w"] = f(router_w)
    m["expert_w_gu"] = f(expert_w_gu); m["expert_w_down"] = f(expert_w_down)
    m["ident"] = np.eye(128, dtype=np.float32)
    m["iota"] = f(np.tile(np.arange(512, dtype=np.float32)[None, :], (128, 1)))
    m["pidx"] = f(np.arange(128, dtype=np.float32)[:, None] + 128.0 * np.arange(4, dtype=np.float32)[None, :])
    inv = np.zeros((4, TC + T), np.float32)
    GH = T // GW

    def cnt(L, w):
        pos = np.arange(L)
        lo = np.clip(pos - w // 2, 0, L); hi = np.clip(pos + w - w // 2, 0, L)
        return (hi - lo).astype(np.float64)
    for g, w in enumerate(POOL_WINDOWS):
        inv[g, :TC] = 1.0 / cnt(TC, w)
        inv[g, TC:] = (1.0 / (cnt(GH, w)[:, None] * cnt(GW, w)[None, :])).reshape(-1)
    m["invcnt"] = inv
    s_, t_ = np.meshgrid(np.arange(128), np.arange(128), indexing="ij")
    m["mmask"] = f(np.stack([(s_ <= t_), (s_ >= t_)], axis=1).astype(np.float32))
    return m


_NC_CACHE = {}


def run(cfg, inputs, n_samples=2):
    key = tuple(sorted((k, v) for k, v in cfg.items() if k != 'debug')) + (tuple(cfg.get('debug', ())),)
    if key not in _NC_CACHE:
        _NC_CACHE[key] = Builder(cfg).build()
    nc = _NC_CACHE[key]
    in_maps = [host_inputs(cfg, b, **inputs) for b in range(n_samples)]
    res = run_bass_kernel_spmd(nc, in_maps, core_ids=list(range(n_samples)))
    return np.stack([np.asarray(r["out"]) for r in res.results], axis=0).astype(np.float32)


def kernel(**inputs):
    inputs = {k: np.asarray(v) for k, v in inputs.items()}
    return run(CFG_FULL, inputs, n_samples=2)
```
